# Optimizing a Trainium2 kernel written in Bass

```python
import math
import jax, jax.numpy as jnp
from jax import lax
import numpy as np

D_MODEL = 1024
BATCH = 32
SEQ = 2048
DEPTH = 1

GDN_HEADS = 8
GDN_DK = 128
GDN_DV = 128
SSD_HEADS = 16
SSD_HEADDIM = 64
SSD_GROUPS = 2
SSD_STATE = 128
CONV_K = 4
CHUNK = 64
D_FF = 2816
FFN_CONV_K = 3
EPS = 1e-6

GDN_QK = GDN_HEADS * GDN_DK
GDN_V = GDN_HEADS * GDN_DV
SSD_D = SSD_HEADS * SSD_HEADDIM
SSD_BC = SSD_GROUPS * SSD_STATE
SSD_HPG = SSD_HEADS // SSD_GROUPS
MIX_WIDTH = GDN_V + SSD_D
GDN_CONV_CH = 2 * GDN_QK + GDN_V
SSD_CONV_CH = SSD_D + 2 * SSD_BC
IN_SPLITS = (GDN_QK, GDN_QK, GDN_V, GDN_V, GDN_HEADS, GDN_HEADS,
             SSD_D, SSD_D, SSD_BC, SSD_BC, SSD_HEADS)
D_IN_PROJ = sum(IN_SPLITS)

kernel_name = "hybrid_gdn_ssd_parallel_heads_convffn"


def rms_norm(x, w):
    xf = x.astype(jnp.float32)
    y = xf * lax.rsqrt(jnp.mean(xf * xf, axis=-1, keepdims=True) + EPS)
    return (y * w.astype(jnp.float32)).astype(x.dtype)


def l2_normalize(x):
    xf = x.astype(jnp.float32)
    return xf * lax.rsqrt(jnp.sum(xf * xf, axis=-1, keepdims=True) + EPS)


def causal_dwconv(x, w, b=None):
    k_w, ch = w.shape
    y = lax.conv_general_dilated(
        x, w[:, None, :].astype(x.dtype), window_strides=(1,), padding=[(k_w - 1, 0)],
        dimension_numbers=("NWC", "WIO", "NWC"), feature_group_count=ch)
    if b is not None:
        y = y + b.astype(x.dtype)
    return y


def gdn_chunked(q, k, v, g, beta):
    bsz, s, h, dk = q.shape
    dv = v.shape[-1]
    n = s // CHUNK

    def chunk(t):
        return jnp.moveaxis(t.reshape(bsz, n, CHUNK, h, *t.shape[3:]), 3, 1)

    q, k, v, g, beta = (chunk(t) for t in (q, k, v, g, beta))
    gc = jnp.cumsum(g, axis=-1)
    causal = jnp.tril(jnp.ones((CHUNK, CHUNK), dtype=bool))
    strict = jnp.tril(jnp.ones((CHUNK, CHUNK), dtype=bool), -1)
    decay = jnp.exp(jnp.where(causal, gc[..., :, None] - gc[..., None, :], -jnp.inf))
    kb = k * beta[..., None]
    a_in = jnp.einsum("bhnid,bhnjd->bhnij", kb, k) * decay
    lmat = jnp.where(strict, a_in, 0.0) + jnp.eye(CHUNK, dtype=a_in.dtype)
    rhs = jnp.concatenate([v * beta[..., None], kb * jnp.exp(gc)[..., None]], axis=-1)
    sol = lax.linalg.triangular_solve(lmat, rhs, left_side=True, lower=True, unit_diagonal=True)
    u, w = sol[..., :dv], sol[..., dv:]
    qk = jnp.einsum("bhnid,bhnjd->bhnij", q, k) * decay
    q_dec = q * jnp.exp(gc)[..., None]
    k_dec = k * jnp.exp(gc[..., -1:] - gc)[..., None]
    g_last = jnp.exp(gc[..., -1])

    def step(state, xs):
        u_i, w_i, qk_i, q_i, k_i, gl = xs
        v_new = u_i - jnp.einsum("bhcd,bhde->bhce", w_i, state)
        o = jnp.einsum("bhcd,bhde->bhce", q_i, state) + jnp.einsum("bhij,bhje->bhie", qk_i, v_new)
        state = state * gl[..., None, None] + jnp.einsum("bhcd,bhce->bhde", k_i, v_new)
        return state, o

    xs = tuple(jnp.moveaxis(t, 2, 0) for t in (u, w, qk, q_dec, k_dec, g_last))
    s0 = jnp.zeros((bsz, h, dk, dv), dtype=q.dtype)
    _, o = lax.scan(step, s0, xs)
    return jnp.transpose(o, (1, 0, 3, 2, 4)).reshape(bsz, s, h, dv)


def ssd_chunked(x, dt, a_neg, bmat, cmat):
    bsz, s, grp, hg, p = x.shape
    nst = bmat.shape[-1]
    n = s // CHUNK
    xdt = x * dt[..., None]
    adt = dt * a_neg

    def chunk(t):
        return jnp.moveaxis(t.reshape(bsz, n, CHUNK, *t.shape[2:]), 1, 0)

    xs = tuple(chunk(t) for t in (xdt, adt, bmat, cmat))
    causal = jnp.tril(jnp.ones((CHUNK, CHUNK), dtype=bool))

    def step(state, xs):
        xdt_i, adt_i, b_i, c_i = xs
        acs = jnp.cumsum(jnp.moveaxis(adt_i, 1, -1), axis=-1)
        lmat = jnp.exp(jnp.where(causal, acs[..., :, None] - acs[..., None, :], -jnp.inf))
        cb = jnp.einsum("blgn,bsgn->bgls", c_i, b_i)
        y_diag = jnp.einsum("bgls,bghls,bsghp->blghp", cb, lmat, xdt_i)
        y_off = jnp.einsum("blgn,bghpn,bghl->blghp", c_i, state, jnp.exp(acs))
        decay_s = jnp.exp(acs[..., -1:] - acs)
        state = state * jnp.exp(acs[..., -1])[..., None, None] + jnp.einsum(
            "bsgn,bghs,bsghp->bghpn", b_i, decay_s, xdt_i)
        return state, y_diag + y_off

    s0 = jnp.zeros((bsz, grp, hg, p, nst), dtype=x.dtype)
    _, y = lax.scan(step, s0, xs)
    return jnp.moveaxis(y, 0, 1).reshape(bsz, s, grp, hg, p)


def hybrid_layer(x, pre_mix_norm, w_in, gdn_conv_w, gdn_a_log, gdn_dt_bias, gdn_norm_w,
                 ssd_conv_w, ssd_conv_b, ssd_a_log, ssd_dt_bias, ssd_d, ssd_norm_w,
                 w_out, post_mix_norm, pre_ffn_norm, w_up, ffn_conv_w, ffn_conv_b,
                 w_down, post_ffn_norm):
    f32 = jnp.float32
    bsz, s, _ = x.shape
    h = rms_norm(x, pre_mix_norm)
    proj = h @ w_in
    offsets = np.cumsum(IN_SPLITS)[:-1].tolist()
    q, k, v, z_a, b_a, a_a, z_s, x_s, b_s, c_s, dt_s = jnp.split(proj, offsets, axis=-1)

    qkv = jax.nn.silu(causal_dwconv(jnp.concatenate([q, k, v], axis=-1), gdn_conv_w))
    q, k, v = jnp.split(qkv, [GDN_QK, 2 * GDN_QK], axis=-1)
    q = l2_normalize(q.reshape(bsz, s, GDN_HEADS, GDN_DK)) * (GDN_DK ** -0.5)
    k = l2_normalize(k.reshape(bsz, s, GDN_HEADS, GDN_DK))
    v = v.reshape(bsz, s, GDN_HEADS, GDN_DV).astype(f32)
    beta = jax.nn.sigmoid(b_a.astype(f32))
    g = -jnp.exp(gdn_a_log.astype(f32)) * jax.nn.softplus(a_a.astype(f32) + gdn_dt_bias.astype(f32))
    o_a = gdn_chunked(q, k, v, g, beta)
    o_a = rms_norm(o_a, gdn_norm_w) * jax.nn.silu(z_a.reshape(bsz, s, GDN_HEADS, GDN_DV).astype(f32))
    o_a = o_a.reshape(bsz, s, GDN_V).astype(x.dtype)

    xbc = jax.nn.silu(causal_dwconv(jnp.concatenate([x_s, b_s, c_s], axis=-1), ssd_conv_w, ssd_conv_b))
    xs_, bm, cm = jnp.split(xbc, [SSD_D, SSD_D + SSD_BC], axis=-1)
    xs_ = xs_.reshape(bsz, s, SSD_GROUPS, SSD_HPG, SSD_HEADDIM).astype(f32)
    bm = bm.reshape(bsz, s, SSD_GROUPS, SSD_STATE).astype(f32)
    cm = cm.reshape(bsz, s, SSD_GROUPS, SSD_STATE).astype(f32)
    dt = jax.nn.softplus(dt_s.astype(f32) + ssd_dt_bias.astype(f32)).reshape(bsz, s, SSD_GROUPS, SSD_HPG)
    a_neg = -jnp.exp(ssd_a_log.astype(f32)).reshape(SSD_GROUPS, SSD_HPG)
    y = ssd_chunked(xs_, dt, a_neg, bm, cm)
    y = y + ssd_d.astype(f32).reshape(SSD_GROUPS, SSD_HPG)[..., None] * xs_
    y = y.reshape(bsz, s, SSD_D) * jax.nn.silu(z_s.astype(f32))
    y = rms_norm(y.reshape(bsz, s, SSD_GROUPS, SSD_D // SSD_GROUPS),
                 ssd_norm_w.reshape(SSD_GROUPS, SSD_D // SSD_GROUPS))
    o_s = y.reshape(bsz, s, SSD_D).astype(x.dtype)

    mix = jnp.concatenate([o_a, o_s], axis=-1) @ w_out
    x = x + rms_norm(mix, post_mix_norm)

    h = rms_norm(x, pre_ffn_norm)
    u = causal_dwconv(h @ w_up, ffn_conv_w, ffn_conv_b)
    gate, up = jnp.split(u, 2, axis=-1)
    f = (jax.nn.silu(gate) * up) @ w_down
    return x + rms_norm(f, post_ffn_norm)


def setup_inputs(seed: int = 0) -> dict:
    key = jax.random.key(seed)
    ks = jax.random.split(key, 24)
    f32 = jnp.float32
    L = DEPTH

    def nrm(k, shape, scale):
        return jax.random.normal(k, shape, f32) * scale

    def gain(k, shape):
        return 1.0 + 0.05 * jax.random.normal(k, shape, f32)

    def dt_bias(k, shape):
        dt = jnp.exp(jax.random.uniform(k, shape, f32, math.log(1e-3), math.log(1e-1)))
        return dt + jnp.log(-jnp.expm1(-dt))

    def a_log(k, shape):
        return jnp.log(jax.random.uniform(k, shape, f32, 1.0, 16.0))

    return {
        "x": nrm(ks[0], (BATCH, SEQ, D_MODEL), 1.0),
        "pre_mix_norm": gain(ks[1], (L, D_MODEL)),
        "w_in": nrm(ks[2], (L, D_MODEL, D_IN_PROJ), D_MODEL ** -0.5),
        "gdn_conv_w": nrm(ks[3], (L, CONV_K, GDN_CONV_CH), CONV_K ** -0.5),
        "gdn_a_log": a_log(ks[4], (L, GDN_HEADS)),
        "gdn_dt_bias": dt_bias(ks[5], (L, GDN_HEADS)),
        "gdn_norm_w": gain(ks[6], (L, GDN_DV)),
        "ssd_conv_w": nrm(ks[7], (L, CONV_K, SSD_CONV_CH), CONV_K ** -0.5),
        "ssd_conv_b": nrm(ks[8], (L, SSD_CONV_CH), 0.02),
        "ssd_a_log": a_log(ks[9], (L, SSD_HEADS)),
        "ssd_dt_bias": dt_bias(ks[10], (L, SSD_HEADS)),
        "ssd_d": gain(ks[11], (L, SSD_HEADS)),
        "ssd_norm_w": gain(ks[12], (L, SSD_D)),
        "w_out": nrm(ks[13], (L, MIX_WIDTH, D_MODEL), MIX_WIDTH ** -0.5),
        "post_mix_norm": gain(ks[14], (L, D_MODEL)),
        "pre_ffn_norm": gain(ks[15], (L, D_MODEL)),
        "w_up": nrm(ks[16], (L, D_MODEL, 2 * D_FF), D_MODEL ** -0.5),
        "ffn_conv_w": nrm(ks[17], (L, FFN_CONV_K, 2 * D_FF), FFN_CONV_K ** -0.5),
        "ffn_conv_b": nrm(ks[18], (L, 2 * D_FF), 0.02),
        "w_down": nrm(ks[19], (L, D_FF, D_MODEL), D_FF ** -0.5),
        "post_ffn_norm": gain(ks[20], (L, D_MODEL)),
    }


def reference(x, pre_mix_norm, w_in, gdn_conv_w, gdn_a_log, gdn_dt_bias, gdn_norm_w,
              ssd_conv_w, ssd_conv_b, ssd_a_log, ssd_dt_bias, ssd_d, ssd_norm_w,
              w_out, post_mix_norm, pre_ffn_norm, w_up, ffn_conv_w, ffn_conv_b,
              w_down, post_ffn_norm):
    for l in range(DEPTH):
        x = hybrid_layer(x, pre_mix_norm[l], w_in[l], gdn_conv_w[l], gdn_a_log[l], gdn_dt_bias[l],
                         gdn_norm_w[l], ssd_conv_w[l], ssd_conv_b[l], ssd_a_log[l], ssd_dt_bias[l],
                         ssd_d[l], ssd_norm_w[l], w_out[l], post_mix_norm[l], pre_ffn_norm[l],
                         w_up[l], ffn_conv_w[l], ffn_conv_b[l], w_down[l], post_ffn_norm[l])
    return x
```

```python
import numpy as np
import concourse.bass as bass
import concourse.mybir as mybir
from concourse.bass_utils import run_bass_kernel_spmd

F32 = mybir.dt.float32
BF16 = mybir.dt.bfloat16
AF = mybir.ActivationFunctionType
ALU = mybir.AluOpType

D = 1024
NH = 8
SH = 16
DFF = 2816
NJ_IN = 52
NJ_UP = 44
KC_DN = 22
EPS = 1e-6
BIG = 30000.0
NCORES = 8

PO = {}
_o = 0
for _n, _w in [("w1", 8), ("w2", 8), ("w3", 8), ("w4", 8), ("gnw", 1), ("snw", 8), ("dexp", 8),
               ("cwg", 96), ("cws", 48), ("cbs", 12), ("cwf", 132), ("cbf", 44), ("alog", 24), ("dtb", 24)]:
    PO[_n] = (_o, _o + _w)
    _o += _w
NPAR = _o


class Prog:
    def __init__(self, nc):
        self.nc = nc
        self.eng = {'pe': nc.tensor, 'act': nc.scalar, 'dve': nc.vector, 'pool': nc.gpsimd, 'sp': nc.sync}
        self.sem = {e: nc.alloc_semaphore('sem_' + e) for e in self.eng}
        self.cnt = {e: 0 for e in self.eng}
        self.waited = {e: {} for e in self.eng}
        self.lastw = {}
        self.readers = {}
        self.dsem = {}
        self.ninst = 0
        self.nwait = 0
        self.rr = 0

    def _wait(self, e, dep):
        key, h, v, src = dep
        if self.waited[e].get(key, 0) >= v:
            return
        if src == e and v > self.cnt[e]:
            return
        self.eng[e].wait_ge(h, v)
        self.nwait += 1
        self.waited[e][key] = v

    def _deps(self, e, reads, writes):
        deps = []
        for r in reads:
            d = self.lastw.get(r)
            if d is not None:
                deps.append(d)
        for w in writes:
            d = self.lastw.get(w)
            if d is not None:
                deps.append(d)
            rd = self.readers.get(w)
            if rd:
                for k, d in rd.items():
                    deps.append(d)
        for d in deps:
            self._wait(e, d)

    def op(self, e, fn, reads=(), writes=(), signal=True):
        self._deps(e, reads, writes)
        ins = fn(self.eng[e])
        self.ninst += 1
        if signal:
            ins.then_inc(self.sem[e], 1)
            self.cnt[e] += 1
            v = self.cnt[e]
        else:
            v = self.cnt[e] + 1
        dep = ('e_' + e, self.sem[e], v, e)
        for w in writes:
            self.lastw[w] = dep
            self.readers[w] = {}
        for r in reads:
            if r not in writes:
                self.readers.setdefault(r, {})[e] = dep
        return ins

    def dma(self, q, out, in_, reads=(), writes=(), semname=None, **kw):
        self._deps(q, reads, writes)
        if semname is None:
            semname = 'd_' + (writes[0] if writes else reads[0])
        if semname not in self.dsem:
            self.dsem[semname] = [self.nc.alloc_semaphore(semname), 0]
        s = self.dsem[semname]
        ins = self.eng[q].dma_start(out=out, in_=in_, **kw)
        ins.then_inc(s[0], 16)
        s[1] += 16
        dep = (semname, s[0], s[1], None)
        for w in writes:
            self.lastw[w] = dep
            self.readers[w] = {}
        for r in reads:
            self.readers.setdefault(r, {})[semname] = dep
        return ins

    def retire(self, old, new):
        deps = []
        for o in old:
            d = self.lastw.get(o)
            if d is not None:
                deps.append(d)
            for k, d in self.readers.get(o, {}).items():
                deps.append(d)
        best = {}
        for d in deps:
            if d[0] not in best or best[d[0]][2] < d[2]:
                best[d[0]] = d
        for n in new:
            rd = self.readers.setdefault(n, {})
            for k, d in best.items():
                rd['al_' + k] = (d[0], d[1], d[2], 'alias')

    def finish(self, e='sp'):
        for name, (h, v) in self.dsem.items():
            self._wait(e, (name, h, v, None))
        for f in self.eng:
            if f != e and self.cnt[f] > 0:
                self._wait(e, ('e_' + f, self.sem[f], self.cnt[f], f))

    def ew(self):
        self.rr += 1
        return 'dve' if (self.rr & 1) else 'pool'


def bc_mid(ap, n):
    return ap.unsqueeze(1).to_broadcast([ap.shape[0], n, ap.shape[1]])


def bc_last(ap, n):
    return ap.unsqueeze(2).to_broadcast([ap.shape[0], ap.shape[1], n])


def build(NSEQ, SEQ, TB, debug=()):
    nc = bass.Bass("TRN2", target_bir_lowering=False)
    NBLK = SEQ // TB
    NT = TB // 128
    dt_ = nc.dram_tensor
    xT = dt_("xT", [NSEQ, 8, 128, SEQ], F32, kind="ExternalInput").ap()
    params_d = dt_("params", [128, NPAR], F32, kind="ExternalInput").ap()
    consts_d = dt_("consts", [128, 8, 128], F32, kind="ExternalInput").ap()
    w_in_d = dt_("w_in_t", [NJ_IN, 128, 1024], F32, kind="ExternalInput").ap()
    w_sm_d = dt_("w_small", [128, 256], F32, kind="ExternalInput").ap()
    w_out_d = dt_("w_out_t", [8, 128, 2048], F32, kind="ExternalInput").ap()
    w_up_d = dt_("w_up_t", [NJ_UP, 128, 1024], F32, kind="ExternalInput").ap()
    w_dn_d = dt_("w_dn_t", [8, 128, KC_DN * 128], F32, kind="ExternalInput").ap()
    outT = dt_("outT", [NSEQ, 8, 128, SEQ], F32, kind="ExternalOutput").ap()
    sc_in = dt_("sc_in", [NJ_IN, 128, 1024], BF16, kind="Internal").ap()
    sc_out = dt_("sc_out", [8, 128, 2048], BF16, kind="Internal").ap()
    sc_up = dt_("sc_up", [NJ_UP, 128, 1024], BF16, kind="Internal").ap()
    sc_dn = dt_("sc_dn", [8, 128, KC_DN * 128], BF16, kind="Internal").ap()

    P = Prog(nc)
    dbg_outs = {}

    def sb(name, shape, dt=F32):
        return nc.alloc_sbuf_tensor(name, shape, dt).ap()

    def dump(name, ap, res):
        if name not in debug:
            return
        shp = list(ap.shape)
        key = "dbg_" + name
        k = dbg_outs.get(key, 0)
        dbg_outs[key] = k + 1
        t = dt_("%s_%d" % (key, k), shp, ap.dtype, kind="ExternalOutput").ap()
        P.dma('sp', t, ap, reads=res, semname='dbgsem')

    PAR = sb("PAR", [128, NPAR])
    CST = sb("CST", [128, 8, 128])
    identF = CST[:, 0, :]
    mcumF = CST[:, 1, :]
    onesF = CST[:, 2, :]
    mL = CST[:, 3, :]
    mUs = CST[:, 4, :]
    mUi = CST[:, 5, :]
    mBD = CST[:, 6, :]
    mOFF = CST[:, 7, :]
    identB = sb("identB", [128, 128], BF16)
    onesB = sb("onesB", [128, 128], BF16)
    WSM = sb("WSM", [128, 8, 32], BF16)
    NEGA = sb("NEGA", [128, 24])
    X = sb("X", [128, 8, TB])
    H = sb("H", [128, 8, TB], BF16)
    RSTD = sb("RSTD", [128, TB])
    REGA = sb("REGA", [128, 52 * 512], BF16)
    QKV = REGA[:, 0:24 * TB].rearrange("p (a b) -> p a b", a=24)
    XBC = REGA[:, 24 * 512:24 * 512 + 12 * TB].rearrange("p (a b) -> p a b", a=12)
    ZA = REGA[:, 36 * 512:36 * 512 + 8 * TB].rearrange("p (a b) -> p a b", a=8)
    ZS = REGA[:, 44 * 512:44 * 512 + 8 * TB].rearrange("p (a b) -> p a b", a=8)
    ACTB = REGA[:, 0:22 * TB].rearrange("p (a b) -> p a b", a=22)
    MO = REGA[:, 22 * 512:22 * 512 + 16 * TB].bitcast(F32).rearrange("p (a b) -> p a b", a=8)
    MIX = sb("MIX", [128, 16, TB], BF16)
    SQ = MIX[:, 0:8, :]
    STGall = sb("STGall", [128, 4, TB + 3])
    ACCall = sb("ACCall", [128, 4, TB])
    STG = [STGall[:, i, :] for i in range(4)]
    ACC = [ACCall[:, i, :] for i in range(4)]
    NW = 7
    WS = [sb("WS%d" % i, [128, 1024], BF16) for i in range(NW)]
    TAILG = sb("TAILG", [128, 36, 3])
    TAILF = sb("TAILF", [128, 44, 2])
    SMALL = sb("SMALL", [128, NT, 32])
    S = sb("S", [128, 8, 128])
    Sb = sb("Sb", [128, 8, 128], BF16)
    SS = sb("SS", [128, 16, 64])
    SSb = sb("SSb", [128, 16, 64], BF16)
    import collections

    class RPool:
        def __init__(self, items, lag):
            self.free = collections.deque(items)
            self.recent = collections.deque()
            self.lag = lag

        def get(self):
            while len(self.recent) >= self.lag:
                self.free.append(self.recent.popleft())
            it = self.free.popleft()
            self.recent.append(it)
            return it

        def flush(self):
            while self.recent:
                self.free.append(self.recent.popleft())

        def acquire(self):
            while not self.free:
                yield
            return self.free.popleft()

        def release(self, it):
            self.free.append(it)

    def run_chains(chains):
        chains = list(chains)
        guard = 0
        while chains:
            n0 = P.ninst
            for c in list(chains):
                try:
                    next(c)
                except StopIteration:
                    chains.remove(c)
            guard = guard + 1 if P.ninst == n0 else 0
            assert guard < 1000, "chain deadlock"

    NF = 7
    FPP = RPool([(sb("FP%d" % i, [128, 512]), 'FP%d' % i) for i in range(NF)], 2)

    def ftmp():
        return FPP.get()

    WSMF = FPP.free[0][0][:, 0:256].rearrange("p (a b) -> p a b", a=8)

    NB = 8
    BPP = RPool([(sb("BP%d" % i, [128, 512], BF16), 'BP%d' % i) for i in range(NB)], 2)

    def btmp():
        return BPP.get()

    GP = RPool([(sb("GP%d" % i, [128, 4, 128], BF16), "GP%d" % i) for i in range(27)], 99)

    def acquire_n(pool, n):
        while len(pool.free) < n:
            yield
        return [pool.free.popleft() for _ in range(n)]

    BTM = sb("BTM", [128, 2, 128], BF16)
    GTb = [sb("GT%d" % i, [128, 4, 128], BF16) for i in range(2)]
    CDECb = [sb("CDEC%d" % i, [128, 4, 128], BF16) for i in range(2)]
    XDTb = [sb("XDT%d" % i, [128, 4, 64], BF16) for i in range(2)]
    XDDb = [sb("XDD%d" % i, [128, 4, 64], BF16) for i in range(2)]
    YG = sb("YG", [128, 4, 128])
    BCS = sb("BCS", [128, 256])
    SMB = [{n_: sb("sm%d%s" % (i, n_), [128, w_]) for n_, w_ in
            [("NLB", 8), ("BETA", 8), ("U", 24), ("SPL", 24), ("V", 24), ("CSUM", 24), ("TOT", 24),
             ("DIFF", 24), ("NCS", 24), ("ECS", 24), ("EREM", 24), ("ETOT", 24), ("GB", 8), ("BEG", 8), ("DTE", 16)]}
           for i in range(2)]
    BANKS = RPool([(nc.alloc_psum_tensor("ps%d" % i, [128, 512], F32).ap(), 'ps%d' % i) for i in range(8)], 5)

    def bank():
        return BANKS.get()

    def par(name):
        a, b = PO[name]
        return PAR[:, a:b]

    P.dma('sp', PAR, params_d, writes=['PAR'])
    P.dma('sp', CST, consts_d, writes=['CST'])
    P.dma('sp', FPP.free[0][0][:, 0:256], w_sm_d, writes=['FP0'])
    for j in range(NJ_IN):
        P.dma('pool', sc_in[j], w_in_d[j], writes=['sc_in_g%d' % (j // 13)], semname='c_in%d' % (j // 13))
    for j in range(8):
        P.dma('pool', sc_out[j], w_out_d[j], writes=['sc_out'], semname='c_out')

    def emit_rest_casts():
        for j in range(NJ_UP):
            P.dma('pool', sc_up[j], w_up_d[j], writes=['sc_up'], semname='c_up')
        for j in range(8):
            P.dma('pool', sc_dn[j], w_dn_d[j], writes=['sc_dn'], semname='c_dn')

    P.op('dve', lambda e: e.tensor_copy(out=identB, in_=identF), reads=['CST'], writes=['identB'])
    P.op('dve', lambda e: e.tensor_copy(out=onesB, in_=onesF), reads=['CST'], writes=['onesB'])
    P.op('dve', lambda e: e.tensor_copy(out=WSM, in_=WSMF), reads=['FP0'], writes=['WSM'])
    P.op('act', lambda e: e.activation(out=NEGA, in_=par("alog"), func=AF.Exp), reads=['PAR'], writes=['NEGA'])
    P.op('dve', lambda e: e.tensor_scalar_mul(out=NEGA, in0=NEGA, scalar1=-1.0), reads=['NEGA'], writes=['NEGA'])

    wreq = []
    for s_ in range(NSEQ):
        for b_ in range(NBLK):
            wreq += [('in', j, 0) for j in range(NJ_IN)]
            wreq += [('out', j, q) for j in range(8) for q in range(2)]
            for jj in range(22):
                wreq += [('up', jj, 0), ('up', 22 + jj, 0)]
            wreq += [('dn', j, q) for j in range(8) for q in range(3)]
    wstate = {'issued': 0, 'next': 0}

    def w_issue(upto):
        while wstate['issued'] < min(upto, len(wreq)):
            k = wstate['issued']
            kind, j, q = wreq[k]
            slot = k % NW
            if kind == 'in':
                P.dma('sp', WS[slot], sc_in[j], reads=['sc_in_g%d' % (j // 13)], writes=['WS%d' % slot])
            elif kind == 'up':
                P.dma('sp', WS[slot], sc_up[j], reads=['sc_up'], writes=['WS%d' % slot])
            elif kind == 'out':
                P.dma('sp', WS[slot], sc_out[j][:, q * 1024:(q + 1) * 1024], reads=['sc_out'], writes=['WS%d' % slot])
            else:
                n = 1024 if q < 2 else (KC_DN * 128 - 2048)
                P.dma('sp', WS[slot][:, 0:n], sc_dn[j][:, q * 1024:q * 1024 + n], reads=['sc_dn'], writes=['WS%d' % slot])
            wstate['issued'] += 1

    def w_get(kind, j, q=0):
        k = wstate['next']
        assert wreq[k] == (kind, j, q), (wreq[k], kind, j, q)
        w_issue(k + NW)
        wstate['next'] += 1
        slot = k % NW
        return WS[slot].rearrange("p (a b) -> p a b", a=8), 'WS%d' % slot

    def rms_sq_step(bst, bstr, kc, src_ap, srcres, nk=8, src_is_psum=None, sqbuf=None, sqname='MIX'):
        sq_ = SQ if sqbuf is None else sqbuf
        rn_ = '%s%d' % (sqname, kc)
        w_ = [rn_] + ([src_is_psum] if src_is_psum else [])
        P.op('act', lambda e: e.activation(out=sq_[:, kc, :], in_=src_ap, func=AF.Square), reads=srcres, writes=w_)
        P.op('pe', lambda e: e.matmul(bst[:, 0:TB], lhsT=onesB, rhs=sq_[:, kc, :], start=(kc == 0), stop=(kc == nk - 1)),
             reads=[rn_, 'onesB'], writes=[bstr], signal=(kc == nk - 1))

    def rms_finish(bst, bstr, rstd, rstdr, nfeat):
        P.op('act', lambda e: e.activation(out=rstd[:, 0:TB], in_=bst[:, 0:TB], func=AF.Ln, scale=1.0 / nfeat, bias=EPS),
             reads=[], writes=[bstr, rstdr])
        BANKS.release((bst, bstr))
        P.op('act', lambda e: e.activation(out=rstd[:, 0:TB], in_=rstd[:, 0:TB], func=AF.Exp, scale=-0.5),
             reads=[rstdr], writes=[rstdr])

    mixres = ['MIX%d' % j for j in range(16)]
    sqres_all = mixres[0:8]
    regA_mix = (['QKV%d_%d' % (j, i) for j in range(24) for i in range(NT)] + ['XBC%d' % j for j in range(12)] +
                ['ZA%d' % j for j in range(8)] + ['ZS%d' % j for j in range(8)])
    regA_ffn = ['ACTB%d' % j for j in range(22)] + ['MO%d' % j for j in range(8)]

    STGP = RPool([(STG[i], ACC[i], 'STG%d' % i, 'ACC%d' % i) for i in range(len(STG))], 99)

    def conv_chain(bk, bkr, cw, ntap, bias_ap, tail, tres, first_blk, T, final, npool, tiny):
        h_ = ntap - 1
        slot = yield from STGP.acquire()
        stg, acc, stgr, accr = slot
        if bias_ap is None:
            P.op('act', lambda e: e.activation(out=acc[:, 0:T], in_=bk[:, 0:T], func=AF.Identity, scale=cw[:, h_:h_ + 1]),
                 reads=['PAR'], writes=[bkr, accr])
        else:
            P.op('act', lambda e: e.activation(out=acc[:, 0:T], in_=bk[:, 0:T], func=AF.Identity, scale=cw[:, h_:h_ + 1], bias=bias_ap),
                 reads=['PAR'], writes=[bkr, accr])
        P.op('act', lambda e: e.activation(out=stg[:, h_:h_ + T], in_=bk[:, 0:T], func=AF.Copy), reads=[], writes=[bkr, stgr])
        BANKS.release((bk, bkr))
        if first_blk:
            P.op(tiny, lambda e: e.memset(stg[:, 0:h_], 0.0), reads=[], writes=[stgr])
        else:
            P.op(tiny, lambda e: e.tensor_copy(out=stg[:, 0:h_], in_=tail), reads=[tres], writes=[stgr])
        yield
        ptmp = []
        for k in range(h_ - npool, h_):
            tmp, tmpr = yield from FPP.acquire()
            P.op('pool', lambda e, k=k, tmp=tmp: e.tensor_tensor(out=tmp[:, 0:T], in0=stg[:, k:k + T], in1=cw[:, k:k + 1].to_broadcast([128, T]), op=ALU.mult),
                 reads=[stgr, 'PAR'], writes=[tmpr])
            ptmp.append((tmp, tmpr))
        for k in range(h_ - npool):
            P.op('dve', lambda e, k=k: e.scalar_tensor_tensor(out=acc[:, 0:T], in0=stg[:, k:k + T], scalar=cw[:, k:k + 1], in1=acc[:, 0:T],
                                                            op0=ALU.mult, op1=ALU.add),
                 reads=[stgr, accr, 'PAR'], writes=[accr])
        for tmp, tmpr in ptmp:
            P.op('pool', lambda e, tmp=tmp: e.tensor_tensor(out=acc[:, 0:T], in0=acc[:, 0:T], in1=tmp[:, 0:T], op=ALU.add), reads=[tmpr, accr], writes=[accr])
            FPP.release((tmp, tmpr))
        P.op(tiny, lambda e: e.tensor_copy(out=tail, in_=stg[:, T:T + h_]), reads=[stgr], writes=[tres])
        yield
        yield
        final(acc, accr)
        STGP.release(slot)

    def step_active(active):
        for c in list(active):
            try:
                next(c)
            except StopIteration:
                active.remove(c)

    def take_bank(active=()):
        n = 0
        while not BANKS.free:
            step_active(active)
            n += 1
            assert n < 100, "no free PSUM bank"
        return BANKS.free.popleft()

    for s_ in range(NSEQ):
        for b_ in range(NBLK):
            first_blk = (b_ == 0)
            t0b = b_ * TB
            xres = ['X%d' % kc for kc in range(8)]

            xs = [ACC[k][:, 0:TB] for k in range(4)] + [STG[k][:, 0:TB] for k in range(4)]
            xsres = ['ACC%d' % k for k in range(4)] + ['STG%d' % k for k in range(4)]

            def load_x(ss, bb):
                t0_ = bb * TB
                P.dma('sp', ACCall[:, :, 0:TB], xT[ss, 0:4, :, t0_:t0_ + TB].rearrange("k p t -> p k t"), writes=xsres[0:4], semname='d_xsa')
                P.dma('sp', STGall[:, :, 0:TB], xT[ss, 4:8, :, t0_:t0_ + TB].rearrange("k p t -> p k t"), writes=xsres[4:8], semname='d_xsb')

            if s_ == 0 and b_ == 0:
                load_x(0, 0)
            if first_blk:
                P.op('dve', lambda e: e.memset(S, 0.0), writes=['S0', 'S1'])
                P.op('dve', lambda e: e.memset(Sb, 0.0), writes=['Sb0', 'Sb1'])
                P.op('dve', lambda e: e.memset(SS, 0.0), writes=['SS%d' % k for k in range(4)])
                P.op('dve', lambda e: e.memset(SSb, 0.0), writes=['SSb%d' % k for k in range(4)])
            P.retire(regA_ffn, regA_mix)
            BANKS.flush()
            FPP.flush()
            BPP.flush()
            bst, bstr = take_bank()
            for kc in range(8):
                rms_sq_step(bst, bstr, kc, xs[kc], [xsres[kc]])
            rms_finish(bst, bstr, RSTD, 'RSTD', D)
            cp_eng = 'act' if (s_ == 0 and b_ == 0) else 'pool'
            for kc in range(8):
                P.op('dve', lambda e, kc=kc: e.scalar_tensor_tensor(out=H[:, kc, :], in0=xs[kc], scalar=par("w1")[:, kc:kc + 1], in1=RSTD,
                                                                    op0=ALU.mult, op1=ALU.mult),
                     reads=[xsres[kc], 'RSTD', 'PAR'], writes=['H%d' % kc])
                if cp_eng == 'act':
                    P.op('act', lambda e, kc=kc: e.activation(out=X[:, kc, :], in_=xs[kc], func=AF.Copy), reads=[xsres[kc]], writes=[xres[kc]])
                else:
                    P.op('pool', lambda e, kc=kc: e.tensor_copy(out=X[:, kc, :], in_=xs[kc]), reads=[xsres[kc]], writes=[xres[kc]])
            hres = ['H%d' % kc for kc in range(8)]
            dump("H", H, hres)
            BANKS.flush()
            FPP.flush()
            BPP.flush()
            blk0 = (s_ == 0 and b_ == 0)
            tiny = 'dve' if blk0 else 'pool'
            npool_in = 0 if blk0 else 1
            allt = lambda nm: ['%s_%d' % (nm, i) for i in range(NT)]
            active = []
            for j in range(NJ_IN):
                wv, wr = w_get('in', j)
                bk, bkr = take_bank(active)
                for kc in range(8):
                    P.op('pe', lambda e, kc=kc: e.matmul(bk[:, 0:TB], lhsT=wv[:, kc, :], rhs=H[:, kc, :], start=(kc == 0), stop=(kc == 7)),
                         reads=[wr] + hres, writes=[bkr], signal=(kc == 7))
                if j < 24:
                    cwj = par("cwg")[:, 4 * j:4 * j + 4]
                    fin = lambda acc, accr, j=j: P.op('act', lambda e: e.activation(out=QKV[:, j, :], in_=acc[:, 0:TB], func=AF.Silu), reads=[accr], writes=allt('QKV%d' % j))
                    active.append(conv_chain(bk, bkr, cwj, 4, None, TAILG[:, j, :], 'TG%d' % j, first_blk, TB, fin, npool_in * (j & 1), tiny))
                elif j < 32:
                    P.op('act', lambda e: e.activation(out=ZA[:, j - 24, :], in_=bk[:, 0:TB], func=AF.Silu), reads=[], writes=[bkr, 'ZA%d' % (j - 24)])
                    BANKS.release((bk, bkr))
                elif j < 40:
                    P.op('act', lambda e: e.activation(out=ZS[:, j - 32, :], in_=bk[:, 0:TB], func=AF.Silu), reads=[], writes=[bkr, 'ZS%d' % (j - 32)])
                    BANKS.release((bk, bkr))
                else:
                    jj = j - 40
                    cwj = par("cws")[:, 4 * jj:4 * jj + 4]
                    fin = lambda acc, accr, jj=jj: P.op('act', lambda e: e.activation(out=XBC[:, jj, :], in_=acc[:, 0:TB], func=AF.Silu), reads=[accr], writes=['XBC%d' % jj])
                    active.append(conv_chain(bk, bkr, cwj, 4, par("cbs")[:, jj:jj + 1], TAILG[:, 24 + jj, :], 'TG%d' % (24 + jj), first_blk, TB, fin, npool_in * (jj & 1), tiny))
                step_active(active)
            run_chains(active)
            if blk0:
                emit_rest_casts()
            for i in range(NT):
                bk, bkr = take_bank()
                for kc in range(8):
                    P.op('pe', lambda e, kc=kc: e.matmul(bk[:, 0:32], lhsT=H[:, kc, i * 128:(i + 1) * 128], rhs=WSM[:, kc, :], start=(kc == 0), stop=(kc == 7)),
                         reads=['WSM'] + hres, writes=[bkr], signal=(kc == 7))
                P.op('dve', lambda e: e.tensor_copy(out=SMALL[:, i, :], in_=bk[:, 0:32]), reads=[], writes=[bkr, 'SMALL%d' % i])
                BANKS.release((bk, bkr))
            dump("QKV", QKV, [x_ for j in range(24) for x_ in allt('QKV%d' % j)])
            dump("XBC", XBC, ['XBC%d' % j for j in range(12)])
            dump("SMALL", SMALL, ['SMALL%d' % i for i in range(NT)])
            P.retire(sqres_all, mixres)
            BANKS.flush()
            FPP.flush()
            BPP.flush()
            gates_emitted = [False] * NT
            mk_eng = 'dve' if blk0 else 'pool'
            cons_done = [0] * NT
            f2 = lambda a: a.rearrange("p a b -> p (a b)")
            v4 = lambda a: a.rearrange("p (a b) -> p a b", a=4)

            def gates_all():
                for i in range(NT):
                    while i >= 2 and cons_done[i - 2] < 3:
                        yield
                    sm = SMB[i % 2]
                    smr = 'SMB%d' % (i % 2)
                    SMi = SMALL[:, i, :]
                    smallr = 'SMALL%d' % i
                    P.op('act', lambda e: e.activation(out=sm["NLB"], in_=SMi[:, 0:8], func=AF.Exp, scale=-1.0), reads=[smallr], writes=[smr + 'NLB'])
                    P.op('act', lambda e: e.activation(out=sm["NLB"], in_=sm["NLB"], func=AF.Ln, bias=1.0), reads=[smr + 'NLB'], writes=[smr + 'NLB'])
                    P.op('act', lambda e: e.activation(out=sm["BETA"], in_=sm["NLB"], func=AF.Exp, scale=-1.0), reads=[smr + 'NLB'], writes=[smr + 'BETA'])
                    P.op('dve', lambda e: e.tensor_tensor(out=sm["U"], in0=SMi[:, 8:32], in1=par("dtb"), op=ALU.add), reads=[smallr, 'PAR'], writes=[smr + 'U'])
                    yield
                    P.op('act', lambda e: e.activation(out=sm["SPL"], in_=sm["U"], func=AF.Exp), reads=[smr + 'U'], writes=[smr + 'SPL'])
                    P.op('act', lambda e: e.activation(out=sm["SPL"], in_=sm["SPL"], func=AF.Ln, bias=1.0), reads=[smr + 'SPL'], writes=[smr + 'SPL'])
                    yield
                    P.op('dve', lambda e: e.tensor_tensor(out=sm["V"], in0=sm["SPL"], in1=NEGA, op=ALU.mult), reads=[smr + 'SPL', 'NEGA'], writes=[smr + 'V'])
                    bk, bkr = yield from BANKS.acquire()
                    bk2, bk2r = yield from BANKS.acquire()
                    P.op('pe', lambda e: e.matmul(bk[:, 0:24], lhsT=mcumF, rhs=sm["V"], start=True, stop=True), reads=[smr + 'V', 'CST'], writes=[bkr])
                    P.op('pe', lambda e: e.matmul(bk2[:, 0:24], lhsT=onesF, rhs=sm["V"], start=True, stop=True), reads=[smr + 'V', 'CST'], writes=[bk2r])
                    yield
                    P.op('dve', lambda e: e.tensor_copy(out=sm["CSUM"], in_=bk[:, 0:24]), reads=[], writes=[bkr, smr + 'CSUM'])
                    P.op('act', lambda e: e.activation(out=sm["TOT"], in_=bk2[:, 0:24], func=AF.Copy), reads=[], writes=[bk2r, smr + 'TOT'])
                    P.op('act', lambda e: e.activation(out=sm["NCS"], in_=bk[:, 0:24], func=AF.Copy, scale=-1.0), reads=[], writes=[bkr, smr + 'NCS'])
                    BANKS.release((bk, bkr))
                    BANKS.release((bk2, bk2r))
                    yield
                    P.op('dve', lambda e: e.tensor_tensor(out=sm["DIFF"], in0=sm["TOT"], in1=sm["CSUM"], op=ALU.subtract), reads=[smr + 'TOT', smr + 'CSUM'], writes=[smr + 'DIFF'])
                    P.op('act', lambda e: e.activation(out=sm["ECS"], in_=sm["CSUM"], func=AF.Exp), reads=[smr + 'CSUM'], writes=[smr + 'ECS'])
                    P.op('act', lambda e: e.activation(out=sm["ETOT"], in_=sm["TOT"], func=AF.Exp), reads=[smr + 'TOT'], writes=[smr + 'ETOT'])
                    P.op('dve', lambda e: e.tensor_tensor(out=sm["GB"], in0=sm["CSUM"][:, 0:8], in1=sm["NLB"], op=ALU.subtract), reads=[smr + 'CSUM', smr + 'NLB'], writes=[smr + 'GB'])
                    yield
                    P.op('act', lambda e: e.activation(out=sm["EREM"], in_=sm["DIFF"], func=AF.Exp), reads=[smr + 'DIFF'], writes=[smr + 'EREM'])
                    P.op('act', lambda e: e.activation(out=sm["BEG"], in_=sm["GB"], func=AF.Exp), reads=[smr + 'GB'], writes=[smr + 'BEG'])
                    yield
                    P.op('dve', lambda e: e.tensor_tensor(out=sm["DTE"], in0=sm["SPL"][:, 8:24], in1=sm["EREM"][:, 8:24], op=ALU.mult), reads=[smr + 'SPL', smr + 'EREM'], writes=[smr + 'DTE'])
                    gates_emitted[i] = True
                    yield

            rec_done = [[False] * NT for _ in range(2)]
            gdn_items = [(i, hg) for i in range(NT) for hg in range(2)]

            def gdn_worker():
                while gdn_items:
                    i, hg = gdn_items.pop(0)
                    while not gates_emitted[i]:
                        yield
                    yield from gdn_chain(i, hg)
                    cons_done[i] += 1

            def ssd_all():
                for i in range(NT):
                    while not gates_emitted[i]:
                        yield
                    yield from ssd_chain(i)
                    cons_done[i] += 1

            def gdn_chain(i, hg):
                tk = slice(i * 128, (i + 1) * 128)
                sm = SMB[i % 2]
                smr = 'SMB%d' % (i % 2)
                h0 = 4 * hg
                qres = ['QKV%d_%d' % (h0 + k, i) for k in range(4)]
                kres = ['QKV%d_%d' % (8 + h0 + k, i) for k in range(4)]
                vres = ['QKV%d_%d' % (16 + h0 + k, i) for k in range(4)]
                q4 = QKV[:, h0:h0 + 4, tk]
                k4 = QKV[:, 8 + h0:8 + h0 + 4, tk]
                csum4 = bc_last(sm["CSUM"][:, h0:h0 + 4], 128)
                gb4 = bc_last(sm["GB"][:, h0:h0 + 4], 128)
                for x4, xres, sc in ((k4, kres, 1.0), (q4, qres, 128.0 ** -0.5)):
                    sq, sqr = yield from BPP.acquire()
                    P.op('act', lambda e: e.activation(out=v4(sq), in_=x4, func=AF.Square), reads=xres, writes=[sqr])
                    yield
                    bk, bkr = yield from BANKS.acquire()
                    P.op('pe', lambda e: e.matmul(bk, lhsT=onesB, rhs=sq, start=True, stop=True), reads=[sqr, 'onesB'], writes=[bkr])
                    BPP.release((sq, sqr))
                    yield
                    rn, rnr = yield from FPP.acquire()
                    P.op('act', lambda e: e.activation(out=rn, in_=bk, func=AF.Ln, bias=EPS), reads=[], writes=[bkr, rnr])
                    BANKS.release((bk, bkr))
                    P.op('act', lambda e: e.activation(out=rn, in_=rn, func=AF.Exp, scale=-0.5), reads=[rnr], writes=[rnr])
                    yield
                    P.op('dve', lambda e, sc=sc: e.scalar_tensor_tensor(out=x4, in0=x4, scalar=sc, in1=v4(rn), op0=ALU.mult, op1=ALU.mult),
                         reads=xres + [rnr], writes=xres)
                    FPP.release((rn, rnr))
                    yield
                bR1, bR1r = yield from BANKS.acquire()
                for hl in range(4):
                    P.op('pe', lambda e, hl=hl: e.matmul(bR1[:, hl * 128:(hl + 1) * 128], lhsT=sm["CSUM"][:, h0 + hl:h0 + hl + 1].to_broadcast([128, 128]), rhs=identF, start=True, stop=True),
                         reads=[smr + 'CSUM', 'CST'], writes=[bR1r], signal=(hl == 3))
                yield
                gml, gmlr = yield from FPP.acquire()
                P.op('dve', lambda e: e.scalar_tensor_tensor(out=v4(gml), in0=v4(bR1), scalar=-1.0, in1=bc_mid(mL, 4), op0=ALU.mult, op1=ALU.add), reads=['CST'], writes=[bR1r, gmlr])
                yield
                EL, ELr = yield from BPP.acquire()
                for hl in range(4):
                    P.op('act', lambda e, hl=hl: e.activation(out=EL[:, hl * 128:(hl + 1) * 128], in_=gml[:, hl * 128:(hl + 1) * 128], func=AF.Exp, bias=sm["GB"][:, h0 + hl:h0 + hl + 1]),
                         reads=[gmlr, smr + 'GB'], writes=[ELr])
                FPP.release((gml, gmlr))
                gmq, gmqr = yield from FPP.acquire()
                P.op('dve', lambda e: e.tensor_tensor(out=v4(gmq), in0=v4(bR1), in1=bc_mid(mUi, 4), op=ALU.add), reads=['CST'], writes=[bR1r, gmqr])
                yield
                EQ, EQr = yield from BPP.acquire()
                for hl in range(4):
                    P.op('act', lambda e, hl=hl: e.activation(out=EQ[:, hl * 128:(hl + 1) * 128], in_=gmq[:, hl * 128:(hl + 1) * 128], func=AF.Exp, bias=sm["NCS"][:, h0 + hl:h0 + hl + 1]),
                         reads=[gmqr, smr + 'NCS'], writes=[EQr])
                FPP.release((gmq, gmqr))
                er1, er1r = yield from BPP.acquire()
                P.op('act', lambda e: e.activation(out=er1, in_=bR1, func=AF.Exp), reads=[], writes=[bR1r, er1r])
                BANKS.release((bR1, bR1r))
                yield
                bR2, bR2r = yield from BANKS.acquire()
                for hl in range(4):
                    P.op('pe', lambda e, hl=hl: e.matmul(bR2[:, hl * 128:(hl + 1) * 128], lhsT=sm["GB"][:, h0 + hl:h0 + hl + 1].to_broadcast([128, 128]), rhs=identF, start=True, stop=True),
                         reads=[smr + 'GB', 'CST'], writes=[bR2r], signal=(hl == 3))
                (QD, QDr), (QKT, QKTr), (Pc, Pcr), (PTc, PTcr), (Pn, Pnr), (PTn, PTnr), (Tc, Tcr), (Tn, Tnr), (NOT, NOTr) = yield from acquire_n(GP, 9)
                P.op('dve', lambda e: e.tensor_tensor(out=QD, in0=q4, in1=v4(er1), op=ALU.mult), reads=qres + [er1r], writes=[QDr])
                BPP.release((er1, er1r))
                yield
                gmu, gmur = yield from FPP.acquire()
                P.op('dve', lambda e: e.tensor_tensor(out=v4(gmu), in0=v4(bR2), in1=bc_mid(mUs, 4), op=ALU.add), reads=['CST'], writes=[bR2r, gmur])
                BANKS.release((bR2, bR2r))
                yield
                EU, EUr = yield from BPP.acquire()
                for hl in range(4):
                    P.op('act', lambda e, hl=hl: e.activation(out=EU[:, hl * 128:(hl + 1) * 128], in_=gmu[:, hl * 128:(hl + 1) * 128], func=AF.Exp, bias=sm["NCS"][:, h0 + hl:h0 + hl + 1]),
                         reads=[gmur, smr + 'NCS'], writes=[EUr])
                FPP.release((gmu, gmur))
                bK, bKr = yield from BANKS.acquire()
                for hl in range(4):
                    kh = QKV[:, 8 + h0 + hl, tk]
                    P.op('pe', lambda e, kh=kh, hl=hl: e.matmul(bK[:, hl * 128:(hl + 1) * 128], lhsT=kh, rhs=kh, start=True, stop=True),
                         reads=kres, writes=[bKr], signal=(hl == 3))
                bKQ, bKQr = yield from BANKS.acquire()
                for hl in range(4):
                    kh = QKV[:, 8 + h0 + hl, tk]
                    qh = QKV[:, h0 + hl, tk]
                    P.op('pe', lambda e, kh=kh, qh=qh, hl=hl: e.matmul(bKQ[:, hl * 128:(hl + 1) * 128], lhsT=kh, rhs=qh, start=True, stop=True),
                         reads=kres + qres, writes=[bKQr], signal=(hl == 3))
                yield
                P.op('dve', lambda e: e.scalar_tensor_tensor(out=f2(PTc), in0=bK, scalar=-1.0, in1=EL, op0=ALU.mult, op1=ALU.mult), reads=[ELr], writes=[bKr, PTcr])
                BPP.release((EL, ELr))
                yield
                P.op('dve', lambda e: e.scalar_tensor_tensor(out=f2(Pc), in0=bK, scalar=-1.0, in1=EU, op0=ALU.mult, op1=ALU.mult), reads=[EUr], writes=[bKr, Pcr])
                BPP.release((EU, EUr))
                BANKS.release((bK, bKr))
                yield
                P.op('dve', lambda e: e.tensor_tensor(out=f2(QKT), in0=bKQ, in1=EQ, op=ALU.mult), reads=[EQr], writes=[bKQr, QKTr])
                BPP.release((EQ, EQr))
                BANKS.release((bKQ, bKQr))
                P.op(mk_eng, lambda e: e.tensor_tensor(out=NOT, in0=PTc, in1=bc_mid(mOFF, 4), op=ALU.mult), reads=[PTcr, 'CST'], writes=[NOTr])
                P.op(mk_eng, lambda e: e.tensor_tensor(out=PTc, in0=PTc, in1=bc_mid(mBD, 4), op=ALU.mult), reads=[PTcr, 'CST'], writes=[PTcr])
                yield
                P.op(mk_eng, lambda e: e.tensor_tensor(out=Pc, in0=Pc, in1=bc_mid(mBD, 4), op=ALU.mult), reads=[Pcr, 'CST'], writes=[Pcr])
                P.op(mk_eng, lambda e: e.tensor_tensor(out=Tc, in0=Pc, in1=bc_mid(identB, 4), op=ALU.add), reads=[Pcr, 'identB'], writes=[Tcr])
                yield
                NLEV = 5
                for m in range(NLEV):
                    last = (m == NLEV - 1)
                    bPT, bPTr = yield from BANKS.acquire()
                    for hl in range(4):
                        P.op('pe', lambda e, hl=hl: e.matmul(bPT[:, hl * 128:(hl + 1) * 128], lhsT=Pc[:, hl, :], rhs=PTc[:, hl, :], start=True, stop=True),
                             reads=[PTcr, Pcr], writes=[bPTr], signal=(hl == 3))
                    if not last:
                        bP, bPr = yield from BANKS.acquire()
                        for hl in range(4):
                            P.op('pe', lambda e, hl=hl: e.matmul(bP[:, hl * 128:(hl + 1) * 128], lhsT=PTc[:, hl, :], rhs=Pc[:, hl, :], start=True, stop=True),
                                 reads=[PTcr, Pcr], writes=[bPr], signal=(hl == 3))
                    yield
                    P.op('act', lambda e: e.activation(out=f2(PTn), in_=bPT, func=AF.Copy), reads=[], writes=[bPTr, PTnr])
                    BANKS.release((bPT, bPTr))
                    if not last:
                        P.op('dve', lambda e: e.tensor_copy(out=f2(Pn), in_=bP), reads=[], writes=[bPr, Pnr])
                        BANKS.release((bP, bPr))
                    yield
                    bT, bTr = yield from BANKS.acquire()
                    for hl in range(4):
                        P.op('pe', lambda e, hl=hl: e.matmul(bT[:, hl * 128:(hl + 1) * 128], lhsT=identB, rhs=Tc[:, hl, :], start=True, stop=False),
                             reads=[Tcr, 'identB'], writes=[bTr], signal=False)
                        P.op('pe', lambda e, hl=hl: e.matmul(bT[:, hl * 128:(hl + 1) * 128], lhsT=PTn[:, hl, :], rhs=Tc[:, hl, :], start=False, stop=True),
                             reads=[Tcr, PTnr], writes=[bTr], signal=(hl == 3))
                    yield
                    P.op('dve' if (m & 1) else 'act',
                         (lambda e: e.tensor_copy(out=f2(Tn), in_=bT)) if (m & 1) else (lambda e: e.activation(out=f2(Tn), in_=bT, func=AF.Copy)),
                         reads=[], writes=[bTr, Tnr])
                    BANKS.release((bT, bTr))
                    Pc, Pcr, Pn, Pnr = Pn, Pnr, Pc, Pcr
                    PTc, PTcr, PTn, PTnr = PTn, PTnr, PTc, PTcr
                    Tc, Tcr, Tn, Tnr = Tn, Tnr, Tc, Tcr
                    yield
                bTT, bTTr = yield from BANKS.acquire()
                bTTb = bTT.bitcast(BF16)
                for hl in range(4):
                    P.op('pe', lambda e, hl=hl: e.transpose(out=bTTb[:, hl * 128:(hl + 1) * 128], in_=Tc[:, hl, :], identity=identB),
                         reads=[Tcr, 'identB'], writes=[bTTr], signal=(hl == 3))
                bA, bAr = yield from BANKS.acquire()
                for hl in range(4):
                    P.op('pe', lambda e, hl=hl: e.matmul(bA[:, hl * 128:(hl + 1) * 128], lhsT=NOT[:, hl, :], rhs=Tc[:, hl, :], start=True, stop=True),
                         reads=[NOTr, Tcr], writes=[bAr], signal=(hl == 3))
                yield
                DG, DGr = Pn, Pnr
                P.op('act', lambda e: e.activation(out=f2(DG), in_=bTTb[:, 0:512], func=AF.Copy), reads=[], writes=[bTTr, DGr])
                BANKS.release((bTT, bTTr))
                A1, A1r = PTn, PTnr
                P.op('act', lambda e: e.activation(out=f2(A1), in_=bA, func=AF.Copy), reads=[], writes=[bAr, A1r])
                BANKS.release((bA, bAr))
                yield
                bF, bFr = yield from BANKS.acquire()
                for hl in range(4):
                    P.op('pe', lambda e, hl=hl: e.matmul(bF[:, hl * 128:(hl + 1) * 128], lhsT=identB, rhs=Tc[:, hl, :], start=True, stop=False),
                         reads=[Tcr, 'identB'], writes=[bFr], signal=False)
                    P.op('pe', lambda e, hl=hl: e.matmul(bF[:, hl * 128:(hl + 1) * 128], lhsT=DG[:, hl, :], rhs=A1[:, hl, :], start=False, stop=True),
                         reads=[DGr, A1r], writes=[bFr], signal=(hl == 3))
                bTr_, bTr_r = yield from BANKS.acquire()
                bTb = bTr_.bitcast(BF16)
                for hl in range(4):
                    P.op('pe', lambda e, hl=hl: e.transpose(out=bTb[:, hl * 128:(hl + 1) * 128], in_=QKV[:, 8 + h0 + hl, tk], identity=identB),
                         reads=kres + ['identB'], writes=[bTr_r], signal=False)
                for hl in range(4):
                    P.op('pe', lambda e, hl=hl: e.transpose(out=bTb[:, 512 + hl * 128:512 + (hl + 1) * 128], in_=QKV[:, 16 + h0 + hl, tk], identity=identB),
                         reads=vres + ['identB'], writes=[bTr_r], signal=(hl == 3))
                yield
                P.op('act', lambda e: e.activation(out=f2(Tn), in_=bF, func=AF.Copy), reads=[], writes=[bFr, Tnr])
                BANKS.release((bF, bFr))
                Tc, Tcr, Tn, Tnr = Tn, Tnr, Tc, Tcr
                for it_ in ((Pc, Pcr), (PTc, PTcr), (Pn, Pnr), (PTn, PTnr), (NOT, NOTr), (Tn, Tnr)):
                    GP.release(it_)
                ktm = bTb[:, 0:512].rearrange("p (a b) -> p a b", a=4)
                vtm = bTb[:, 512:1024].rearrange("p (a b) -> p a b", a=4)
                (XK, XKr), (KDEC, KDECr), (BV, BVr), (WTN, WTNr), (VN, VNr) = yield from acquire_n(GP, 5)
                P.op('dve', lambda e: e.tensor_tensor(out=BV, in0=vtm, in1=bc_last(sm["BETA"][:, h0:h0 + 4], 128), op=ALU.mult), reads=[smr + 'BETA'], writes=[bTr_r, BVr])
                yield
                P.op('dve', lambda e: e.tensor_tensor(out=XK, in0=ktm, in1=bc_last(sm["BEG"][:, h0:h0 + 4], 128), op=ALU.mult), reads=[smr + 'BEG'], writes=[bTr_r, XKr])
                yield
                P.op('dve', lambda e: e.tensor_tensor(out=KDEC, in0=ktm, in1=bc_last(sm["EREM"][:, h0:h0 + 4], 128), op=ALU.mult), reads=[smr + 'EREM'], writes=[bTr_r, KDECr])
                BANKS.release((bTr_, bTr_r))
                bU, bUr = yield from BANKS.acquire()
                for hl in range(4):
                    P.op('pe', lambda e, hl=hl: e.matmul(bU[:, hl * 128:(hl + 1) * 128], lhsT=Tc[:, hl, :], rhs=BV[:, hl, :], start=True, stop=True),
                         reads=[Tcr, BVr], writes=[bUr], signal=(hl == 3))
                bW, bWr = yield from BANKS.acquire()
                for hl in range(4):
                    P.op('pe', lambda e, hl=hl: e.matmul(bW[:, hl * 128:(hl + 1) * 128], lhsT=XK[:, hl, :], rhs=Tc[:, hl, :], start=True, stop=True),
                         reads=[Tcr, XKr], writes=[bWr], signal=(hl == 3))
                yield
                UFf, UFr = yield from FPP.acquire()
                UF = v4(UFf)
                P.op('act', lambda e: e.activation(out=f2(UF), in_=bU, func=AF.Copy), reads=[], writes=[bUr, UFr])
                BANKS.release((bU, bUr))
                P.op('act', lambda e: e.activation(out=f2(WTN), in_=bW, func=AF.Copy, scale=-1.0), reads=[], writes=[bWr, WTNr])
                BANKS.release((bW, bWr))
                yield
                while i > 0 and not rec_done[hg][i - 1]:
                    yield
                sres = 'Sb%d' % hg
                bWS, bWSr = yield from BANKS.acquire()
                for hl in range(4):
                    P.op('pe', lambda e, hl=hl: e.matmul(bWS[:, hl * 128:(hl + 1) * 128], lhsT=WTN[:, hl, :], rhs=Sb[:, h0 + hl, :], start=True, stop=True),
                         reads=[WTNr, sres], writes=[bWSr], signal=(hl == 3))
                yield
                P.op('dve', lambda e: e.tensor_tensor(out=f2(VN), in0=bWS, in1=f2(UF), op=ALU.add), reads=[UFr], writes=[bWSr, VNr])
                BANKS.release((bWS, bWSr))
                yield
                bO, bOr = yield from BANKS.acquire()
                for hl in range(4):
                    P.op('pe', lambda e, hl=hl: e.matmul(bO[:, hl * 128:(hl + 1) * 128], lhsT=Sb[:, h0 + hl, :], rhs=QD[:, hl, :], start=True, stop=False),
                         reads=[sres, QDr], writes=[bOr], signal=False)
                    P.op('pe', lambda e, hl=hl: e.matmul(bO[:, hl * 128:(hl + 1) * 128], lhsT=VN[:, hl, :], rhs=QKT[:, hl, :], start=False, stop=True),
                         reads=[VNr, QKTr], writes=[bOr], signal=(hl == 3))
                bDS, bDSr = yield from BANKS.acquire()
                for hl in range(4):
                    P.op('pe', lambda e, hl=hl: e.matmul(bDS[:, hl * 128:(hl + 1) * 128], lhsT=KDEC[:, hl, :], rhs=VN[:, hl, :], start=True, stop=True),
                         reads=[KDECr, VNr], writes=[bDSr], signal=(hl == 3))
                S4 = S[:, h0:h0 + 4, :]
                srf = 'S%d' % hg
                yield
                for hl in range(4):
                    P.op('dve', lambda e, hl=hl: e.scalar_tensor_tensor(out=S[:, h0 + hl, :], in0=S[:, h0 + hl, :], scalar=sm["ETOT"][:, h0 + hl:h0 + hl + 1], in1=bDS[:, hl * 128:(hl + 1) * 128],
                                                                      op0=ALU.mult, op1=ALU.add),
                         reads=[srf, smr + 'ETOT'], writes=[bDSr, srf])
                BANKS.release((bDS, bDSr))
                sqo, sqor = yield from BPP.acquire()
                P.op('act', lambda e: e.activation(out=sqo, in_=bO, func=AF.Square), reads=[], writes=[bOr, sqor])
                yield
                P.op('act', lambda e: e.activation(out=Sb[:, h0:h0 + 4, :], in_=S4, func=AF.Copy), reads=[srf], writes=[sres])
                rec_done[hg][i] = True
                bN, bNr = yield from BANKS.acquire()
                P.op('pe', lambda e: e.matmul(bN, lhsT=onesB, rhs=sqo, start=True, stop=True), reads=[sqor, 'onesB'], writes=[bNr])
                BPP.release((sqo, sqor))
                yield
                rno, rnor = yield from FPP.acquire()
                P.op('act', lambda e: e.activation(out=rno, in_=bN, func=AF.Ln, scale=1.0 / 128, bias=EPS), reads=[], writes=[bNr, rnor])
                BANKS.release((bN, bNr))
                P.op('act', lambda e: e.activation(out=rno, in_=rno, func=AF.Exp, scale=-0.5), reads=[rnor], writes=[rnor])
                yield
                P.op('dve', lambda e: e.scalar_tensor_tensor(out=rno, in0=bO, scalar=par("gnw")[:, 0:1], in1=rno, op0=ALU.mult, op1=ALU.mult),
                     reads=[rnor, 'PAR'], writes=[bOr, rnor])
                BANKS.release((bO, bOr))
                yield
                P.op('dve', lambda e: e.tensor_tensor(out=MIX[:, h0:h0 + 4, tk], in0=v4(rno), in1=ZA[:, h0:h0 + 4, tk], op=ALU.mult),
                     reads=[rnor] + ['ZA%d' % (h0 + k) for k in range(4)], writes=['MIX%d' % (h0 + k) for k in range(4)])
                FPP.release((rno, rnor))
                FPP.release((UFf, UFr))
                for it_ in ((Tc, Tcr), (QD, QDr), (QKT, QKTr), (XK, XKr), (KDEC, KDECr), (BV, BVr), (WTN, WTNr), (VN, VNr)):
                    GP.release(it_)
                yield

            def ssd_chain(i):
                tk = slice(i * 128, (i + 1) * 128)
                sm = SMB[i % 2]
                smr = 'SMB%d' % (i % 2)
                bBC, bBCr = yield from BANKS.acquire()
                for gq in range(2):
                    P.op('pe', lambda e, gq=gq: e.matmul(bBC[:, gq * 128:(gq + 1) * 128], lhsT=XBC[:, 8 + gq, tk], rhs=XBC[:, 10 + gq, tk], start=True, stop=True),
                         reads=['XBC%d' % (8 + gq), 'XBC%d' % (10 + gq)], writes=[bBCr], signal=(gq == 1))
                bBT, bBTr = yield from BANKS.acquire()
                bBTb = bBT.bitcast(BF16)
                for gq in range(2):
                    P.op('pe', lambda e, gq=gq: e.transpose(out=bBTb[:, gq * 128:(gq + 1) * 128], in_=XBC[:, 8 + gq, tk], identity=identB),
                         reads=['XBC%d' % (8 + gq), 'identB'], writes=[bBTr], signal=(gq == 1))
                yield
                P.op('act', lambda e: e.activation(out=BCS, in_=bBC[:, 0:256], func=AF.Copy), reads=[], writes=[bBCr, 'BCS'])
                BANKS.release((bBC, bBCr))
                P.op('dve', lambda e: e.tensor_copy(out=BTM.rearrange("p a b -> p (a b)"), in_=bBTb[:, 0:256]), reads=[], writes=[bBTr, 'BTM'])
                BANKS.release((bBT, bBTr))
                yield
                for hq in range(4):
                    h0 = 4 * hq
                    gq = hq // 2
                    pi = hq % 2
                    acs4 = bc_last(sm["CSUM"][:, 8 + h0:8 + h0 + 4], 128)
                    bR3, bR3r = yield from BANKS.acquire()
                    for hl in range(4):
                        P.op('pe', lambda e, hl=hl: e.matmul(bR3[:, hl * 128:(hl + 1) * 128], lhsT=sm["CSUM"][:, 8 + h0 + hl:8 + h0 + hl + 1].to_broadcast([128, 128]), rhs=identF, start=True, stop=True),
                             reads=[smr + 'CSUM', 'CST'], writes=[bR3r], signal=(hl == 3))
                    bXT, bXTr = yield from BANKS.acquire()
                    bXTb = bXT.bitcast(BF16)
                    for k in range(2):
                        P.op('pe', lambda e, k=k: e.transpose(out=bXTb[:, k * 128:(k + 1) * 128], in_=XBC[:, 2 * hq + k, tk], identity=identB),
                             reads=['XBC%d' % (2 * hq + k), 'identB'], writes=[bXTr], signal=(k == 1))
                    yield
                    gms, gmsr = yield from FPP.acquire()
                    P.op('dve', lambda e: e.tensor_tensor(out=v4(gms), in0=v4(bR3), in1=bc_mid(mUi, 4), op=ALU.add), reads=['CST'], writes=[bR3r, gmsr])
                    yield
                    ES, ESr = yield from BPP.acquire()
                    for hl in range(4):
                        P.op('act', lambda e, hl=hl: e.activation(out=ES[:, hl * 128:(hl + 1) * 128], in_=gms[:, hl * 128:(hl + 1) * 128], func=AF.Exp, bias=sm["NCS"][:, 8 + h0 + hl:8 + h0 + hl + 1]),
                             reads=[gmsr, smr + 'NCS'], writes=[ESr])
                    FPP.release((gms, gmsr))
                    er3, er3r = yield from BPP.acquire()
                    P.op('act', lambda e: e.activation(out=er3, in_=bR3, func=AF.Exp), reads=[], writes=[bR3r, er3r])
                    BANKS.release((bR3, bR3r))
                    xtm = bXTb[:, 0:256].rearrange("p (a b) -> p a b", a=4)
                    XDT, XDTr = XDTb[pi], 'XDT%d' % pi
                    XDD, XDDr = XDDb[pi], 'XDD%d' % pi
                    P.op('dve', lambda e: e.tensor_tensor(out=XDT, in0=xtm, in1=bc_last(sm["SPL"][:, 8 + h0:8 + h0 + 4], 64), op=ALU.mult), reads=[smr + 'SPL'], writes=[bXTr, XDTr])
                    yield
                    P.op('dve', lambda e: e.tensor_tensor(out=XDD, in0=xtm, in1=bc_last(sm["DTE"][:, h0:h0 + 4], 64), op=ALU.mult), reads=[smr + 'DTE'], writes=[bXTr, XDDr])
                    BANKS.release((bXT, bXTr))
                    GT, GTr = GTb[pi], 'GT%d' % pi
                    P.op('dve', lambda e: e.tensor_tensor(out=GT, in0=v4(ES), in1=bc_mid(BCS[:, gq * 128:(gq + 1) * 128], 4), op=ALU.mult),
                         reads=[ESr, 'BCS'], writes=[GTr])
                    BPP.release((ES, ESr))
                    yield
                    CD, CDr = CDECb[pi], 'CDEC%d' % pi
                    P.op('dve', lambda e: e.tensor_tensor(out=CD, in0=v4(er3), in1=bc_mid(XBC[:, 10 + gq, tk], 4), op=ALU.mult),
                         reads=[er3r, 'XBC%d' % (10 + gq)], writes=[CDr])
                    BPP.release((er3, er3r))
                    yield
                    ssr = 'SSb%d' % hq
                    bY, bYr = yield from BANKS.acquire()
                    for hl in range(4):
                        pr = hl // 2
                        hf = hl % 2
                        o_ = bY[hf * 64:(hf + 1) * 64, pr * 128:(pr + 1) * 128]
                        P.op('pe', lambda e, hl=hl, o_=o_, hf=hf: e.matmul(o_, lhsT=XDT[:, hl, :], rhs=GT[:, hl, :], start=True, stop=False, tile_position=(0, 64 * hf)),
                             reads=[XDTr, GTr], writes=[bYr], signal=False)
                        P.op('pe', lambda e, hl=hl, o_=o_, hf=hf: e.matmul(o_, lhsT=SSb[:, h0 + hl, :], rhs=CD[:, hl, :], start=False, stop=True, tile_position=(0, 64 * hf)),
                             reads=[ssr, CDr], writes=[bYr], signal=(hl == 3))
                    bDSS, bDSSr = yield from BANKS.acquire()
                    P.op('pe', lambda e: e.matmul(bDSS[:, 0:256], lhsT=BTM[:, gq, :], rhs=XDD.rearrange("p a b -> p (a b)"), start=True, stop=True),
                         reads=['BTM', XDDr], writes=[bDSSr])
                    SS4 = SS[:, h0:h0 + 4, :]
                    ssf = 'SS%d' % hq
                    yield
                    for hl in range(4):
                        P.op('dve', lambda e, hl=hl: e.scalar_tensor_tensor(out=SS[:, h0 + hl, :], in0=SS[:, h0 + hl, :], scalar=sm["ETOT"][:, 8 + h0 + hl:8 + h0 + hl + 1], in1=bDSS[:, hl * 64:(hl + 1) * 64],
                                                                          op0=ALU.mult, op1=ALU.add),
                             reads=[ssf, smr + 'ETOT'], writes=[bDSSr, ssf])
                    BANKS.release((bDSS, bDSSr))
                    yield
                    P.op('act', lambda e: e.activation(out=SSb[:, h0:h0 + 4, :], in_=SS4, func=AF.Copy), reads=[ssf], writes=[ssr])
                    ys, ysr = yield from FPP.acquire()
                    for pr in range(2):
                        xt = 2 * hq + pr
                        P.op('dve', lambda e, pr=pr, xt=xt: e.scalar_tensor_tensor(out=ys[:, pr * 128:(pr + 1) * 128], in0=XBC[:, xt, tk], scalar=par("dexp")[:, xt:xt + 1],
                                                                               in1=bY[:, pr * 128:(pr + 1) * 128], op0=ALU.mult, op1=ALU.add),
                             reads=['XBC%d' % xt, 'PAR'], writes=[bYr, ysr])
                    BANKS.release((bY, bYr))
                    yield
                    P.op('dve', lambda e: e.tensor_tensor(out=YG[:, 2 * pi:2 * pi + 2, :], in0=ys[:, 0:256].rearrange("p (a b) -> p a b", a=2), in1=ZS[:, 2 * hq:2 * hq + 2, tk], op=ALU.mult),
                         reads=[ysr, 'ZS%d' % (2 * hq), 'ZS%d' % (2 * hq + 1)], writes=['YG%d' % pi])
                    FPP.release((ys, ysr))
                    yield
                    if pi == 1:
                        ygr = ['YG0', 'YG1']
                        sqy, sqyr = yield from BPP.acquire()
                        P.op('act', lambda e: e.activation(out=sqy, in_=YG.rearrange("p a b -> p (a b)"), func=AF.Square), reads=ygr, writes=[sqyr])
                        yield
                        bNS, bNSr = yield from BANKS.acquire()
                        for k in range(4):
                            P.op('pe', lambda e, k=k: e.matmul(bNS[:, 0:128], lhsT=onesB, rhs=sqy[:, k * 128:(k + 1) * 128], start=(k == 0), stop=(k == 3)),
                                 reads=[sqyr, 'onesB'], writes=[bNSr], signal=(k == 3))
                        BPP.release((sqy, sqyr))
                        yield
                        rns, rnsr = yield from FPP.acquire()
                        P.op('act', lambda e: e.activation(out=rns[:, 0:128], in_=bNS[:, 0:128], func=AF.Ln, scale=1.0 / 512, bias=EPS), reads=[], writes=[bNSr, rnsr])
                        BANKS.release((bNS, bNSr))
                        P.op('act', lambda e: e.activation(out=rns[:, 0:128], in_=rns[:, 0:128], func=AF.Exp, scale=-0.5), reads=[rnsr], writes=[rnsr])
                        yield
                        for k in range(4):
                            xt = 4 * gq + k
                            P.op('dve', lambda e, xt=xt, k=k: e.scalar_tensor_tensor(out=MIX[:, 8 + xt, tk], in0=YG[:, k, :], scalar=par("snw")[:, xt:xt + 1], in1=rns[:, 0:128],
                                                                                op0=ALU.mult, op1=ALU.mult),
                                 reads=ygr + [rnsr, 'PAR'], writes=['MIX%d' % (8 + xt)])
                        FPP.release((rns, rnsr))
                        yield

            run_chains([gates_all(), gdn_worker(), gdn_worker(), gdn_worker(), ssd_all()])
            dump("MIX", MIX, mixres)
            BANKS.flush()
            FPP.flush()
            BPP.flush()
            P.retire(regA_mix, regA_ffn)
            mores = ['MO%d' % j for j in range(8)]

            def proj_norm_residual(kind, npiece, nkc, src3, srcres, wname, sqbuf, sqname):
                bst, bstr = take_bank()
                for j in range(8):
                    bk, bkr = take_bank()
                    for q in range(npiece):
                        wv, wr = w_get(kind, j, q)
                        for k8 in range(min(8, nkc - 8 * q)):
                            kc = q * 8 + k8
                            P.op('pe', lambda e, kc=kc, k8=k8: e.matmul(bk[:, 0:TB], lhsT=wv[:, k8, :], rhs=src3[:, kc, :], start=(kc == 0), stop=(kc == nkc - 1)),
                                 reads=[wr] + srcres, writes=[bkr], signal=(kc == nkc - 1))
                    if j & 1:
                        P.op('act', lambda e: e.activation(out=MO[:, j, :], in_=bk[:, 0:TB], func=AF.Identity, scale=par(wname)[:, j:j + 1]),
                             reads=['PAR'], writes=[bkr, 'MO%d' % j])
                    else:
                        P.op('dve', lambda e: e.tensor_scalar_mul(out=MO[:, j, :], in0=bk[:, 0:TB], scalar1=par(wname)[:, j:j + 1]),
                             reads=['PAR'], writes=[bkr, 'MO%d' % j])
                    rms_sq_step(bst, bstr, j, bk[:, 0:TB], [], src_is_psum=bkr, sqbuf=sqbuf, sqname=sqname)
                    BANKS.release((bk, bkr))
                rms_finish(bst, bstr, RSTD, 'RSTD', D)

            proj_norm_residual('out', 2, 16, MIX, mixres, "w2", ACTB, 'ACTB')
            bst, bstr = take_bank()
            for kc in range(8):
                P.op('dve', lambda e, kc=kc: e.tensor_tensor(out=MO[:, kc, :], in0=MO[:, kc, :], in1=RSTD, op=ALU.mult),
                     reads=['MO%d' % kc, 'RSTD'], writes=['MO%d' % kc])
                P.op('dve', lambda e, kc=kc: e.tensor_tensor(out=X[:, kc, :], in0=X[:, kc, :], in1=MO[:, kc, :], op=ALU.add), reads=[xres[kc], 'MO%d' % kc], writes=[xres[kc]])
                rms_sq_step(bst, bstr, kc, X[:, kc, :], [xres[kc]])
            dump("X1", X, xres)
            rstd2, rstd2r = ftmp()
            rms_finish(bst, bstr, rstd2, rstd2r, D)
            for kc in range(8):
                P.op('dve', lambda e, kc=kc: e.scalar_tensor_tensor(out=H[:, kc, :], in0=X[:, kc, :], scalar=par("w3")[:, kc:kc + 1], in1=rstd2[:, 0:TB],
                                                                    op0=ALU.mult, op1=ALU.mult),
                     reads=[xres[kc], rstd2r, 'PAR'], writes=['H%d' % kc])
            BANKS.flush()
            FPP.flush()
            BPP.flush()
            active = []
            for jj in range(22):
                for half in range(2):
                    j = jj + 22 * half
                    wv, wr = w_get('up', j)
                    bk, bkr = take_bank(active)
                    for kc in range(8):
                        P.op('pe', lambda e, kc=kc: e.matmul(bk[:, 0:TB], lhsT=wv[:, kc, :], rhs=H[:, kc, :], start=(kc == 0), stop=(kc == 7)),
                             reads=[wr] + hres, writes=[bkr], signal=(kc == 7))
                    cwj = par("cwf")[:, 3 * j:3 * j + 3]
                    if half == 0:
                        fin = lambda acc, accr, jj=jj: P.op('act', lambda e: e.activation(out=ACTB[:, jj, :], in_=acc[:, 0:TB], func=AF.Silu), reads=[accr], writes=['ACTB%d' % jj])
                        npl = 0
                    else:
                        fin = lambda acc, accr, jj=jj: P.op('dve', lambda e: e.tensor_tensor(out=ACTB[:, jj, :], in0=ACTB[:, jj, :], in1=acc[:, 0:TB], op=ALU.mult),
                                                            reads=[accr, 'ACTB%d' % jj], writes=['ACTB%d' % jj])
                        npl = 0 if blk0 else 1
                    active.append(conv_chain(bk, bkr, cwj, 3, par("cbf")[:, j:j + 1], TAILF[:, j, :], 'TF%d' % j, first_blk, TB, fin, npl, tiny))
                    step_active(active)
            run_chains(active)
            actres = ['ACTB%d' % j for j in range(22)]
            BANKS.flush()
            nb_ = b_ + 1
            ns_ = s_
            if nb_ == NBLK:
                nb_, ns_ = 0, s_ + 1
            if ns_ < NSEQ:
                load_x(ns_, nb_)
            proj_norm_residual('dn', 3, KC_DN, ACTB, actres, "w4", None, 'MIX')
            for kc in range(8):
                P.op('dve', lambda e, kc=kc: e.tensor_tensor(out=MO[:, kc, :], in0=MO[:, kc, :], in1=RSTD, op=ALU.mult),
                     reads=['MO%d' % kc, 'RSTD'], writes=['MO%d' % kc])
                P.op('dve', lambda e, kc=kc: e.tensor_tensor(out=MO[:, kc, :], in0=MO[:, kc, :], in1=X[:, kc, :], op=ALU.add), reads=[xres[kc], 'MO%d' % kc], writes=['MO%d' % kc])
            P.dma('sp', outT[s_, 0:4, :, t0b:t0b + TB].rearrange("k p t -> p k t"), MO[:, 0:4, :], reads=mores[0:4], semname='st_outa')
            P.dma('sp', outT[s_, 4:8, :, t0b:t0b + TB].rearrange("k p t -> p k t"), MO[:, 4:8, :], reads=mores[4:8], semname='st_outb')
    P.finish('sp')
    return nc, P, dbg_outs


def _tile_w(w, ncol_tiles):
    K, N = w.shape
    kc = K // 128
    t = w.reshape(kc, 128, ncol_tiles, 128).transpose(2, 1, 0, 3)
    return np.ascontiguousarray(t).reshape(ncol_tiles, 128, kc * 128)


def make_consts():
    i = np.arange(128)
    c = np.zeros((128, 8, 128), np.float32)
    c[:, 0, :] = np.eye(128)
    c[:, 1, :] = (i[:, None] <= i[None, :])
    c[:, 2, :] = 1.0
    c[:, 3, :] = np.where(i[None, :] < i[:, None], 0.0, -BIG)
    c[:, 4, :] = np.where(i[None, :] > i[:, None], 0.0, -BIG)
    c[:, 5, :] = np.where(i[None, :] >= i[:, None], 0.0, -BIG)
    c[:, 6, :] = ((i[:, None] // 64) == (i[None, :] // 64))
    c[:, 7, :] = ((i[:, None] >= 64) & (i[None, :] < 64))
    return c


def prep_shared(inp):
    g = lambda n: np.asarray(inp[n], dtype=np.float32)[0]
    w_in = g("w_in")
    offs = np.cumsum([0, 1024, 1024, 1024, 1024, 8, 8, 1024, 1024, 256, 256, 16])
    sl = lambda k: w_in[:, offs[k]:offs[k + 1]]
    main = np.concatenate([sl(0), sl(1), sl(2), sl(3), sl(6), sl(7), sl(8), sl(9)], axis=1)
    small = np.concatenate([sl(4), sl(5), sl(10)], axis=1)
    w_in_t = _tile_w(main, NJ_IN)
    w_small = np.ascontiguousarray(small.reshape(8, 128, 32).transpose(1, 0, 2)).reshape(128, 256)
    w_out_t = _tile_w(g("w_out"), 8)
    w_up_t = _tile_w(g("w_up"), NJ_UP)
    w_dn_t = _tile_w(g("w_down"), 8)
    par = np.zeros((128, NPAR), np.float32)

    def put(name, arr):
        a, b = PO[name]
        par[:, a:b] = arr.reshape(128, b - a)

    pp = lambda v: np.ascontiguousarray(v.reshape(-1, 128).T)
    put("w1", pp(g("pre_mix_norm")))
    put("w2", pp(g("post_mix_norm")))
    put("w3", pp(g("pre_ffn_norm")))
    put("w4", pp(g("post_ffn_norm")))
    put("gnw", g("gdn_norm_w").reshape(128, 1))
    put("snw", pp(g("ssd_norm_w")))
    put("dexp", pp(np.repeat(g("ssd_d"), 64)))
    put("cwg", np.ascontiguousarray(g("gdn_conv_w").reshape(4, 24, 128).transpose(2, 1, 0)))
    put("cws", np.ascontiguousarray(g("ssd_conv_w").reshape(4, 12, 128).transpose(2, 1, 0)))
    put("cbs", pp(g("ssd_conv_b")))
    put("cwf", np.ascontiguousarray(g("ffn_conv_w").reshape(3, 44, 128).transpose(2, 1, 0)))
    put("cbf", pp(g("ffn_conv_b")))
    put("alog", np.broadcast_to(np.concatenate([g("gdn_a_log"), g("ssd_a_log")])[None, :], (128, 24)))
    put("dtb", np.broadcast_to(np.concatenate([g("gdn_dt_bias"), g("ssd_dt_bias")])[None, :], (128, 24)))
    return {"params": par, "consts": make_consts(), "w_in_t": w_in_t, "w_small": w_small,
            "w_out_t": w_out_t, "w_up_t": w_up_t, "w_dn_t": w_dn_t}


def x_to_dev(xs):
    n, s, _ = xs.shape
    return np.ascontiguousarray(xs.transpose(0, 2, 1)).reshape(n, 8, 128, s)


def out_from_dev(o):
    n, _, _, s = o.shape
    return np.ascontiguousarray(o.reshape(n, 1024, s).transpose(0, 2, 1))


_CACHE = {}


def kernel(**inputs):
    x = np.asarray(inputs["x"], dtype=np.float32)
    B, SEQ, _ = x.shape
    nseq = B // NCORES
    key = (nseq, SEQ)
    if key not in _CACHE:
        _CACHE[key] = build(nseq, SEQ, 512)[0]
    nc = _CACHE[key]
    shared = prep_shared(inputs)
    in_maps = []
    for c in range(NCORES):
        m = dict(shared)
        m["xT"] = x_to_dev(x[c * nseq:(c + 1) * nseq])
        in_maps.append(m)
    res = run_bass_kernel_spmd(nc, in_maps, core_ids=list(range(NCORES)))
    outs = [out_from_dev(np.asarray(r["outT"])) for r in res.results]
    return np.concatenate(outs, axis=0).astype(np.float32)
```

```python
import numpy as np
import concourse.bass as bass
import concourse.mybir as mybir
from concourse.bass_utils import run_bass_kernel_spmd

F32 = mybir.dt.float32
BF16 = mybir.dt.bfloat16
AF = mybir.ActivationFunctionType
ALU = mybir.AluOpType

D = 1024
NH = 8
SH = 16
DFF = 2816
NJ_IN = 52
NJ_UP = 44
KC_DN = 22
EPS = 1e-6
BIG = 30000.0
NCORES = 8

PO = {}
_o = 0
for _n, _w in [("w1", 8), ("w2", 8), ("w3", 8), ("w4", 8), ("gnw", 1), ("snw", 8), ("dexp", 8),
               ("cwg", 96), ("cws", 48), ("cbs", 12), ("cwf", 132), ("cbf", 44), ("alog", 24), ("dtb", 24)]:
    PO[_n] = (_o, _o + _w)
    _o += _w
NPAR = _o


class Prog:
    def __init__(self, nc):
        self.nc = nc
        self.eng = {'pe': nc.tensor, 'act': nc.scalar, 'dve': nc.vector, 'pool': nc.gpsimd, 'sp': nc.sync}
        self.sem = {e: nc.alloc_semaphore('sem_' + e) for e in self.eng}
        self.cnt = {e: 0 for e in self.eng}
        self.waited = {e: {} for e in self.eng}
        self.lastw = {}
        self.readers = {}
        self.dsem = {}
        self.ninst = 0
        self.nwait = 0
        self.rr = 0

    def _wait(self, e, dep):
        key, h, v, src = dep
        if self.waited[e].get(key, 0) >= v:
            return
        if src == e and v > self.cnt[e]:
            return
        self.eng[e].wait_ge(h, v)
        self.nwait += 1
        self.waited[e][key] = v

    def _deps(self, e, reads, writes):
        deps = []
        for r in reads:
            d = self.lastw.get(r)
            if d is not None:
                deps.append(d)
        for w in writes:
            d = self.lastw.get(w)
            if d is not None:
                deps.append(d)
            rd = self.readers.get(w)
            if rd:
                for k, d in rd.items():
                    deps.append(d)
        for d in deps:
            self._wait(e, d)

    def op(self, e, fn, reads=(), writes=(), signal=True):
        self._deps(e, reads, writes)
        ins = fn(self.eng[e])
        self.ninst += 1
        if signal:
            ins.then_inc(self.sem[e], 1)
            self.cnt[e] += 1
            v = self.cnt[e]
        else:
            v = self.cnt[e] + 1
        dep = ('e_' + e, self.sem[e], v, e)
        for w in writes:
            self.lastw[w] = dep
            self.readers[w] = {}
        for r in reads:
            if r not in writes:
                self.readers.setdefault(r, {})[e] = dep
        return ins

    def dma(self, q, out, in_, reads=(), writes=(), semname=None, **kw):
        self._deps(q, reads, writes)
        if semname is None:
            semname = 'd_' + (writes[0] if writes else reads[0])
        if semname not in self.dsem:
            self.dsem[semname] = [self.nc.alloc_semaphore(semname), 0]
        s = self.dsem[semname]
        ins = self.eng[q].dma_start(out=out, in_=in_, **kw)
        ins.then_inc(s[0], 16)
        s[1] += 16
        dep = (semname, s[0], s[1], None)
        for w in writes:
            self.lastw[w] = dep
            self.readers[w] = {}
        for r in reads:
            self.readers.setdefault(r, {})[semname] = dep
        return ins

    def retire(self, old, new):
        deps = []
        for o in old:
            d = self.lastw.get(o)
            if d is not None:
                deps.append(d)
            for k, d in self.readers.get(o, {}).items():
                deps.append(d)
        best = {}
        for d in deps:
            if d[0] not in best or best[d[0]][2] < d[2]:
                best[d[0]] = d
        for n in new:
            rd = self.readers.setdefault(n, {})
            for k, d in best.items():
                rd['al_' + k] = (d[0], d[1], d[2], 'alias')

    def finish(self, e='sp'):
        for name, (h, v) in self.dsem.items():
            self._wait(e, (name, h, v, None))
        for f in self.eng:
            if f != e and self.cnt[f] > 0:
                self._wait(e, ('e_' + f, self.sem[f], self.cnt[f], f))

    def ew(self):
        self.rr += 1
        return 'dve' if (self.rr & 1) else 'pool'


def bc_mid(ap, n):
    return ap.unsqueeze(1).to_broadcast([ap.shape[0], n, ap.shape[1]])


def bc_last(ap, n):
    return ap.unsqueeze(2).to_broadcast([ap.shape[0], ap.shape[1], n])


def build(NSEQ, SEQ, TB, debug=()):
    nc = bass.Bass("TRN2", target_bir_lowering=False)
    NBLK = SEQ // TB
    NT = TB // 128
    dt_ = nc.dram_tensor
    xT = dt_("xT", [NSEQ, 8, 128, SEQ], F32, kind="ExternalInput").ap()
    params_d = dt_("params", [128, NPAR], F32, kind="ExternalInput").ap()
    consts_d = dt_("consts", [128, 8, 128], F32, kind="ExternalInput").ap()
    w_in_d = dt_("w_in_t", [NJ_IN, 128, 1024], F32, kind="ExternalInput").ap()
    w_sm_d = dt_("w_small", [128, 256], F32, kind="ExternalInput").ap()
    w_out_d = dt_("w_out_t", [8, 128, 2048], F32, kind="ExternalInput").ap()
    w_up_d = dt_("w_up_t", [NJ_UP, 128, 1024], F32, kind="ExternalInput").ap()
    w_dn_d = dt_("w_dn_t", [8, 128, KC_DN * 128], F32, kind="ExternalInput").ap()
    outT = dt_("outT", [NSEQ, 8, 128, SEQ], F32, kind="ExternalOutput").ap()
    sc_in = dt_("sc_in", [NJ_IN, 128, 1024], BF16, kind="Internal").ap()
    sc_out = dt_("sc_out", [8, 128, 2048], BF16, kind="Internal").ap()
    sc_up = dt_("sc_up", [NJ_UP, 128, 1024], BF16, kind="Internal").ap()
    sc_dn = dt_("sc_dn", [8, 128, KC_DN * 128], BF16, kind="Internal").ap()

    P = Prog(nc)
    dbg_outs = {}

    def sb(name, shape, dt=F32):
        return nc.alloc_sbuf_tensor(name, shape, dt).ap()

    def dump(name, ap, res):
        if name not in debug:
            return
        shp = list(ap.shape)
        key = "dbg_" + name
        k = dbg_outs.get(key, 0)
        dbg_outs[key] = k + 1
        t = dt_("%s_%d" % (key, k), shp, ap.dtype, kind="ExternalOutput").ap()
        P.dma('sp', t, ap, reads=res, semname='dbgsem')

    PAR = sb("PAR", [128, NPAR])
    CST = sb("CST", [128, 8, 128])
    identF = CST[:, 0, :]
    mcumF = CST[:, 1, :]
    onesF = CST[:, 2, :]
    mL = CST[:, 3, :]
    mUs = CST[:, 4, :]
    mUi = CST[:, 5, :]
    mBD = CST[:, 6, :]
    mOFF = CST[:, 7, :]
    identB = sb("identB", [128, 128], BF16)
    onesB = sb("onesB", [128, 128], BF16)
    WSM = sb("WSM", [128, 8, 32], BF16)
    NEGA = sb("NEGA", [128, 24])
    X = sb("X", [128, 8, TB])
    H = sb("H", [128, 8, TB], BF16)
    RSTD = sb("RSTD", [128, TB])
    REGA = sb("REGA", [128, 52 * 512], BF16)
    QKV = REGA[:, 0:24 * TB].rearrange("p (a b) -> p a b", a=24)
    XBC = REGA[:, 24 * 512:24 * 512 + 12 * TB].rearrange("p (a b) -> p a b", a=12)
    ZA = REGA[:, 36 * 512:36 * 512 + 8 * TB].rearrange("p (a b) -> p a b", a=8)
    ZS = REGA[:, 44 * 512:44 * 512 + 8 * TB].rearrange("p (a b) -> p a b", a=8)
    ACTB = REGA[:, 0:22 * TB].rearrange("p (a b) -> p a b", a=22)
    MO = REGA[:, 22 * 512:22 * 512 + 16 * TB].bitcast(F32).rearrange("p (a b) -> p a b", a=8)
    MIX = sb("MIX", [128, 16, TB], BF16)
    SQ = MIX[:, 0:8, :]
    STGall = sb("STGall", [128, 4, TB + 3])
    ACCall = sb("ACCall", [128, 4, TB])
    STG = [STGall[:, i, :] for i in range(4)]
    ACC = [ACCall[:, i, :] for i in range(4)]
    NW = 7
    WS = [sb("WS%d" % i, [128, 1024], BF16) for i in range(NW)]
    TAILG = sb("TAILG", [128, 36, 3])
    TAILF = sb("TAILF", [128, 44, 2])
    SMALL = sb("SMALL", [128, NT, 32])
    S = sb("S", [128, 8, 128])
    Sb = sb("Sb", [128, 8, 128], BF16)
    SS = sb("SS", [128, 16, 64])
    SSb = sb("SSb", [128, 16, 64], BF16)
    import collections

    class RPool:
        def __init__(self, items, lag):
            self.free = collections.deque(items)
            self.recent = collections.deque()
            self.lag = lag

        def get(self):
            while len(self.recent) >= self.lag:
                self.free.append(self.recent.popleft())
            it = self.free.popleft()
            self.recent.append(it)
            return it

        def flush(self):
            while self.recent:
                self.free.append(self.recent.popleft())

        def acquire(self):
            while not self.free:
                yield
            return self.free.popleft()

        def release(self, it):
            self.free.append(it)

    def run_chains(chains):
        chains = list(chains)
        guard = 0
        while chains:
            n0 = P.ninst
            for c in list(chains):
                try:
                    next(c)
                except StopIteration:
                    chains.remove(c)
            guard = guard + 1 if P.ninst == n0 else 0
            assert guard < 1000, "chain deadlock"

    NF = 7
    FPP = RPool([(sb("FP%d" % i, [128, 512]), 'FP%d' % i) for i in range(NF)], 2)

    def ftmp():
        return FPP.get()

    WSMF = FPP.free[0][0][:, 0:256].rearrange("p (a b) -> p a b", a=8)

    NB = 8
    BPP = RPool([(sb("BP%d" % i, [128, 512], BF16), 'BP%d' % i) for i in range(NB)], 2)

    def btmp():
        return BPP.get()

    GP = RPool([(sb("GP%d" % i, [128, 4, 128], BF16), "GP%d" % i) for i in range(27)], 99)

    def acquire_n(pool, n):
        while len(pool.free) < n:
            yield
        return [pool.free.popleft() for _ in range(n)]

    BTM = sb("BTM", [128, 2, 128], BF16)
    GTb = [sb("GT%d" % i, [128, 4, 128], BF16) for i in range(2)]
    CDECb = [sb("CDEC%d" % i, [128, 4, 128], BF16) for i in range(2)]
    XDTb = [sb("XDT%d" % i, [128, 4, 64], BF16) for i in range(2)]
    XDDb = [sb("XDD%d" % i, [128, 4, 64], BF16) for i in range(2)]
    YG = sb("YG", [128, 4, 128])
    BCS = sb("BCS", [128, 256])
    SMB = [{n_: sb("sm%d%s" % (i, n_), [128, w_]) for n_, w_ in
            [("NLB", 8), ("BETA", 8), ("U", 24), ("SPL", 24), ("V", 24), ("CSUM", 24), ("TOT", 24),
             ("DIFF", 24), ("NCS", 24), ("ECS", 24), ("EREM", 24), ("ETOT", 24), ("GB", 8), ("BEG", 8), ("DTE", 16)]}
           for i in range(2)]
    BANKS = RPool([(nc.alloc_psum_tensor("ps%d" % i, [128, 512], F32).ap(), 'ps%d' % i) for i in range(8)], 5)

    def bank():
        return BANKS.get()

    def par(name):
        a, b = PO[name]
        return PAR[:, a:b]

    P.dma('sp', PAR, params_d, writes=['PAR'])
    P.dma('sp', CST, consts_d, writes=['CST'])
    P.dma('sp', FPP.free[0][0][:, 0:256], w_sm_d, writes=['FP0'])
    for j in range(NJ_IN):
        P.dma('pool', sc_in[j], w_in_d[j], writes=['sc_in_g%d' % (j // 13)], semname='c_in%d' % (j // 13))
    for j in range(8):
        P.dma('pool', sc_out[j], w_out_d[j], writes=['sc_out'], semname='c_out')

    def emit_rest_casts():
        for j in range(NJ_UP):
            P.dma('pool', sc_up[j], w_up_d[j], writes=['sc_up'], semname='c_up')
        for j in range(8):
            P.dma('pool', sc_dn[j], w_dn_d[j], writes=['sc_dn'], semname='c_dn')

    P.op('dve', lambda e: e.tensor_copy(out=identB, in_=identF), reads=['CST'], writes=['identB'])
    P.op('dve', lambda e: e.tensor_copy(out=onesB, in_=onesF), reads=['CST'], writes=['onesB'])
    P.op('dve', lambda e: e.tensor_copy(out=WSM, in_=WSMF), reads=['FP0'], writes=['WSM'])
    P.op('act', lambda e: e.activation(out=NEGA, in_=par("alog"), func=AF.Exp), reads=['PAR'], writes=['NEGA'])
    P.op('dve', lambda e: e.tensor_scalar_mul(out=NEGA, in0=NEGA, scalar1=-1.0), reads=['NEGA'], writes=['NEGA'])

    wreq = []
    for s_ in range(NSEQ):
        for b_ in range(NBLK):
            wreq += [('in', j, 0) for j in range(NJ_IN)]
            wreq += [('out', j, q) for j in range(8) for q in range(2)]
            for jj in range(22):
                wreq += [('up', jj, 0), ('up', 22 + jj, 0)]
            wreq += [('dn', j, q) for j in range(8) for q in range(3)]
    wstate = {'issued': 0, 'next': 0}

    def w_issue(upto):
        while wstate['issued'] < min(upto, len(wreq)):
            k = wstate['issued']
            kind, j, q = wreq[k]
            slot = k % NW
            if kind == 'in':
                P.dma('sp', WS[slot], sc_in[j], reads=['sc_in_g%d' % (j // 13)], writes=['WS%d' % slot])
            elif kind == 'up':
                P.dma('sp', WS[slot], sc_up[j], reads=['sc_up'], writes=['WS%d' % slot])
            elif kind == 'out':
                P.dma('sp', WS[slot], sc_out[j][:, q * 1024:(q + 1) * 1024], reads=['sc_out'], writes=['WS%d' % slot])
            else:
                n = 1024 if q < 2 else (KC_DN * 128 - 2048)
                P.dma('sp', WS[slot][:, 0:n], sc_dn[j][:, q * 1024:q * 1024 + n], reads=['sc_dn'], writes=['WS%d' % slot])
            wstate['issued'] += 1

    def w_get(kind, j, q=0):
        k = wstate['next']
        assert wreq[k] == (kind, j, q), (wreq[k], kind, j, q)
        w_issue(k + NW)
        wstate['next'] += 1
        slot = k % NW
        return WS[slot].rearrange("p (a b) -> p a b", a=8), 'WS%d' % slot

    def rms_sq_step(bst, bstr, kc, src_ap, srcres, nk=8, src_is_psum=None, sqbuf=None, sqname='MIX'):
        sq_ = SQ if sqbuf is None else sqbuf
        rn_ = '%s%d' % (sqname, kc)
        w_ = [rn_] + ([src_is_psum] if src_is_psum else [])
        P.op('act', lambda e: e.activation(out=sq_[:, kc, :], in_=src_ap, func=AF.Square), reads=srcres, writes=w_)
        P.op('pe', lambda e: e.matmul(bst[:, 0:TB], lhsT=onesB, rhs=sq_[:, kc, :], start=(kc == 0), stop=(kc == nk - 1)),
             reads=[rn_, 'onesB'], writes=[bstr], signal=(kc == nk - 1))

    def rms_finish(bst, bstr, rstd, rstdr, nfeat):
        P.op('act', lambda e: e.activation(out=rstd[:, 0:TB], in_=bst[:, 0:TB], func=AF.Ln, scale=1.0 / nfeat, bias=EPS),
             reads=[], writes=[bstr, rstdr])
        BANKS.release((bst, bstr))
        P.op('act', lambda e: e.activation(out=rstd[:, 0:TB], in_=rstd[:, 0:TB], func=AF.Exp, scale=-0.5),
             reads=[rstdr], writes=[rstdr])

    mixres = ['MIX%d' % j for j in range(16)]
    sqres_all = mixres[0:8]
    regA_mix = (['QKV%d_%d' % (j, i) for j in range(24) for i in range(NT)] + ['XBC%d' % j for j in range(12)] +
                ['ZA%d' % j for j in range(8)] + ['ZS%d' % j for j in range(8)])
    regA_ffn = ['ACTB%d' % j for j in range(22)] + ['MO%d' % j for j in range(8)]

    STGP = RPool([(STG[i], ACC[i], 'STG%d' % i, 'ACC%d' % i) for i in range(len(STG))], 99)

    def conv_chain(bk, bkr, cw, ntap, bias_ap, tail, tres, first_blk, T, final, npool, tiny):
        h_ = ntap - 1
        slot = yield from STGP.acquire()
        stg, acc, stgr, accr = slot
        if bias_ap is None:
            P.op('act', lambda e: e.activation(out=acc[:, 0:T], in_=bk[:, 0:T], func=AF.Identity, scale=cw[:, h_:h_ + 1]),
                 reads=['PAR'], writes=[bkr, accr])
        else:
            P.op('act', lambda e: e.activation(out=acc[:, 0:T], in_=bk[:, 0:T], func=AF.Identity, scale=cw[:, h_:h_ + 1], bias=bias_ap),
                 reads=['PAR'], writes=[bkr, accr])
        P.op('act', lambda e: e.activation(out=stg[:, h_:h_ + T], in_=bk[:, 0:T], func=AF.Copy), reads=[], writes=[bkr, stgr])
        BANKS.release((bk, bkr))
        if first_blk:
            P.op(tiny, lambda e: e.memset(stg[:, 0:h_], 0.0), reads=[], writes=[stgr])
        else:
            P.op(tiny, lambda e: e.tensor_copy(out=stg[:, 0:h_], in_=tail), reads=[tres], writes=[stgr])
        yield
        ptmp = []
        for k in range(h_ - npool, h_):
            tmp, tmpr = yield from FPP.acquire()
            P.op('pool', lambda e, k=k, tmp=tmp: e.tensor_tensor(out=tmp[:, 0:T], in0=stg[:, k:k + T], in1=cw[:, k:k + 1].to_broadcast([128, T]), op=ALU.mult),
                 reads=[stgr, 'PAR'], writes=[tmpr])
            ptmp.append((tmp, tmpr))
        for k in range(h_ - npool):
            P.op('dve', lambda e, k=k: e.scalar_tensor_tensor(out=acc[:, 0:T], in0=stg[:, k:k + T], scalar=cw[:, k:k + 1], in1=acc[:, 0:T],
                                                            op0=ALU.mult, op1=ALU.add),
                 reads=[stgr, accr, 'PAR'], writes=[accr])
        for tmp, tmpr in ptmp:
            P.op('pool', lambda e, tmp=tmp: e.tensor_tensor(out=acc[:, 0:T], in0=acc[:, 0:T], in1=tmp[:, 0:T], op=ALU.add), reads=[tmpr, accr], writes=[accr])
            FPP.release((tmp, tmpr))
        P.op(tiny, lambda e: e.tensor_copy(out=tail, in_=stg[:, T:T + h_]), reads=[stgr], writes=[tres])
        yield
        yield
        final(acc, accr)
        STGP.release(slot)

    def step_active(active):
        for c in list(active):
            try:
                next(c)
            except StopIteration:
                active.remove(c)

    def take_bank(active=()):
        n = 0
        while not BANKS.free:
            step_active(active)
            n += 1
            assert n < 100, "no free PSUM bank"
        return BANKS.free.popleft()

    for s_ in range(NSEQ):
        for b_ in range(NBLK):
            first_blk = (b_ == 0)
            t0b = b_ * TB
            xres = ['X%d' % kc for kc in range(8)]

            xs = [ACC[k][:, 0:TB] for k in range(4)] + [STG[k][:, 0:TB] for k in range(4)]
            xsres = ['ACC%d' % k for k in range(4)] + ['STG%d' % k for k in range(4)]

            def load_x(ss, bb):
                t0_ = bb * TB
                P.dma('sp', ACCall[:, :, 0:TB], xT[ss, 0:4, :, t0_:t0_ + TB].rearrange("k p t -> p k t"), writes=xsres[0:4], semname='d_xsa')
                P.dma('sp', STGall[:, :, 0:TB], xT[ss, 4:8, :, t0_:t0_ + TB].rearrange("k p t -> p k t"), writes=xsres[4:8], semname='d_xsb')

            if s_ == 0 and b_ == 0:
                load_x(0, 0)
            if first_blk:
                P.op('dve', lambda e: e.memset(S, 0.0), writes=['S0', 'S1'])
                P.op('dve', lambda e: e.memset(Sb, 0.0), writes=['Sb0', 'Sb1'])
                P.op('dve', lambda e: e.memset(SS, 0.0), writes=['SS%d' % k for k in range(4)])
                P.op('dve', lambda e: e.memset(SSb, 0.0), writes=['SSb%d' % k for k in range(4)])
            P.retire(regA_ffn, regA_mix)
            BANKS.flush()
            FPP.flush()
            BPP.flush()
            bst, bstr = take_bank()
            for kc in range(8):
                rms_sq_step(bst, bstr, kc, xs[kc], [xsres[kc]])
            rms_finish(bst, bstr, RSTD, 'RSTD', D)
            cp_eng = 'act' if (s_ == 0 and b_ == 0) else 'pool'
            for kc in range(8):
                P.op('dve', lambda e, kc=kc: e.scalar_tensor_tensor(out=H[:, kc, :], in0=xs[kc], scalar=par("w1")[:, kc:kc + 1], in1=RSTD,
                                                                    op0=ALU.mult, op1=ALU.mult),
                     reads=[xsres[kc], 'RSTD', 'PAR'], writes=['H%d' % kc])
                if cp_eng == 'act':
                    P.op('act', lambda e, kc=kc: e.activation(out=X[:, kc, :], in_=xs[kc], func=AF.Copy), reads=[xsres[kc]], writes=[xres[kc]])
                else:
                    P.op('pool', lambda e, kc=kc: e.tensor_copy(out=X[:, kc, :], in_=xs[kc]), reads=[xsres[kc]], writes=[xres[kc]])
            hres = ['H%d' % kc for kc in range(8)]
            dump("H", H, hres)
            BANKS.flush()
            FPP.flush()
            BPP.flush()
            blk0 = (s_ == 0 and b_ == 0)
            tiny = 'dve' if blk0 else 'pool'
            npool_in = 0 if blk0 else 1
            allt = lambda nm: ['%s_%d' % (nm, i) for i in range(NT)]
            active = []
            for j in range(NJ_IN):
                wv, wr = w_get('in', j)
                bk, bkr = take_bank(active)
                for kc in range(8):
                    P.op('pe', lambda e, kc=kc: e.matmul(bk[:, 0:TB], lhsT=wv[:, kc, :], rhs=H[:, kc, :], start=(kc == 0), stop=(kc == 7)),
                         reads=[wr] + hres, writes=[bkr], signal=(kc == 7))
                if j < 24:
                    cwj = par("cwg")[:, 4 * j:4 * j + 4]
                    fin = lambda acc, accr, j=j: P.op('act', lambda e: e.activation(out=QKV[:, j, :], in_=acc[:, 0:TB], func=AF.Silu), reads=[accr], writes=allt('QKV%d' % j))
                    active.append(conv_chain(bk, bkr, cwj, 4, None, TAILG[:, j, :], 'TG%d' % j, first_blk, TB, fin, npool_in * (j & 1), tiny))
                elif j < 32:
                    P.op('act', lambda e: e.activation(out=ZA[:, j - 24, :], in_=bk[:, 0:TB], func=AF.Silu), reads=[], writes=[bkr, 'ZA%d' % (j - 24)])
                    BANKS.release((bk, bkr))
                elif j < 40:
                    P.op('act', lambda e: e.activation(out=ZS[:, j - 32, :], in_=bk[:, 0:TB], func=AF.Silu), reads=[], writes=[bkr, 'ZS%d' % (j - 32)])
                    BANKS.release((bk, bkr))
                else:
                    jj = j - 40
                    cwj = par("cws")[:, 4 * jj:4 * jj + 4]
                    fin = lambda acc, accr, jj=jj: P.op('act', lambda e: e.activation(out=XBC[:, jj, :], in_=acc[:, 0:TB], func=AF.Silu), reads=[accr], writes=['XBC%d' % jj])
                    active.append(conv_chain(bk, bkr, cwj, 4, par("cbs")[:, jj:jj + 1], TAILG[:, 24 + jj, :], 'TG%d' % (24 + jj), first_blk, TB, fin, npool_in * (jj & 1), tiny))
                step_active(active)
            run_chains(active)
            if blk0:
                emit_rest_casts()
            for i in range(NT):
                bk, bkr = take_bank()
                for kc in range(8):
                    P.op('pe', lambda e, kc=kc: e.matmul(bk[:, 0:32], lhsT=H[:, kc, i * 128:(i + 1) * 128], rhs=WSM[:, kc, :], start=(kc == 0), stop=(kc == 7)),
                         reads=['WSM'] + hres, writes=[bkr], signal=(kc == 7))
                P.op('dve', lambda e: e.tensor_copy(out=SMALL[:, i, :], in_=bk[:, 0:32]), reads=[], writes=[bkr, 'SMALL%d' % i])
                BANKS.release((bk, bkr))
            dump("QKV", QKV, [x_ for j in range(24) for x_ in allt('QKV%d' % j)])
            dump("XBC", XBC, ['XBC%d' % j for j in range(12)])
            dump("SMALL", SMALL, ['SMALL%d' % i for i in range(NT)])
            P.retire(sqres_all, mixres)
            BANKS.flush()
            FPP.flush()
            BPP.flush()
            gates_emitted = [False] * NT
            mk_eng = 'dve' if blk0 else 'pool'
            cons_done = [0] * NT
            f2 = lambda a: a.rearrange("p a b -> p (a b)")
            v4 = lambda a: a.rearrange("p (a b) -> p a b", a=4)

            def gates_all():
                for i in range(NT):
                    while i >= 2 and cons_done[i - 2] < 3:
                        yield
                    sm = SMB[i % 2]
                    smr = 'SMB%d' % (i % 2)
                    SMi = SMALL[:, i, :]
                    smallr = 'SMALL%d' % i
                    P.op('act', lambda e: e.activation(out=sm["NLB"], in_=SMi[:, 0:8], func=AF.Exp, scale=-1.0), reads=[smallr], writes=[smr + 'NLB'])
                    P.op('act', lambda e: e.activation(out=sm["NLB"], in_=sm["NLB"], func=AF.Ln, bias=1.0), reads=[smr + 'NLB'], writes=[smr + 'NLB'])
                    P.op('act', lambda e: e.activation(out=sm["BETA"], in_=sm["NLB"], func=AF.Exp, scale=-1.0), reads=[smr + 'NLB'], writes=[smr + 'BETA'])
                    P.op('dve', lambda e: e.tensor_tensor(out=sm["U"], in0=SMi[:, 8:32], in1=par("dtb"), op=ALU.add), reads=[smallr, 'PAR'], writes=[smr + 'U'])
                    yield
                    P.op('act', lambda e: e.activation(out=sm["SPL"], in_=sm["U"], func=AF.Exp), reads=[smr + 'U'], writes=[smr + 'SPL'])
                    P.op('act', lambda e: e.activation(out=sm["SPL"], in_=sm["SPL"], func=AF.Ln, bias=1.0), reads=[smr + 'SPL'], writes=[smr + 'SPL'])
                    yield
                    P.op('dve', lambda e: e.tensor_tensor(out=sm["V"], in0=sm["SPL"], in1=NEGA, op=ALU.mult), reads=[smr + 'SPL', 'NEGA'], writes=[smr + 'V'])
                    bk, bkr = yield from BANKS.acquire()
                    bk2, bk2r = yield from BANKS.acquire()
                    P.op('pe', lambda e: e.matmul(bk[:, 0:24], lhsT=mcumF, rhs=sm["V"], start=True, stop=True), reads=[smr + 'V', 'CST'], writes=[bkr])
                    P.op('pe', lambda e: e.matmul(bk2[:, 0:24], lhsT=onesF, rhs=sm["V"], start=True, stop=True), reads=[smr + 'V', 'CST'], writes=[bk2r])
                    yield
                    P.op('dve', lambda e: e.tensor_copy(out=sm["CSUM"], in_=bk[:, 0:24]), reads=[], writes=[bkr, smr + 'CSUM'])
                    P.op('act', lambda e: e.activation(out=sm["TOT"], in_=bk2[:, 0:24], func=AF.Copy), reads=[], writes=[bk2r, smr + 'TOT'])
                    P.op('act', lambda e: e.activation(out=sm["NCS"], in_=bk[:, 0:24], func=AF.Copy, scale=-1.0), reads=[], writes=[bkr, smr + 'NCS'])
                    BANKS.release((bk, bkr))
                    BANKS.release((bk2, bk2r))
                    yield
                    P.op('dve', lambda e: e.tensor_tensor(out=sm["DIFF"], in0=sm["TOT"], in1=sm["CSUM"], op=ALU.subtract), reads=[smr + 'TOT', smr + 'CSUM'], writes=[smr + 'DIFF'])
                    P.op('act', lambda e: e.activation(out=sm["ECS"], in_=sm["CSUM"], func=AF.Exp), reads=[smr + 'CSUM'], writes=[smr + 'ECS'])
                    P.op('act', lambda e: e.activation(out=sm["ETOT"], in_=sm["TOT"], func=AF.Exp), reads=[smr + 'TOT'], writes=[smr + 'ETOT'])
                    P.op('dve', lambda e: e.tensor_tensor(out=sm["GB"], in0=sm["CSUM"][:, 0:8], in1=sm["NLB"], op=ALU.subtract), reads=[smr + 'CSUM', smr + 'NLB'], writes=[smr + 'GB'])
                    yield
                    P.op('act', lambda e: e.activation(out=sm["EREM"], in_=sm["DIFF"], func=AF.Exp), reads=[smr + 'DIFF'], writes=[smr + 'EREM'])
                    P.op('act', lambda e: e.activation(out=sm["BEG"], in_=sm["GB"], func=AF.Exp), reads=[smr + 'GB'], writes=[smr + 'BEG'])
                    yield
                    P.op('dve', lambda e: e.tensor_tensor(out=sm["DTE"], in0=sm["SPL"][:, 8:24], in1=sm["EREM"][:, 8:24], op=ALU.mult), reads=[smr + 'SPL', smr + 'EREM'], writes=[smr + 'DTE'])
                    gates_emitted[i] = True
                    yield

            rec_done = [[False] * NT for _ in range(2)]
            gdn_items = [(i, hg) for i in range(NT) for hg in range(2)]

            def gdn_worker():
                while gdn_items:
                    i, hg = gdn_items.pop(0)
                    while not gates_emitted[i]:
                        yield
                    yield from gdn_chain(i, hg)
                    cons_done[i] += 1

            def ssd_all():
                for i in range(NT):
                    while not gates_emitted[i]:
                        yield
                    yield from ssd_chain(i)
                    cons_done[i] += 1

            def gdn_chain(i, hg):
                tk = slice(i * 128, (i + 1) * 128)
                sm = SMB[i % 2]
                smr = 'SMB%d' % (i % 2)
                h0 = 4 * hg
                qres = ['QKV%d_%d' % (h0 + k, i) for k in range(4)]
                kres = ['QKV%d_%d' % (8 + h0 + k, i) for k in range(4)]
                vres = ['QKV%d_%d' % (16 + h0 + k, i) for k in range(4)]
                q4 = QKV[:, h0:h0 + 4, tk]
                k4 = QKV[:, 8 + h0:8 + h0 + 4, tk]
                csum4 = bc_last(sm["CSUM"][:, h0:h0 + 4], 128)
                gb4 = bc_last(sm["GB"][:, h0:h0 + 4], 128)
                for x4, xres, sc in ((k4, kres, 1.0), (q4, qres, 128.0 ** -0.5)):
                    sq, sqr = yield from BPP.acquire()
                    P.op('act', lambda e: e.activation(out=v4(sq), in_=x4, func=AF.Square), reads=xres, writes=[sqr])
                    yield
                    bk, bkr = yield from BANKS.acquire()
                    P.op('pe', lambda e: e.matmul(bk, lhsT=onesB, rhs=sq, start=True, stop=True), reads=[sqr, 'onesB'], writes=[bkr])
                    BPP.release((sq, sqr))
                    yield
                    rn, rnr = yield from FPP.acquire()
                    P.op('act', lambda e: e.activation(out=rn, in_=bk, func=AF.Ln, bias=EPS), reads=[], writes=[bkr, rnr])
                    BANKS.release((bk, bkr))
                    P.op('act', lambda e: e.activation(out=rn, in_=rn, func=AF.Exp, scale=-0.5), reads=[rnr], writes=[rnr])
                    yield
                    P.op('dve', lambda e, sc=sc: e.scalar_tensor_tensor(out=x4, in0=x4, scalar=sc, in1=v4(rn), op0=ALU.mult, op1=ALU.mult),
                         reads=xres + [rnr], writes=xres)
                    FPP.release((rn, rnr))
                    yield
                bR1, bR1r = yield from BANKS.acquire()
                for hl in range(4):
                    P.op('pe', lambda e, hl=hl: e.matmul(bR1[:, hl * 128:(hl + 1) * 128], lhsT=sm["CSUM"][:, h0 + hl:h0 + hl + 1].to_broadcast([128, 128]), rhs=identF, start=True, stop=True),
                         reads=[smr + 'CSUM', 'CST'], writes=[bR1r], signal=(hl == 3))
                yield
                gml, gmlr = yield from FPP.acquire()
                P.op('dve', lambda e: e.scalar_tensor_tensor(out=v4(gml), in0=v4(bR1), scalar=-1.0, in1=bc_mid(mL, 4), op0=ALU.mult, op1=ALU.add), reads=['CST'], writes=[bR1r, gmlr])
                yield
                EL, ELr = yield from BPP.acquire()
                for hl in range(4):
                    P.op('act', lambda e, hl=hl: e.activation(out=EL[:, hl * 128:(hl + 1) * 128], in_=gml[:, hl * 128:(hl + 1) * 128], func=AF.Exp, bias=sm["GB"][:, h0 + hl:h0 + hl + 1]),
                         reads=[gmlr, smr + 'GB'], writes=[ELr])
                FPP.release((gml, gmlr))
                gmq, gmqr = yield from FPP.acquire()
                P.op('dve', lambda e: e.tensor_tensor(out=v4(gmq), in0=v4(bR1), in1=bc_mid(mUi, 4), op=ALU.add), reads=['CST'], writes=[bR1r, gmqr])
                yield
                EQ, EQr = yield from BPP.acquire()
                for hl in range(4):
                    P.op('act', lambda e, hl=hl: e.activation(out=EQ[:, hl * 128:(hl + 1) * 128], in_=gmq[:, hl * 128:(hl + 1) * 128], func=AF.Exp, bias=sm["NCS"][:, h0 + hl:h0 + hl + 1]),
                         reads=[gmqr, smr + 'NCS'], writes=[EQr])
                FPP.release((gmq, gmqr))
                er1, er1r = yield from BPP.acquire()
                P.op('act', lambda e: e.activation(out=er1, in_=bR1, func=AF.Exp), reads=[], writes=[bR1r, er1r])
                BANKS.release((bR1, bR1r))
                yield
                bR2, bR2r = yield from BANKS.acquire()
                for hl in range(4):
                    P.op('pe', lambda e, hl=hl: e.matmul(bR2[:, hl * 128:(hl + 1) * 128], lhsT=sm["GB"][:, h0 + hl:h0 + hl + 1].to_broadcast([128, 128]), rhs=identF, start=True, stop=True),
                         reads=[smr + 'GB', 'CST'], writes=[bR2r], signal=(hl == 3))
                (QD, QDr), (QKT, QKTr), (Pc, Pcr), (PTc, PTcr), (Pn, Pnr), (PTn, PTnr), (Tc, Tcr), (Tn, Tnr), (NOT, NOTr) = yield from acquire_n(GP, 9)
                P.op('dve', lambda e: e.tensor_tensor(out=QD, in0=q4, in1=v4(er1), op=ALU.mult), reads=qres + [er1r], writes=[QDr])
                BPP.release((er1, er1r))
                yield
                gmu, gmur = yield from FPP.acquire()
                P.op('dve', lambda e: e.tensor_tensor(out=v4(gmu), in0=v4(bR2), in1=bc_mid(mUs, 4), op=ALU.add), reads=['CST'], writes=[bR2r, gmur])
                BANKS.release((bR2, bR2r))
                yield
                EU, EUr = yield from BPP.acquire()
                for hl in range(4):
                    P.op('act', lambda e, hl=hl: e.activation(out=EU[:, hl * 128:(hl + 1) * 128], in_=gmu[:, hl * 128:(hl + 1) * 128], func=AF.Exp, bias=sm["NCS"][:, h0 + hl:h0 + hl + 1]),
                         reads=[gmur, smr + 'NCS'], writes=[EUr])
                FPP.release((gmu, gmur))
                bK, bKr = yield from BANKS.acquire()
                for hl in range(4):
                    kh = QKV[:, 8 + h0 + hl, tk]
                    P.op('pe', lambda e, kh=kh, hl=hl: e.matmul(bK[:, hl * 128:(hl + 1) * 128], lhsT=kh, rhs=kh, start=True, stop=True),
                         reads=kres, writes=[bKr], signal=(hl == 3))
                bKQ, bKQr = yield from BANKS.acquire()
                for hl in range(4):
                    kh = QKV[:, 8 + h0 + hl, tk]
                    qh = QKV[:, h0 + hl, tk]
                    P.op('pe', lambda e, kh=kh, qh=qh, hl=hl: e.matmul(bKQ[:, hl * 128:(hl + 1) * 128], lhsT=kh, rhs=qh, start=True, stop=True),
                         reads=kres + qres, writes=[bKQr], signal=(hl == 3))
                yield
                P.op('dve', lambda e: e.scalar_tensor_tensor(out=f2(PTc), in0=bK, scalar=-1.0, in1=EL, op0=ALU.mult, op1=ALU.mult), reads=[ELr], writes=[bKr, PTcr])
                BPP.release((EL, ELr))
                yield
                P.op('dve', lambda e: e.scalar_tensor_tensor(out=f2(Pc), in0=bK, scalar=-1.0, in1=EU, op0=ALU.mult, op1=ALU.mult), reads=[EUr], writes=[bKr, Pcr])
                BPP.release((EU, EUr))
                BANKS.release((bK, bKr))
                yield
                P.op('dve', lambda e: e.tensor_tensor(out=f2(QKT), in0=bKQ, in1=EQ, op=ALU.mult), reads=[EQr], writes=[bKQr, QKTr])
                BPP.release((EQ, EQr))
                BANKS.release((bKQ, bKQr))
                P.op(mk_eng, lambda e: e.tensor_tensor(out=NOT, in0=PTc, in1=bc_mid(mOFF, 4), op=ALU.mult), reads=[PTcr, 'CST'], writes=[NOTr])
                P.op(mk_eng, lambda e: e.tensor_tensor(out=PTc, in0=PTc, in1=bc_mid(mBD, 4), op=ALU.mult), reads=[PTcr, 'CST'], writes=[PTcr])
                yield
                P.op(mk_eng, lambda e: e.tensor_tensor(out=Pc, in0=Pc, in1=bc_mid(mBD, 4), op=ALU.mult), reads=[Pcr, 'CST'], writes=[Pcr])
                P.op(mk_eng, lambda e: e.tensor_tensor(out=Tc, in0=Pc, in1=bc_mid(identB, 4), op=ALU.add), reads=[Pcr, 'identB'], writes=[Tcr])
                yield
                NLEV = 5
                for m in range(NLEV):
                    last = (m == NLEV - 1)
                    bPT, bPTr = yield from BANKS.acquire()
                    for hl in range(4):
                        P.op('pe', lambda e, hl=hl: e.matmul(bPT[:, hl * 128:(hl + 1) * 128], lhsT=Pc[:, hl, :], rhs=PTc[:, hl, :], start=True, stop=True),
                             reads=[PTcr, Pcr], writes=[bPTr], signal=(hl == 3))
                    if not last:
                        bP, bPr = yield from BANKS.acquire()
                        for hl in range(4):
                            P.op('pe', lambda e, hl=hl: e.matmul(bP[:, hl * 128:(hl + 1) * 128], lhsT=PTc[:, hl, :], rhs=Pc[:, hl, :], start=True, stop=True),
                                 reads=[PTcr, Pcr], writes=[bPr], signal=(hl == 3))
                    yield
                    P.op('act', lambda e: e.activation(out=f2(PTn), in_=bPT, func=AF.Copy), reads=[], writes=[bPTr, PTnr])
                    BANKS.release((bPT, bPTr))
                    if not last:
                        P.op('dve', lambda e: e.tensor_copy(out=f2(Pn), in_=bP), reads=[], writes=[bPr, Pnr])
                        BANKS.release((bP, bPr))
                    yield
                    bT, bTr = yield from BANKS.acquire()
                    for hl in range(4):
                        P.op('pe', lambda e, hl=hl: e.matmul(bT[:, hl * 128:(hl + 1) * 128], lhsT=identB, rhs=Tc[:, hl, :], start=True, stop=False),
                             reads=[Tcr, 'identB'], writes=[bTr], signal=False)
                        P.op('pe', lambda e, hl=hl: e.matmul(bT[:, hl * 128:(hl + 1) * 128], lhsT=PTn[:, hl, :], rhs=Tc[:, hl, :], start=False, stop=True),
                             reads=[Tcr, PTnr], writes=[bTr], signal=(hl == 3))
                    yield
                    P.op('dve' if (m & 1) else 'act',
                         (lambda e: e.tensor_copy(out=f2(Tn), in_=bT)) if (m & 1) else (lambda e: e.activation(out=f2(Tn), in_=bT, func=AF.Copy)),
                         reads=[], writes=[bTr, Tnr])
                    BANKS.release((bT, bTr))
                    Pc, Pcr, Pn, Pnr = Pn, Pnr, Pc, Pcr
                    PTc, PTcr, PTn, PTnr = PTn, PTnr, PTc, PTcr
                    Tc, Tcr, Tn, Tnr = Tn, Tnr, Tc, Tcr
                    yield
                bTT, bTTr = yield from BANKS.acquire()
                bTTb = bTT.bitcast(BF16)
                for hl in range(4):
                    P.op('pe', lambda e, hl=hl: e.transpose(out=bTTb[:, hl * 128:(hl + 1) * 128], in_=Tc[:, hl, :], identity=identB),
                         reads=[Tcr, 'identB'], writes=[bTTr], signal=(hl == 3))
                bA, bAr = yield from BANKS.acquire()
                for hl in range(4):
                    P.op('pe', lambda e, hl=hl: e.matmul(bA[:, hl * 128:(hl + 1) * 128], lhsT=NOT[:, hl, :], rhs=Tc[:, hl, :], start=True, stop=True),
                         reads=[NOTr, Tcr], writes=[bAr], signal=(hl == 3))
                yield
                DG, DGr = Pn, Pnr
                P.op('act', lambda e: e.activation(out=f2(DG), in_=bTTb[:, 0:512], func=AF.Copy), reads=[], writes=[bTTr, DGr])
                BANKS.release((bTT, bTTr))
                A1, A1r = PTn, PTnr
                P.op('act', lambda e: e.activation(out=f2(A1), in_=bA, func=AF.Copy), reads=[], writes=[bAr, A1r])
                BANKS.release((bA, bAr))
                yield
                bF, bFr = yield from BANKS.acquire()
                for hl in range(4):
                    P.op('pe', lambda e, hl=hl: e.matmul(bF[:, hl * 128:(hl + 1) * 128], lhsT=identB, rhs=Tc[:, hl, :], start=True, stop=False),
                         reads=[Tcr, 'identB'], writes=[bFr], signal=False)
                    P.op('pe', lambda e, hl=hl: e.matmul(bF[:, hl * 128:(hl + 1) * 128], lhsT=DG[:, hl, :], rhs=A1[:, hl, :], start=False, stop=True),
                         reads=[DGr, A1r], writes=[bFr], signal=(hl == 3))
                bTr_, bTr_r = yield from BANKS.acquire()
                bTb = bTr_.bitcast(BF16)
                for hl in range(4):
                    P.op('pe', lambda e, hl=hl: e.transpose(out=bTb[:, hl * 128:(hl + 1) * 128], in_=QKV[:, 8 + h0 + hl, tk], identity=identB),
                         reads=kres + ['identB'], writes=[bTr_r], signal=False)
                for hl in range(4):
                    P.op('pe', lambda e, hl=hl: e.transpose(out=bTb[:, 512 + hl * 128:512 + (hl + 1) * 128], in_=QKV[:, 16 + h0 + hl, tk], identity=identB),
                         reads=vres + ['identB'], writes=[bTr_r], signal=(hl == 3))
                yield
                P.op('act', lambda e: e.activation(out=f2(Tn), in_=bF, func=AF.Copy), reads=[], writes=[bFr, Tnr])
                BANKS.release((bF, bFr))
                Tc, Tcr, Tn, Tnr = Tn, Tnr, Tc, Tcr
                for it_ in ((Pc, Pcr), (PTc, PTcr), (Pn, Pnr), (PTn, PTnr), (NOT, NOTr), (Tn, Tnr)):
                    GP.release(it_)
                ktm = bTb[:, 0:512].rearrange("p (a b) -> p a b", a=4)
                vtm = bTb[:, 512:1024].rearrange("p (a b) -> p a b", a=4)
                (XK, XKr), (KDEC, KDECr), (BV, BVr), (WTN, WTNr), (VN, VNr) = yield from acquire_n(GP, 5)
                P.op('dve', lambda e: e.tensor_tensor(out=BV, in0=vtm, in1=bc_last(sm["BETA"][:, h0:h0 + 4], 128), op=ALU.mult), reads=[smr + 'BETA'], writes=[bTr_r, BVr])
                yield
                P.op('dve', lambda e: e.tensor_tensor(out=XK, in0=ktm, in1=bc_last(sm["BEG"][:, h0:h0 + 4], 128), op=ALU.mult), reads=[smr + 'BEG'], writes=[bTr_r, XKr])
                yield
                P.op('dve', lambda e: e.tensor_tensor(out=KDEC, in0=ktm, in1=bc_last(sm["EREM"][:, h0:h0 + 4], 128), op=ALU.mult), reads=[smr + 'EREM'], writes=[bTr_r, KDECr])
                BANKS.release((bTr_, bTr_r))
                bU, bUr = yield from BANKS.acquire()
                for hl in range(4):
                    P.op('pe', lambda e, hl=hl: e.matmul(bU[:, hl * 128:(hl + 1) * 128], lhsT=Tc[:, hl, :], rhs=BV[:, hl, :], start=True, stop=True),
                         reads=[Tcr, BVr], writes=[bUr], signal=(hl == 3))
                bW, bWr = yield from BANKS.acquire()
                for hl in range(4):
                    P.op('pe', lambda e, hl=hl: e.matmul(bW[:, hl * 128:(hl + 1) * 128], lhsT=XK[:, hl, :], rhs=Tc[:, hl, :], start=True, stop=True),
                         reads=[Tcr, XKr], writes=[bWr], signal=(hl == 3))
                yield
                UFf, UFr = yield from FPP.acquire()
                UF = v4(UFf)
                P.op('act', lambda e: e.activation(out=f2(UF), in_=bU, func=AF.Copy), reads=[], writes=[bUr, UFr])
                BANKS.release((bU, bUr))
                P.op('act', lambda e: e.activation(out=f2(WTN), in_=bW, func=AF.Copy, scale=-1.0), reads=[], writes=[bWr, WTNr])
                BANKS.release((bW, bWr))
                yield
                while i > 0 and not rec_done[hg][i - 1]:
                    yield
                sres = 'Sb%d' % hg
                bWS, bWSr = yield from BANKS.acquire()
                for hl in range(4):
                    P.op('pe', lambda e, hl=hl: e.matmul(bWS[:, hl * 128:(hl + 1) * 128], lhsT=WTN[:, hl, :], rhs=Sb[:, h0 + hl, :], start=True, stop=True),
                         reads=[WTNr, sres], writes=[bWSr], signal=(hl == 3))
                yield
                P.op('dve', lambda e: e.tensor_tensor(out=f2(VN), in0=bWS, in1=f2(UF), op=ALU.add), reads=[UFr], writes=[bWSr, VNr])
                BANKS.release((bWS, bWSr))
                yield
                bO, bOr = yield from BANKS.acquire()
                for hl in range(4):
                    P.op('pe', lambda e, hl=hl: e.matmul(bO[:, hl * 128:(hl + 1) * 128], lhsT=Sb[:, h0 + hl, :], rhs=QD[:, hl, :], start=True, stop=False),
                         reads=[sres, QDr], writes=[bOr], signal=False)
                    P.op('pe', lambda e, hl=hl: e.matmul(bO[:, hl * 128:(hl + 1) * 128], lhsT=VN[:, hl, :], rhs=QKT[:, hl, :], start=False, stop=True),
                         reads=[VNr, QKTr], writes=[bOr], signal=(hl == 3))
                bDS, bDSr = yield from BANKS.acquire()
                for hl in range(4):
                    P.op('pe', lambda e, hl=hl: e.matmul(bDS[:, hl * 128:(hl + 1) * 128], lhsT=KDEC[:, hl, :], rhs=VN[:, hl, :], start=True, stop=True),
                         reads=[KDECr, VNr], writes=[bDSr], signal=(hl == 3))
                S4 = S[:, h0:h0 + 4, :]
                srf = 'S%d' % hg
                yield
                for hl in range(4):
                    P.op('dve', lambda e, hl=hl: e.scalar_tensor_tensor(out=S[:, h0 + hl, :], in0=S[:, h0 + hl, :], scalar=sm["ETOT"][:, h0 + hl:h0 + hl + 1], in1=bDS[:, hl * 128:(hl + 1) * 128],
                                                                      op0=ALU.mult, op1=ALU.add),
                         reads=[srf, smr + 'ETOT'], writes=[bDSr, srf])
                BANKS.release((bDS, bDSr))
                sqo, sqor = yield from BPP.acquire()
                P.op('act', lambda e: e.activation(out=sqo, in_=bO, func=AF.Square), reads=[], writes=[bOr, sqor])
                yield
                P.op('act', lambda e: e.activation(out=Sb[:, h0:h0 + 4, :], in_=S4, func=AF.Copy), reads=[srf], writes=[sres])
                rec_done[hg][i] = True
                bN, bNr = yield from BANKS.acquire()
                P.op('pe', lambda e: e.matmul(bN, lhsT=onesB, rhs=sqo, start=True, stop=True), reads=[sqor, 'onesB'], writes=[bNr])
                BPP.release((sqo, sqor))
                yield
                rno, rnor = yield from FPP.acquire()
                P.op('act', lambda e: e.activation(out=rno, in_=bN, func=AF.Ln, scale=1.0 / 128, bias=EPS), reads=[], writes=[bNr, rnor])
                BANKS.release((bN, bNr))
                P.op('act', lambda e: e.activation(out=rno, in_=rno, func=AF.Exp, scale=-0.5), reads=[rnor], writes=[rnor])
                yield
                P.op('dve', lambda e: e.scalar_tensor_tensor(out=rno, in0=bO, scalar=par("gnw")[:, 0:1], in1=rno, op0=ALU.mult, op1=ALU.mult),
                     reads=[rnor, 'PAR'], writes=[bOr, rnor])
                BANKS.release((bO, bOr))
                yield
                P.op('dve', lambda e: e.tensor_tensor(out=MIX[:, h0:h0 + 4, tk], in0=v4(rno), in1=ZA[:, h0:h0 + 4, tk], op=ALU.mult),
                     reads=[rnor] + ['ZA%d' % (h0 + k) for k in range(4)], writes=['MIX%d' % (h0 + k) for k in range(4)])
                FPP.release((rno, rnor))
                FPP.release((UFf, UFr))
                for it_ in ((Tc, Tcr), (QD, QDr), (QKT, QKTr), (XK, XKr), (KDEC, KDECr), (BV, BVr), (WTN, WTNr), (VN, VNr)):
                    GP.release(it_)
                yield

            def ssd_chain(i):
                tk = slice(i * 128, (i + 1) * 128)
                sm = SMB[i % 2]
                smr = 'SMB%d' % (i % 2)
                bBC, bBCr = yield from BANKS.acquire()
                for gq in range(2):
                    P.op('pe', lambda e, gq=gq: e.matmul(bBC[:, gq * 128:(gq + 1) * 128], lhsT=XBC[:, 8 + gq, tk], rhs=XBC[:, 10 + gq, tk], start=True, stop=True),
                         reads=['XBC%d' % (8 + gq), 'XBC%d' % (10 + gq)], writes=[bBCr], signal=(gq == 1))
                bBT, bBTr = yield from BANKS.acquire()
                bBTb = bBT.bitcast(BF16)
                for gq in range(2):
                    P.op('pe', lambda e, gq=gq: e.transpose(out=bBTb[:, gq * 128:(gq + 1) * 128], in_=XBC[:, 8 + gq, tk], identity=identB),
                         reads=['XBC%d' % (8 + gq), 'identB'], writes=[bBTr], signal=(gq == 1))
                yield
                P.op('act', lambda e: e.activation(out=BCS, in_=bBC[:, 0:256], func=AF.Copy), reads=[], writes=[bBCr, 'BCS'])
                BANKS.release((bBC, bBCr))
                P.op('dve', lambda e: e.tensor_copy(out=BTM.rearrange("p a b -> p (a b)"), in_=bBTb[:, 0:256]), reads=[], writes=[bBTr, 'BTM'])
                BANKS.release((bBT, bBTr))
                yield
                for hq in range(4):
                    h0 = 4 * hq
                    gq = hq // 2
                    pi = hq % 2
                    acs4 = bc_last(sm["CSUM"][:, 8 + h0:8 + h0 + 4], 128)
                    bR3, bR3r = yield from BANKS.acquire()
                    for hl in range(4):
                        P.op('pe', lambda e, hl=hl: e.matmul(bR3[:, hl * 128:(hl + 1) * 128], lhsT=sm["CSUM"][:, 8 + h0 + hl:8 + h0 + hl + 1].to_broadcast([128, 128]), rhs=identF, start=True, stop=True),
                             reads=[smr + 'CSUM', 'CST'], writes=[bR3r], signal=(hl == 3))
                    bXT, bXTr = yield from BANKS.acquire()
                    bXTb = bXT.bitcast(BF16)
                    for k in range(2):
                        P.op('pe', lambda e, k=k: e.transpose(out=bXTb[:, k * 128:(k + 1) * 128], in_=XBC[:, 2 * hq + k, tk], identity=identB),
                             reads=['XBC%d' % (2 * hq + k), 'identB'], writes=[bXTr], signal=(k == 1))
                    yield
                    gms, gmsr = yield from FPP.acquire()
                    P.op('dve', lambda e: e.tensor_tensor(out=v4(gms), in0=v4(bR3), in1=bc_mid(mUi, 4), op=ALU.add), reads=['CST'], writes=[bR3r, gmsr])
                    yield
                    ES, ESr = yield from BPP.acquire()
                    for hl in range(4):
                        P.op('act', lambda e, hl=hl: e.activation(out=ES[:, hl * 128:(hl + 1) * 128], in_=gms[:, hl * 128:(hl + 1) * 128], func=AF.Exp, bias=sm["NCS"][:, 8 + h0 + hl:8 + h0 + hl + 1]),
                             reads=[gmsr, smr + 'NCS'], writes=[ESr])
                    FPP.release((gms, gmsr))
                    er3, er3r = yield from BPP.acquire()
                    P.op('act', lambda e: e.activation(out=er3, in_=bR3, func=AF.Exp), reads=[], writes=[bR3r, er3r])
                    BANKS.release((bR3, bR3r))
                    xtm = bXTb[:, 0:256].rearrange("p (a b) -> p a b", a=4)
                    XDT, XDTr = XDTb[pi], 'XDT%d' % pi
                    XDD, XDDr = XDDb[pi], 'XDD%d' % pi
                    P.op('dve', lambda e: e.tensor_tensor(out=XDT, in0=xtm, in1=bc_last(sm["SPL"][:, 8 + h0:8 + h0 + 4], 64), op=ALU.mult), reads=[smr + 'SPL'], writes=[bXTr, XDTr])
                    yield
                    P.op('dve', lambda e: e.tensor_tensor(out=XDD, in0=xtm, in1=bc_last(sm["DTE"][:, h0:h0 + 4], 64), op=ALU.mult), reads=[smr + 'DTE'], writes=[bXTr, XDDr])
                    BANKS.release((bXT, bXTr))
                    GT, GTr = GTb[pi], 'GT%d' % pi
                    P.op('dve', lambda e: e.tensor_tensor(out=GT, in0=v4(ES), in1=bc_mid(BCS[:, gq * 128:(gq + 1) * 128], 4), op=ALU.mult),
                         reads=[ESr, 'BCS'], writes=[GTr])
                    BPP.release((ES, ESr))
                    yield
                    CD, CDr = CDECb[pi], 'CDEC%d' % pi
                    P.op('dve', lambda e: e.tensor_tensor(out=CD, in0=v4(er3), in1=bc_mid(XBC[:, 10 + gq, tk], 4), op=ALU.mult),
                         reads=[er3r, 'XBC%d' % (10 + gq)], writes=[CDr])
                    BPP.release((er3, er3r))
                    yield
                    ssr = 'SSb%d' % hq
                    bY, bYr = yield from BANKS.acquire()
                    for hl in range(4):
                        pr = hl // 2
                        hf = hl % 2
                        o_ = bY[hf * 64:(hf + 1) * 64, pr * 128:(pr + 1) * 128]
                        P.op('pe', lambda e, hl=hl, o_=o_, hf=hf: e.matmul(o_, lhsT=XDT[:, hl, :], rhs=GT[:, hl, :], start=True, stop=False, tile_position=(0, 64 * hf)),
                             reads=[XDTr, GTr], writes=[bYr], signal=False)
                        P.op('pe', lambda e, hl=hl, o_=o_, hf=hf: e.matmul(o_, lhsT=SSb[:, h0 + hl, :], rhs=CD[:, hl, :], start=False, stop=True, tile_position=(0, 64 * hf)),
                             reads=[ssr, CDr], writes=[bYr], signal=(hl == 3))
                    bDSS, bDSSr = yield from BANKS.acquire()
                    P.op('pe', lambda e: e.matmul(bDSS[:, 0:256], lhsT=BTM[:, gq, :], rhs=XDD.rearrange("p a b -> p (a b)"), start=True, stop=True),
                         reads=['BTM', XDDr], writes=[bDSSr])
                    SS4 = SS[:, h0:h0 + 4, :]
                    ssf = 'SS%d' % hq
                    yield
                    for hl in range(4):
                        P.op('dve', lambda e, hl=hl: e.scalar_tensor_tensor(out=SS[:, h0 + hl, :], in0=SS[:, h0 + hl, :], scalar=sm["ETOT"][:, 8 + h0 + hl:8 + h0 + hl + 1], in1=bDSS[:, hl * 64:(hl + 1) * 64],
                                                                          op0=ALU.mult, op1=ALU.add),
                             reads=[ssf, smr + 'ETOT'], writes=[bDSSr, ssf])
                    BANKS.release((bDSS, bDSSr))
                    yield
                    P.op('act', lambda e: e.activation(out=SSb[:, h0:h0 + 4, :], in_=SS4, func=AF.Copy), reads=[ssf], writes=[ssr])
                    ys, ysr = yield from FPP.acquire()
                    for pr in range(2):
                        xt = 2 * hq + pr
                        P.op('dve', lambda e, pr=pr, xt=xt: e.scalar_tensor_tensor(out=ys[:, pr * 128:(pr + 1) * 128], in0=XBC[:, xt, tk], scalar=par("dexp")[:, xt:xt + 1],
                                                                               in1=bY[:, pr * 128:(pr + 1) * 128], op0=ALU.mult, op1=ALU.add),
                             reads=['XBC%d' % xt, 'PAR'], writes=[bYr, ysr])
                    BANKS.release((bY, bYr))
                    yield
                    P.op('dve', lambda e: e.tensor_tensor(out=YG[:, 2 * pi:2 * pi + 2, :], in0=ys[:, 0:256].rearrange("p (a b) -> p a b", a=2), in1=ZS[:, 2 * hq:2 * hq + 2, tk], op=ALU.mult),
                         reads=[ysr, 'ZS%d' % (2 * hq), 'ZS%d' % (2 * hq + 1)], writes=['YG%d' % pi])
                    FPP.release((ys, ysr))
                    yield
                    if pi == 1:
                        ygr = ['YG0', 'YG1']
                        sqy, sqyr = yield from BPP.acquire()
                        P.op('act', lambda e: e.activation(out=sqy, in_=YG.rearrange("p a b -> p (a b)"), func=AF.Square), reads=ygr, writes=[sqyr])
                        yield
                        bNS, bNSr = yield from BANKS.acquire()
                        for k in range(4):
                            P.op('pe', lambda e, k=k: e.matmul(bNS[:, 0:128], lhsT=onesB, rhs=sqy[:, k * 128:(k + 1) * 128], start=(k == 0), stop=(k == 3)),
                                 reads=[sqyr, 'onesB'], writes=[bNSr], signal=(k == 3))
                        BPP.release((sqy, sqyr))
                        yield
                        rns, rnsr = yield from FPP.acquire()
                        P.op('act', lambda e: e.activation(out=rns[:, 0:128], in_=bNS[:, 0:128], func=AF.Ln, scale=1.0 / 512, bias=EPS), reads=[], writes=[bNSr, rnsr])
                        BANKS.release((bNS, bNSr))
                        P.op('act', lambda e: e.activation(out=rns[:, 0:128], in_=rns[:, 0:128], func=AF.Exp, scale=-0.5), reads=[rnsr], writes=[rnsr])
                        yield
                        for k in range(4):
                            xt = 4 * gq + k
                            P.op('dve', lambda e, xt=xt, k=k: e.scalar_tensor_tensor(out=MIX[:, 8 + xt, tk], in0=YG[:, k, :], scalar=par("snw")[:, xt:xt + 1], in1=rns[:, 0:128],
                                                                                op0=ALU.mult, op1=ALU.mult),
                                 reads=ygr + [rnsr, 'PAR'], writes=['MIX%d' % (8 + xt)])
                        FPP.release((rns, rnsr))
                        yield

            run_chains([gates_all(), gdn_worker(), gdn_worker(), gdn_worker(), ssd_all()])
            dump("MIX", MIX, mixres)
            BANKS.flush()
            FPP.flush()
            BPP.flush()
            P.retire(regA_mix, regA_ffn)
            mores = ['MO%d' % j for j in range(8)]

            def proj_norm_residual(kind, npiece, nkc, src3, srcres, wname, sqbuf, sqname):
                bst, bstr = take_bank()
                for j in range(8):
                    bk, bkr = take_bank()
                    for q in range(npiece):
                        wv, wr = w_get(kind, j, q)
                        for k8 in range(min(8, nkc - 8 * q)):
                            kc = q * 8 + k8
                            P.op('pe', lambda e, kc=kc, k8=k8: e.matmul(bk[:, 0:TB], lhsT=wv[:, k8, :], rhs=src3[:, kc, :], start=(kc == 0), stop=(kc == nkc - 1)),
                                 reads=[wr] + srcres, writes=[bkr], signal=(kc == nkc - 1))
                    if j & 1:
                        P.op('act', lambda e: e.activation(out=MO[:, j, :], in_=bk[:, 0:TB], func=AF.Identity, scale=par(wname)[:, j:j + 1]),
                             reads=['PAR'], writes=[bkr, 'MO%d' % j])
                    else:
                        P.op('dve', lambda e: e.tensor_scalar_mul(out=MO[:, j, :], in0=bk[:, 0:TB], scalar1=par(wname)[:, j:j + 1]),
                             reads=['PAR'], writes=[bkr, 'MO%d' % j])
                    rms_sq_step(bst, bstr, j, bk[:, 0:TB], [], src_is_psum=bkr, sqbuf=sqbuf, sqname=sqname)
                    BANKS.release((bk, bkr))
                rms_finish(bst, bstr, RSTD, 'RSTD', D)

            proj_norm_residual('out', 2, 16, MIX, mixres, "w2", ACTB, 'ACTB')
            bst, bstr = take_bank()
            for kc in range(8):
                P.op('dve', lambda e, kc=kc: e.tensor_tensor(out=MO[:, kc, :], in0=MO[:, kc, :], in1=RSTD, op=ALU.mult),
                     reads=['MO%d' % kc, 'RSTD'], writes=['MO%d' % kc])
                P.op('pool', lambda e, kc=kc: e.tensor_tensor(out=X[:, kc, :], in0=X[:, kc, :], in1=MO[:, kc, :], op=ALU.add), reads=[xres[kc], 'MO%d' % kc], writes=[xres[kc]])
                rms_sq_step(bst, bstr, kc, X[:, kc, :], [xres[kc]])
            dump("X1", X, xres)
            rstd2, rstd2r = ftmp()
            rms_finish(bst, bstr, rstd2, rstd2r, D)
            for kc in range(8):
                P.op('dve', lambda e, kc=kc: e.scalar_tensor_tensor(out=H[:, kc, :], in0=X[:, kc, :], scalar=par("w3")[:, kc:kc + 1], in1=rstd2[:, 0:TB],
                                                                    op0=ALU.mult, op1=ALU.mult),
                     reads=[xres[kc], rstd2r, 'PAR'], writes=['H%d' % kc])
            BANKS.flush()
            FPP.flush()
            BPP.flush()
            active = []
            for jj in range(22):
                for half in range(2):
                    j = jj + 22 * half
                    wv, wr = w_get('up', j)
                    bk, bkr = take_bank(active)
                    for kc in range(8):
                        P.op('pe', lambda e, kc=kc: e.matmul(bk[:, 0:TB], lhsT=wv[:, kc, :], rhs=H[:, kc, :], start=(kc == 0), stop=(kc == 7)),
                             reads=[wr] + hres, writes=[bkr], signal=(kc == 7))
                    cwj = par("cwf")[:, 3 * j:3 * j + 3]
                    if half == 0:
                        fin = lambda acc, accr, jj=jj: P.op('act', lambda e: e.activation(out=ACTB[:, jj, :], in_=acc[:, 0:TB], func=AF.Silu), reads=[accr], writes=['ACTB%d' % jj])
                        npl = 0
                    else:
                        fin = lambda acc, accr, jj=jj: P.op('dve', lambda e: e.tensor_tensor(out=ACTB[:, jj, :], in0=ACTB[:, jj, :], in1=acc[:, 0:TB], op=ALU.mult),
                                                            reads=[accr, 'ACTB%d' % jj], writes=['ACTB%d' % jj])
                        npl = 0
                    active.append(conv_chain(bk, bkr, cwj, 3, par("cbf")[:, j:j + 1], TAILF[:, j, :], 'TF%d' % j, first_blk, TB, fin, npl, tiny))
                    step_active(active)
            run_chains(active)
            actres = ['ACTB%d' % j for j in range(22)]
            BANKS.flush()
            nb_ = b_ + 1
            ns_ = s_
            if nb_ == NBLK:
                nb_, ns_ = 0, s_ + 1
            if ns_ < NSEQ:
                load_x(ns_, nb_)
            proj_norm_residual('dn', 3, KC_DN, ACTB, actres, "w4", None, 'MIX')
            for kc in range(8):
                P.op('dve', lambda e, kc=kc: e.tensor_tensor(out=MO[:, kc, :], in0=MO[:, kc, :], in1=RSTD, op=ALU.mult),
                     reads=['MO%d' % kc, 'RSTD'], writes=['MO%d' % kc])
                P.op('pool', lambda e, kc=kc: e.tensor_tensor(out=MO[:, kc, :], in0=MO[:, kc, :], in1=X[:, kc, :], op=ALU.add), reads=[xres[kc], 'MO%d' % kc], writes=['MO%d' % kc])
            P.dma('sp', outT[s_, 0:4, :, t0b:t0b + TB].rearrange("k p t -> p k t"), MO[:, 0:4, :], reads=mores[0:4], semname='st_outa')
            P.dma('sp', outT[s_, 4:8, :, t0b:t0b + TB].rearrange("k p t -> p k t"), MO[:, 4:8, :], reads=mores[4:8], semname='st_outb')
    P.finish('sp')
    return nc, P, dbg_outs


def _tile_w(w, ncol_tiles):
    K, N = w.shape
    kc = K // 128
    t = w.reshape(kc, 128, ncol_tiles, 128).transpose(2, 1, 0, 3)
    return np.ascontiguousarray(t).reshape(ncol_tiles, 128, kc * 128)


def make_consts():
    i = np.arange(128)
    c = np.zeros((128, 8, 128), np.float32)
    c[:, 0, :] = np.eye(128)
    c[:, 1, :] = (i[:, None] <= i[None, :])
    c[:, 2, :] = 1.0
    c[:, 3, :] = np.where(i[None, :] < i[:, None], 0.0, -BIG)
    c[:, 4, :] = np.where(i[None, :] > i[:, None], 0.0, -BIG)
    c[:, 5, :] = np.where(i[None, :] >= i[:, None], 0.0, -BIG)
    c[:, 6, :] = ((i[:, None] // 64) == (i[None, :] // 64))
    c[:, 7, :] = ((i[:, None] >= 64) & (i[None, :] < 64))
    return c


def prep_shared(inp):
    g = lambda n: np.asarray(inp[n], dtype=np.float32)[0]
    w_in = g("w_in")
    offs = np.cumsum([0, 1024, 1024, 1024, 1024, 8, 8, 1024, 1024, 256, 256, 16])
    sl = lambda k: w_in[:, offs[k]:offs[k + 1]]
    main = np.concatenate([sl(0), sl(1), sl(2), sl(3), sl(6), sl(7), sl(8), sl(9)], axis=1)
    small = np.concatenate([sl(4), sl(5), sl(10)], axis=1)
    w_in_t = _tile_w(main, NJ_IN)
    w_small = np.ascontiguousarray(small.reshape(8, 128, 32).transpose(1, 0, 2)).reshape(128, 256)
    w_out_t = _tile_w(g("w_out"), 8)
    w_up_t = _tile_w(g("w_up"), NJ_UP)
    w_dn_t = _tile_w(g("w_down"), 8)
    par = np.zeros((128, NPAR), np.float32)

    def put(name, arr):
        a, b = PO[name]
        par[:, a:b] = arr.reshape(128, b - a)

    pp = lambda v: np.ascontiguousarray(v.reshape(-1, 128).T)
    put("w1", pp(g("pre_mix_norm")))
    put("w2", pp(g("post_mix_norm")))
    put("w3", pp(g("pre_ffn_norm")))
    put("w4", pp(g("post_ffn_norm")))
    put("gnw", g("gdn_norm_w").reshape(128, 1))
    put("snw", pp(g("ssd_norm_w")))
    put("dexp", pp(np.repeat(g("ssd_d"), 64)))
    put("cwg", np.ascontiguousarray(g("gdn_conv_w").reshape(4, 24, 128).transpose(2, 1, 0)))
    put("cws", np.ascontiguousarray(g("ssd_conv_w").reshape(4, 12, 128).transpose(2, 1, 0)))
    put("cbs", pp(g("ssd_conv_b")))
    put("cwf", np.ascontiguousarray(g("ffn_conv_w").reshape(3, 44, 128).transpose(2, 1, 0)))
    put("cbf", pp(g("ffn_conv_b")))
    put("alog", np.broadcast_to(np.concatenate([g("gdn_a_log"), g("ssd_a_log")])[None, :], (128, 24)))
    put("dtb", np.broadcast_to(np.concatenate([g("gdn_dt_bias"), g("ssd_dt_bias")])[None, :], (128, 24)))
    return {"params": par, "consts": make_consts(), "w_in_t": w_in_t, "w_small": w_small,
            "w_out_t": w_out_t, "w_up_t": w_up_t, "w_dn_t": w_dn_t}


def x_to_dev(xs):
    n, s, _ = xs.shape
    return np.ascontiguousarray(xs.transpose(0, 2, 1)).reshape(n, 8, 128, s)


def out_from_dev(o):
    n, _, _, s = o.shape
    return np.ascontiguousarray(o.reshape(n, 1024, s).transpose(0, 2, 1))


_CACHE = {}


def kernel(**inputs):
    x = np.asarray(inputs["x"], dtype=np.float32)
    B, SEQ, _ = x.shape
    nseq = B // NCORES
    key = (nseq, SEQ)
    if key not in _CACHE:
        _CACHE[key] = build(nseq, SEQ, 512)[0]
    nc = _CACHE[key]
    shared = prep_shared(inputs)
    in_maps = []
    for c in range(NCORES):
        m = dict(shared)
        m["xT"] = x_to_dev(x[c * nseq:(c + 1) * nseq])
        in_maps.append(m)
    res = run_bass_kernel_spmd(nc, in_maps, core_ids=list(range(NCORES)))
    outs = [out_from_dev(np.asarray(r["outT"])) for r in res.results]
    return np.concatenate(outs, axis=0).astype(np.float32)
```

```python
import numpy as np
import concourse.bass as bass
import concourse.mybir as mybir
from concourse.bass_utils import run_bass_kernel_spmd

F32 = mybir.dt.float32
BF16 = mybir.dt.bfloat16
AF = mybir.ActivationFunctionType
ALU = mybir.AluOpType

D = 1024
NH = 8
SH = 16
DFF = 2816
NJ_IN = 52
NJ_UP = 44
KC_DN = 22
EPS = 1e-6
BIG = 30000.0
NCORES = 8

PO = {}
_o = 0
for _n, _w in [("w1", 8), ("w2", 8), ("w3", 8), ("w4", 8), ("gnw", 1), ("snw", 8), ("dexp", 8),
               ("cwg", 96), ("cws", 48), ("cbs", 12), ("cwf", 132), ("cbf", 44), ("alog", 24), ("dtb", 24)]:
    PO[_n] = (_o, _o + _w)
    _o += _w
NPAR = _o


class Prog:
    def __init__(self, nc):
        self.nc = nc
        self.eng = {'pe': nc.tensor, 'act': nc.scalar, 'dve': nc.vector, 'pool': nc.gpsimd, 'sp': nc.sync}
        self.sem = {e: nc.alloc_semaphore('sem_' + e) for e in self.eng}
        self.cnt = {e: 0 for e in self.eng}
        self.waited = {e: {} for e in self.eng}
        self.lastw = {}
        self.readers = {}
        self.dsem = {}
        self.ninst = 0
        self.nwait = 0
        self.rr = 0

    def _wait(self, e, dep):
        key, h, v, src = dep
        if self.waited[e].get(key, 0) >= v:
            return
        if src == e and v > self.cnt[e]:
            return
        self.eng[e].wait_ge(h, v)
        self.nwait += 1
        self.waited[e][key] = v

    def _deps(self, e, reads, writes):
        deps = []
        for r in reads:
            d = self.lastw.get(r)
            if d is not None:
                deps.append(d)
        for w in writes:
            d = self.lastw.get(w)
            if d is not None:
                deps.append(d)
            rd = self.readers.get(w)
            if rd:
                for k, d in rd.items():
                    deps.append(d)
        for d in deps:
            self._wait(e, d)

    def op(self, e, fn, reads=(), writes=(), signal=True):
        self._deps(e, reads, writes)
        ins = fn(self.eng[e])
        self.ninst += 1
        if signal:
            ins.then_inc(self.sem[e], 1)
            self.cnt[e] += 1
            v = self.cnt[e]
        else:
            v = self.cnt[e] + 1
        dep = ('e_' + e, self.sem[e], v, e)
        for w in writes:
            self.lastw[w] = dep
            self.readers[w] = {}
        for r in reads:
            if r not in writes:
                self.readers.setdefault(r, {})[e] = dep
        return ins

    def dma(self, q, out, in_, reads=(), writes=(), semname=None, **kw):
        self._deps(q, reads, writes)
        if semname is None:
            semname = 'd_' + (writes[0] if writes else reads[0])
        if semname not in self.dsem:
            self.dsem[semname] = [self.nc.alloc_semaphore(semname), 0]
        s = self.dsem[semname]
        ins = self.eng[q].dma_start(out=out, in_=in_, **kw)
        ins.then_inc(s[0], 16)
        s[1] += 16
        dep = (semname, s[0], s[1], None)
        for w in writes:
            self.lastw[w] = dep
            self.readers[w] = {}
        for r in reads:
            self.readers.setdefault(r, {})[semname] = dep
        return ins

    def retire(self, old, new):
        deps = []
        for o in old:
            d = self.lastw.get(o)
            if d is not None:
                deps.append(d)
            for k, d in self.readers.get(o, {}).items():
                deps.append(d)
        best = {}
        for d in deps:
            if d[0] not in best or best[d[0]][2] < d[2]:
                best[d[0]] = d
        for n in new:
            rd = self.readers.setdefault(n, {})
            for k, d in best.items():
                rd['al_' + k] = (d[0], d[1], d[2], 'alias')

    def finish(self, e='sp'):
        for name, (h, v) in self.dsem.items():
            self._wait(e, (name, h, v, None))
        for f in self.eng:
            if f != e and self.cnt[f] > 0:
                self._wait(e, ('e_' + f, self.sem[f], self.cnt[f], f))

    def ew(self):
        self.rr += 1
        return 'dve' if (self.rr & 1) else 'pool'


def bc_mid(ap, n):
    return ap.unsqueeze(1).to_broadcast([ap.shape[0], n, ap.shape[1]])


def bc_last(ap, n):
    return ap.unsqueeze(2).to_broadcast([ap.shape[0], ap.shape[1], n])


def build(NSEQ, SEQ, TB, debug=()):
    nc = bass.Bass("TRN2", target_bir_lowering=False)
    NBLK = SEQ // TB
    NT = TB // 128
    dt_ = nc.dram_tensor
    xT = dt_("xT", [NSEQ, 8, 128, SEQ], F32, kind="ExternalInput").ap()
    params_d = dt_("params", [128, NPAR], F32, kind="ExternalInput").ap()
    consts_d = dt_("consts", [128, 8, 128], F32, kind="ExternalInput").ap()
    w_in_d = dt_("w_in_t", [NJ_IN, 128, 1024], F32, kind="ExternalInput").ap()
    w_sm_d = dt_("w_small", [128, 256], F32, kind="ExternalInput").ap()
    w_out_d = dt_("w_out_t", [8, 128, 2048], F32, kind="ExternalInput").ap()
    w_up_d = dt_("w_up_t", [NJ_UP, 128, 1024], F32, kind="ExternalInput").ap()
    w_dn_d = dt_("w_dn_t", [8, 128, KC_DN * 128], F32, kind="ExternalInput").ap()
    outT = dt_("outT", [NSEQ, 8, 128, SEQ], F32, kind="ExternalOutput").ap()
    sc_in = dt_("sc_in", [NJ_IN, 128, 1024], BF16, kind="Internal").ap()
    sc_out = dt_("sc_out", [8, 128, 2048], BF16, kind="Internal").ap()
    sc_up = dt_("sc_up", [NJ_UP, 128, 1024], BF16, kind="Internal").ap()
    sc_dn = dt_("sc_dn", [8, 128, KC_DN * 128], BF16, kind="Internal").ap()

    P = Prog(nc)
    dbg_outs = {}

    def sb(name, shape, dt=F32):
        return nc.alloc_sbuf_tensor(name, shape, dt).ap()

    def dump(name, ap, res):
        if name not in debug:
            return
        shp = list(ap.shape)
        key = "dbg_" + name
        k = dbg_outs.get(key, 0)
        dbg_outs[key] = k + 1
        t = dt_("%s_%d" % (key, k), shp, ap.dtype, kind="ExternalOutput").ap()
        P.dma('sp', t, ap, reads=res, semname='dbgsem')

    PAR = sb("PAR", [128, NPAR])
    CST = sb("CST", [128, 8, 128])
    identF = CST[:, 0, :]
    mcumF = CST[:, 1, :]
    onesF = CST[:, 2, :]
    mL = CST[:, 3, :]
    mUs = CST[:, 4, :]
    mUi = CST[:, 5, :]
    mBD = CST[:, 6, :]
    mOFF = CST[:, 7, :]
    identB = sb("identB", [128, 128], BF16)
    onesB = sb("onesB", [128, 128], BF16)
    WSM = sb("WSM", [128, 8, 32], BF16)
    NEGA = sb("NEGA", [128, 24])
    X = sb("X", [128, 8, TB])
    H = sb("H", [128, 8, TB], BF16)
    RSTD = sb("RSTD", [128, TB])
    REGA = sb("REGA", [128, 52 * 512], BF16)
    QKV = REGA[:, 0:24 * TB].rearrange("p (a b) -> p a b", a=24)
    XBC = REGA[:, 24 * 512:24 * 512 + 12 * TB].rearrange("p (a b) -> p a b", a=12)
    ZA = REGA[:, 36 * 512:36 * 512 + 8 * TB].rearrange("p (a b) -> p a b", a=8)
    ZS = REGA[:, 44 * 512:44 * 512 + 8 * TB].rearrange("p (a b) -> p a b", a=8)
    ACTB = REGA[:, 0:22 * TB].rearrange("p (a b) -> p a b", a=22)
    MO = REGA[:, 22 * 512:22 * 512 + 16 * TB].bitcast(F32).rearrange("p (a b) -> p a b", a=8)
    MIX = sb("MIX", [128, 16, TB], BF16)
    SQ = MIX[:, 0:8, :]
    STGall = sb("STGall", [128, 4, TB + 3])
    ACCall = sb("ACCall", [128, 4, TB])
    STG = [STGall[:, i, :] for i in range(4)]
    ACC = [ACCall[:, i, :] for i in range(4)]
    NW = 7
    WS = [sb("WS%d" % i, [128, 1024], BF16) for i in range(NW)]
    TAILG = sb("TAILG", [128, 36, 3])
    TAILF = sb("TAILF", [128, 44, 2])
    SMALL = sb("SMALL", [128, NT, 32])
    S = sb("S", [128, 8, 128])
    Sb = sb("Sb", [128, 8, 128], BF16)
    SS = sb("SS", [128, 16, 64])
    SSb = sb("SSb", [128, 16, 64], BF16)
    import collections

    class RPool:
        def __init__(self, items, lag):
            self.free = collections.deque(items)
            self.recent = collections.deque()
            self.lag = lag

        def get(self):
            while len(self.recent) >= self.lag:
                self.free.append(self.recent.popleft())
            it = self.free.popleft()
            self.recent.append(it)
            return it

        def flush(self):
            while self.recent:
                self.free.append(self.recent.popleft())

        def acquire(self):
            while not self.free:
                yield
            return self.free.popleft()

        def release(self, it):
            self.free.append(it)

    def run_chains(chains):
        chains = list(chains)
        guard = 0
        while chains:
            n0 = P.ninst
            for c in list(chains):
                try:
                    next(c)
                except StopIteration:
                    chains.remove(c)
            guard = guard + 1 if P.ninst == n0 else 0
            assert guard < 1000, "chain deadlock"

    NF = 7
    FPP = RPool([(sb("FP%d" % i, [128, 512]), 'FP%d' % i) for i in range(NF)], 2)

    def ftmp():
        return FPP.get()

    WSMF = FPP.free[0][0][:, 0:256].rearrange("p (a b) -> p a b", a=8)

    NB = 8
    BPP = RPool([(sb("BP%d" % i, [128, 512], BF16), 'BP%d' % i) for i in range(NB)], 2)

    def btmp():
        return BPP.get()

    GP = RPool([(sb("GP%d" % i, [128, 4, 128], BF16), "GP%d" % i) for i in range(27)], 99)

    def acquire_n(pool, n):
        while len(pool.free) < n:
            yield
        return [pool.free.popleft() for _ in range(n)]

    BTM = sb("BTM", [128, 2, 128], BF16)
    GTb = [sb("GT%d" % i, [128, 4, 128], BF16) for i in range(2)]
    CDECb = [sb("CDEC%d" % i, [128, 4, 128], BF16) for i in range(2)]
    XDTb = [sb("XDT%d" % i, [128, 4, 64], BF16) for i in range(2)]
    XDDb = [sb("XDD%d" % i, [128, 4, 64], BF16) for i in range(2)]
    YG = sb("YG", [128, 4, 128])
    BCS = sb("BCS", [128, 256])
    SMB = [{n_: sb("sm%d%s" % (i, n_), [128, w_]) for n_, w_ in
            [("NLB", 8), ("BETA", 8), ("U", 24), ("SPL", 24), ("V", 24), ("CSUM", 24), ("TOT", 24),
             ("DIFF", 24), ("NCS", 24), ("ECS", 24), ("EREM", 24), ("ETOT", 24), ("GB", 8), ("BEG", 8), ("DTE", 16)]}
           for i in range(2)]
    BANKS = RPool([(nc.alloc_psum_tensor("ps%d" % i, [128, 512], F32).ap(), 'ps%d' % i) for i in range(8)], 5)

    def bank():
        return BANKS.get()

    def par(name):
        a, b = PO[name]
        return PAR[:, a:b]

    P.dma('sp', PAR, params_d, writes=['PAR'])
    P.dma('sp', CST, consts_d, writes=['CST'])
    P.dma('sp', FPP.free[0][0][:, 0:256], w_sm_d, writes=['FP0'])
    for j in range(NJ_IN):
        P.dma('pool', sc_in[j], w_in_d[j], writes=['sc_in_g%d' % (j // 13)], semname='c_in%d' % (j // 13))
    for j in range(8):
        P.dma('pool', sc_out[j], w_out_d[j], writes=['sc_out'], semname='c_out')

    def emit_rest_casts():
        for j in range(NJ_UP):
            P.dma('pool', sc_up[j], w_up_d[j], writes=['sc_up'], semname='c_up')
        for j in range(8):
            P.dma('pool', sc_dn[j], w_dn_d[j], writes=['sc_dn'], semname='c_dn')

    P.op('dve', lambda e: e.tensor_copy(out=identB, in_=identF), reads=['CST'], writes=['identB'])
    P.op('dve', lambda e: e.tensor_copy(out=onesB, in_=onesF), reads=['CST'], writes=['onesB'])
    P.op('dve', lambda e: e.tensor_copy(out=WSM, in_=WSMF), reads=['FP0'], writes=['WSM'])
    P.op('act', lambda e: e.activation(out=NEGA, in_=par("alog"), func=AF.Exp), reads=['PAR'], writes=['NEGA'])
    P.op('dve', lambda e: e.tensor_scalar_mul(out=NEGA, in0=NEGA, scalar1=-1.0), reads=['NEGA'], writes=['NEGA'])

    wreq = []
    for s_ in range(NSEQ):
        for b_ in range(NBLK):
            wreq += [('in', j, 0) for j in range(NJ_IN)]
            wreq += [('out', j, q) for j in range(8) for q in range(2)]
            for jj in range(22):
                wreq += [('up', jj, 0), ('up', 22 + jj, 0)]
            wreq += [('dn', j, q) for j in range(8) for q in range(3)]
    wstate = {'issued': 0, 'next': 0}

    def w_issue(upto):
        while wstate['issued'] < min(upto, len(wreq)):
            k = wstate['issued']
            kind, j, q = wreq[k]
            slot = k % NW
            if kind == 'in':
                P.dma('sp', WS[slot], sc_in[j], reads=['sc_in_g%d' % (j // 13)], writes=['WS%d' % slot])
            elif kind == 'up':
                P.dma('sp', WS[slot], sc_up[j], reads=['sc_up'], writes=['WS%d' % slot])
            elif kind == 'out':
                P.dma('sp', WS[slot], sc_out[j][:, q * 1024:(q + 1) * 1024], reads=['sc_out'], writes=['WS%d' % slot])
            else:
                n = 1024 if q < 2 else (KC_DN * 128 - 2048)
                P.dma('sp', WS[slot][:, 0:n], sc_dn[j][:, q * 1024:q * 1024 + n], reads=['sc_dn'], writes=['WS%d' % slot])
            wstate['issued'] += 1

    def w_get(kind, j, q=0):
        k = wstate['next']
        assert wreq[k] == (kind, j, q), (wreq[k], kind, j, q)
        w_issue(k + NW)
        wstate['next'] += 1
        slot = k % NW
        return WS[slot].rearrange("p (a b) -> p a b", a=8), 'WS%d' % slot

    def rms_sq_step(bst, bstr, kc, src_ap, srcres, nk=8, src_is_psum=None, sqbuf=None, sqname='MIX'):
        sq_ = SQ if sqbuf is None else sqbuf
        rn_ = '%s%d' % (sqname, kc)
        w_ = [rn_] + ([src_is_psum] if src_is_psum else [])
        P.op('act', lambda e: e.activation(out=sq_[:, kc, :], in_=src_ap, func=AF.Square), reads=srcres, writes=w_)
        P.op('pe', lambda e: e.matmul(bst[:, 0:TB], lhsT=onesB, rhs=sq_[:, kc, :], start=(kc == 0), stop=(kc == nk - 1)),
             reads=[rn_, 'onesB'], writes=[bstr], signal=(kc == nk - 1))

    def rms_finish(bst, bstr, rstd, rstdr, nfeat):
        P.op('act', lambda e: e.activation(out=rstd[:, 0:TB], in_=bst[:, 0:TB], func=AF.Ln, scale=1.0 / nfeat, bias=EPS),
             reads=[], writes=[bstr, rstdr])
        BANKS.release((bst, bstr))
        P.op('act', lambda e: e.activation(out=rstd[:, 0:TB], in_=rstd[:, 0:TB], func=AF.Exp, scale=-0.5),
             reads=[rstdr], writes=[rstdr])

    mixres = ['MIX%d' % j for j in range(16)]
    sqres_all = mixres[0:8]
    regA_mix = (['QKV%d_%d' % (j, i) for j in range(24) for i in range(NT)] + ['XBC%d' % j for j in range(12)] +
                ['ZA%d' % j for j in range(8)] + ['ZS%d' % j for j in range(8)])
    regA_ffn = ['ACTB%d' % j for j in range(22)] + ['MO%d' % j for j in range(8)]

    STGP = RPool([(STG[i], ACC[i], 'STG%d' % i, 'ACC%d' % i) for i in range(len(STG))], 99)

    def conv_chain(bk, bkr, cw, ntap, bias_ap, tail, tres, first_blk, T, final, npool, tiny):
        h_ = ntap - 1
        slot = yield from STGP.acquire()
        stg, acc, stgr, accr = slot
        if bias_ap is None:
            P.op('act', lambda e: e.activation(out=acc[:, 0:T], in_=bk[:, 0:T], func=AF.Identity, scale=cw[:, h_:h_ + 1]),
                 reads=['PAR'], writes=[bkr, accr])
        else:
            P.op('act', lambda e: e.activation(out=acc[:, 0:T], in_=bk[:, 0:T], func=AF.Identity, scale=cw[:, h_:h_ + 1], bias=bias_ap),
                 reads=['PAR'], writes=[bkr, accr])
        P.op('act', lambda e: e.activation(out=stg[:, h_:h_ + T], in_=bk[:, 0:T], func=AF.Copy), reads=[], writes=[bkr, stgr])
        BANKS.release((bk, bkr))
        if first_blk:
            P.op(tiny, lambda e: e.memset(stg[:, 0:h_], 0.0), reads=[], writes=[stgr])
        else:
            P.op(tiny, lambda e: e.tensor_copy(out=stg[:, 0:h_], in_=tail), reads=[tres], writes=[stgr])
        yield
        ptmp = []
        for k in range(h_ - npool, h_):
            tmp, tmpr = yield from FPP.acquire()
            P.op('pool', lambda e, k=k, tmp=tmp: e.tensor_tensor(out=tmp[:, 0:T], in0=stg[:, k:k + T], in1=cw[:, k:k + 1].to_broadcast([128, T]), op=ALU.mult),
                 reads=[stgr, 'PAR'], writes=[tmpr])
            ptmp.append((tmp, tmpr))
        for k in range(h_ - npool):
            P.op('dve', lambda e, k=k: e.scalar_tensor_tensor(out=acc[:, 0:T], in0=stg[:, k:k + T], scalar=cw[:, k:k + 1], in1=acc[:, 0:T],
                                                            op0=ALU.mult, op1=ALU.add),
                 reads=[stgr, accr, 'PAR'], writes=[accr])
        for tmp, tmpr in ptmp:
            P.op('pool', lambda e, tmp=tmp: e.tensor_tensor(out=acc[:, 0:T], in0=acc[:, 0:T], in1=tmp[:, 0:T], op=ALU.add), reads=[tmpr, accr], writes=[accr])
            FPP.release((tmp, tmpr))
        P.op(tiny, lambda e: e.tensor_copy(out=tail, in_=stg[:, T:T + h_]), reads=[stgr], writes=[tres])
        yield
        yield
        final(acc, accr)
        STGP.release(slot)

    def step_active(active):
        for c in list(active):
            try:
                next(c)
            except StopIteration:
                active.remove(c)

    def take_bank(active=()):
        n = 0
        while not BANKS.free:
            step_active(active)
            n += 1
            assert n < 100, "no free PSUM bank"
        return BANKS.free.popleft()

    for s_ in range(NSEQ):
        for b_ in range(NBLK):
            first_blk = (b_ == 0)
            t0b = b_ * TB
            xres = ['X%d' % kc for kc in range(8)]

            xs = [ACC[k][:, 0:TB] for k in range(4)] + [STG[k][:, 0:TB] for k in range(4)]
            xsres = ['ACC%d' % k for k in range(4)] + ['STG%d' % k for k in range(4)]

            def load_x(ss, bb):
                t0_ = bb * TB
                P.dma('sp', ACCall[:, :, 0:TB], xT[ss, 0:4, :, t0_:t0_ + TB].rearrange("k p t -> p k t"), writes=xsres[0:4], semname='d_xsa')
                P.dma('sp', STGall[:, :, 0:TB], xT[ss, 4:8, :, t0_:t0_ + TB].rearrange("k p t -> p k t"), writes=xsres[4:8], semname='d_xsb')

            if s_ == 0 and b_ == 0:
                load_x(0, 0)
            if first_blk:
                P.op('dve', lambda e: e.memset(S, 0.0), writes=['S0', 'S1'])
                P.op('dve', lambda e: e.memset(Sb, 0.0), writes=['Sb0', 'Sb1'])
                P.op('dve', lambda e: e.memset(SS, 0.0), writes=['SS%d' % k for k in range(4)])
                P.op('dve', lambda e: e.memset(SSb, 0.0), writes=['SSb%d' % k for k in range(4)])
            P.retire(regA_ffn, regA_mix)
            BANKS.flush()
            FPP.flush()
            BPP.flush()
            bst, bstr = take_bank()
            for kc in range(8):
                rms_sq_step(bst, bstr, kc, xs[kc], [xsres[kc]])
            rms_finish(bst, bstr, RSTD, 'RSTD', D)
            cp_eng = 'act' if (s_ == 0 and b_ == 0) else 'pool'
            for kc in range(8):
                P.op('dve', lambda e, kc=kc: e.scalar_tensor_tensor(out=H[:, kc, :], in0=xs[kc], scalar=par("w1")[:, kc:kc + 1], in1=RSTD,
                                                                    op0=ALU.mult, op1=ALU.mult),
                     reads=[xsres[kc], 'RSTD', 'PAR'], writes=['H%d' % kc])
                if cp_eng == 'act':
                    P.op('act', lambda e, kc=kc: e.activation(out=X[:, kc, :], in_=xs[kc], func=AF.Copy), reads=[xsres[kc]], writes=[xres[kc]])
                else:
                    P.op('pool', lambda e, kc=kc: e.tensor_copy(out=X[:, kc, :], in_=xs[kc]), reads=[xsres[kc]], writes=[xres[kc]])
            hres = ['H%d' % kc for kc in range(8)]
            dump("H", H, hres)
            BANKS.flush()
            FPP.flush()
            BPP.flush()
            blk0 = (s_ == 0 and b_ == 0)
            tiny = 'dve' if blk0 else 'pool'
            npool_in = 0
            allt = lambda nm: ['%s_%d' % (nm, i) for i in range(NT)]
            active = []
            for j in range(NJ_IN):
                wv, wr = w_get('in', j)
                bk, bkr = take_bank(active)
                for kc in range(8):
                    P.op('pe', lambda e, kc=kc: e.matmul(bk[:, 0:TB], lhsT=wv[:, kc, :], rhs=H[:, kc, :], start=(kc == 0), stop=(kc == 7)),
                         reads=[wr] + hres, writes=[bkr], signal=(kc == 7))
                if j < 24:
                    cwj = par("cwg")[:, 4 * j:4 * j + 4]
                    fin = lambda acc, accr, j=j: P.op('act', lambda e: e.activation(out=QKV[:, j, :], in_=acc[:, 0:TB], func=AF.Silu), reads=[accr], writes=allt('QKV%d' % j))
                    active.append(conv_chain(bk, bkr, cwj, 4, None, TAILG[:, j, :], 'TG%d' % j, first_blk, TB, fin, npool_in * (j & 1), tiny))
                elif j < 32:
                    P.op('act', lambda e: e.activation(out=ZA[:, j - 24, :], in_=bk[:, 0:TB], func=AF.Silu), reads=[], writes=[bkr, 'ZA%d' % (j - 24)])
                    BANKS.release((bk, bkr))
                elif j < 40:
                    P.op('act', lambda e: e.activation(out=ZS[:, j - 32, :], in_=bk[:, 0:TB], func=AF.Silu), reads=[], writes=[bkr, 'ZS%d' % (j - 32)])
                    BANKS.release((bk, bkr))
                else:
                    jj = j - 40
                    cwj = par("cws")[:, 4 * jj:4 * jj + 4]
                    fin = lambda acc, accr, jj=jj: P.op('act', lambda e: e.activation(out=XBC[:, jj, :], in_=acc[:, 0:TB], func=AF.Silu), reads=[accr], writes=['XBC%d' % jj])
                    active.append(conv_chain(bk, bkr, cwj, 4, par("cbs")[:, jj:jj + 1], TAILG[:, 24 + jj, :], 'TG%d' % (24 + jj), first_blk, TB, fin, npool_in * (jj & 1), tiny))
                step_active(active)
            run_chains(active)
            if blk0:
                emit_rest_casts()
            for i in range(NT):
                bk, bkr = take_bank()
                for kc in range(8):
                    P.op('pe', lambda e, kc=kc: e.matmul(bk[:, 0:32], lhsT=H[:, kc, i * 128:(i + 1) * 128], rhs=WSM[:, kc, :], start=(kc == 0), stop=(kc == 7)),
                         reads=['WSM'] + hres, writes=[bkr], signal=(kc == 7))
                P.op('dve', lambda e: e.tensor_copy(out=SMALL[:, i, :], in_=bk[:, 0:32]), reads=[], writes=[bkr, 'SMALL%d' % i])
                BANKS.release((bk, bkr))
            dump("QKV", QKV, [x_ for j in range(24) for x_ in allt('QKV%d' % j)])
            dump("XBC", XBC, ['XBC%d' % j for j in range(12)])
            dump("SMALL", SMALL, ['SMALL%d' % i for i in range(NT)])
            P.retire(sqres_all, mixres)
            BANKS.flush()
            FPP.flush()
            BPP.flush()
            gates_emitted = [False] * NT
            mk_eng = 'dve' if blk0 else 'pool'
            cons_done = [0] * NT
            f2 = lambda a: a.rearrange("p a b -> p (a b)")
            v4 = lambda a: a.rearrange("p (a b) -> p a b", a=4)

            def gates_all():
                for i in range(NT):
                    while i >= 2 and cons_done[i - 2] < 3:
                        yield
                    sm = SMB[i % 2]
                    smr = 'SMB%d' % (i % 2)
                    SMi = SMALL[:, i, :]
                    smallr = 'SMALL%d' % i
                    P.op('act', lambda e: e.activation(out=sm["NLB"], in_=SMi[:, 0:8], func=AF.Exp, scale=-1.0), reads=[smallr], writes=[smr + 'NLB'])
                    P.op('act', lambda e: e.activation(out=sm["NLB"], in_=sm["NLB"], func=AF.Ln, bias=1.0), reads=[smr + 'NLB'], writes=[smr + 'NLB'])
                    P.op('act', lambda e: e.activation(out=sm["BETA"], in_=sm["NLB"], func=AF.Exp, scale=-1.0), reads=[smr + 'NLB'], writes=[smr + 'BETA'])
                    P.op('dve', lambda e: e.tensor_tensor(out=sm["U"], in0=SMi[:, 8:32], in1=par("dtb"), op=ALU.add), reads=[smallr, 'PAR'], writes=[smr + 'U'])
                    yield
                    P.op('act', lambda e: e.activation(out=sm["SPL"], in_=sm["U"], func=AF.Exp), reads=[smr + 'U'], writes=[smr + 'SPL'])
                    P.op('act', lambda e: e.activation(out=sm["SPL"], in_=sm["SPL"], func=AF.Ln, bias=1.0), reads=[smr + 'SPL'], writes=[smr + 'SPL'])
                    yield
                    P.op('dve', lambda e: e.tensor_tensor(out=sm["V"], in0=sm["SPL"], in1=NEGA, op=ALU.mult), reads=[smr + 'SPL', 'NEGA'], writes=[smr + 'V'])
                    bk, bkr = yield from BANKS.acquire()
                    bk2, bk2r = yield from BANKS.acquire()
                    P.op('pe', lambda e: e.matmul(bk[:, 0:24], lhsT=mcumF, rhs=sm["V"], start=True, stop=True), reads=[smr + 'V', 'CST'], writes=[bkr])
                    P.op('pe', lambda e: e.matmul(bk2[:, 0:24], lhsT=onesF, rhs=sm["V"], start=True, stop=True), reads=[smr + 'V', 'CST'], writes=[bk2r])
                    yield
                    P.op('dve', lambda e: e.tensor_copy(out=sm["CSUM"], in_=bk[:, 0:24]), reads=[], writes=[bkr, smr + 'CSUM'])
                    P.op('act', lambda e: e.activation(out=sm["TOT"], in_=bk2[:, 0:24], func=AF.Copy), reads=[], writes=[bk2r, smr + 'TOT'])
                    P.op('act', lambda e: e.activation(out=sm["NCS"], in_=bk[:, 0:24], func=AF.Copy, scale=-1.0), reads=[], writes=[bkr, smr + 'NCS'])
                    BANKS.release((bk, bkr))
                    BANKS.release((bk2, bk2r))
                    yield
                    P.op('dve', lambda e: e.tensor_tensor(out=sm["DIFF"], in0=sm["TOT"], in1=sm["CSUM"], op=ALU.subtract), reads=[smr + 'TOT', smr + 'CSUM'], writes=[smr + 'DIFF'])
                    P.op('act', lambda e: e.activation(out=sm["ECS"], in_=sm["CSUM"], func=AF.Exp), reads=[smr + 'CSUM'], writes=[smr + 'ECS'])
                    P.op('act', lambda e: e.activation(out=sm["ETOT"], in_=sm["TOT"], func=AF.Exp), reads=[smr + 'TOT'], writes=[smr + 'ETOT'])
                    P.op('dve', lambda e: e.tensor_tensor(out=sm["GB"], in0=sm["CSUM"][:, 0:8], in1=sm["NLB"], op=ALU.subtract), reads=[smr + 'CSUM', smr + 'NLB'], writes=[smr + 'GB'])
                    yield
                    P.op('act', lambda e: e.activation(out=sm["EREM"], in_=sm["DIFF"], func=AF.Exp), reads=[smr + 'DIFF'], writes=[smr + 'EREM'])
                    P.op('act', lambda e: e.activation(out=sm["BEG"], in_=sm["GB"], func=AF.Exp), reads=[smr + 'GB'], writes=[smr + 'BEG'])
                    yield
                    P.op('dve', lambda e: e.tensor_tensor(out=sm["DTE"], in0=sm["SPL"][:, 8:24], in1=sm["EREM"][:, 8:24], op=ALU.mult), reads=[smr + 'SPL', smr + 'EREM'], writes=[smr + 'DTE'])
                    gates_emitted[i] = True
                    yield

            rec_done = [[False] * NT for _ in range(2)]
            gdn_items = [(i, hg) for i in range(NT) for hg in range(2)]

            def gdn_worker():
                while gdn_items:
                    i, hg = gdn_items.pop(0)
                    while not gates_emitted[i]:
                        yield
                    yield from gdn_chain(i, hg)
                    cons_done[i] += 1

            def ssd_all():
                for i in range(NT):
                    while not gates_emitted[i]:
                        yield
                    yield from ssd_chain(i)
                    cons_done[i] += 1

            def gdn_chain(i, hg):
                tk = slice(i * 128, (i + 1) * 128)
                sm = SMB[i % 2]
                smr = 'SMB%d' % (i % 2)
                h0 = 4 * hg
                qres = ['QKV%d_%d' % (h0 + k, i) for k in range(4)]
                kres = ['QKV%d_%d' % (8 + h0 + k, i) for k in range(4)]
                vres = ['QKV%d_%d' % (16 + h0 + k, i) for k in range(4)]
                q4 = QKV[:, h0:h0 + 4, tk]
                k4 = QKV[:, 8 + h0:8 + h0 + 4, tk]
                csum4 = bc_last(sm["CSUM"][:, h0:h0 + 4], 128)
                gb4 = bc_last(sm["GB"][:, h0:h0 + 4], 128)
                for x4, xres, sc in ((k4, kres, 1.0), (q4, qres, 128.0 ** -0.5)):
                    sq, sqr = yield from BPP.acquire()
                    P.op('act', lambda e: e.activation(out=v4(sq), in_=x4, func=AF.Square), reads=xres, writes=[sqr])
                    yield
                    bk, bkr = yield from BANKS.acquire()
                    P.op('pe', lambda e: e.matmul(bk, lhsT=onesB, rhs=sq, start=True, stop=True), reads=[sqr, 'onesB'], writes=[bkr])
                    BPP.release((sq, sqr))
                    yield
                    rn, rnr = yield from FPP.acquire()
                    P.op('act', lambda e: e.activation(out=rn, in_=bk, func=AF.Ln, bias=EPS), reads=[], writes=[bkr, rnr])
                    BANKS.release((bk, bkr))
                    P.op('act', lambda e: e.activation(out=rn, in_=rn, func=AF.Exp, scale=-0.5), reads=[rnr], writes=[rnr])
                    yield
                    P.op('dve', lambda e, sc=sc: e.scalar_tensor_tensor(out=x4, in0=x4, scalar=sc, in1=v4(rn), op0=ALU.mult, op1=ALU.mult),
                         reads=xres + [rnr], writes=xres)
                    FPP.release((rn, rnr))
                    yield
                bR1, bR1r = yield from BANKS.acquire()
                for hl in range(4):
                    P.op('pe', lambda e, hl=hl: e.matmul(bR1[:, hl * 128:(hl + 1) * 128], lhsT=sm["CSUM"][:, h0 + hl:h0 + hl + 1].to_broadcast([128, 128]), rhs=identF, start=True, stop=True),
                         reads=[smr + 'CSUM', 'CST'], writes=[bR1r], signal=(hl == 3))
                yield
                gml, gmlr = yield from FPP.acquire()
                P.op('dve', lambda e: e.scalar_tensor_tensor(out=v4(gml), in0=v4(bR1), scalar=-1.0, in1=bc_mid(mL, 4), op0=ALU.mult, op1=ALU.add), reads=['CST'], writes=[bR1r, gmlr])
                yield
                EL, ELr = yield from BPP.acquire()
                for hl in range(4):
                    P.op('act', lambda e, hl=hl: e.activation(out=EL[:, hl * 128:(hl + 1) * 128], in_=gml[:, hl * 128:(hl + 1) * 128], func=AF.Exp, bias=sm["GB"][:, h0 + hl:h0 + hl + 1]),
                         reads=[gmlr, smr + 'GB'], writes=[ELr])
                FPP.release((gml, gmlr))
                gmq, gmqr = yield from FPP.acquire()
                P.op('dve', lambda e: e.tensor_tensor(out=v4(gmq), in0=v4(bR1), in1=bc_mid(mUi, 4), op=ALU.add), reads=['CST'], writes=[bR1r, gmqr])
                yield
                EQ, EQr = yield from BPP.acquire()
                for hl in range(4):
                    P.op('act', lambda e, hl=hl: e.activation(out=EQ[:, hl * 128:(hl + 1) * 128], in_=gmq[:, hl * 128:(hl + 1) * 128], func=AF.Exp, bias=sm["NCS"][:, h0 + hl:h0 + hl + 1]),
                         reads=[gmqr, smr + 'NCS'], writes=[EQr])
                FPP.release((gmq, gmqr))
                er1, er1r = yield from BPP.acquire()
                P.op('act', lambda e: e.activation(out=er1, in_=bR1, func=AF.Exp), reads=[], writes=[bR1r, er1r])
                BANKS.release((bR1, bR1r))
                yield
                bR2, bR2r = yield from BANKS.acquire()
                for hl in range(4):
                    P.op('pe', lambda e, hl=hl: e.matmul(bR2[:, hl * 128:(hl + 1) * 128], lhsT=sm["GB"][:, h0 + hl:h0 + hl + 1].to_broadcast([128, 128]), rhs=identF, start=True, stop=True),
                         reads=[smr + 'GB', 'CST'], writes=[bR2r], signal=(hl == 3))
                (QD, QDr), (QKT, QKTr), (Pc, Pcr), (PTc, PTcr), (Pn, Pnr), (PTn, PTnr), (Tc, Tcr), (Tn, Tnr), (NOT, NOTr) = yield from acquire_n(GP, 9)
                P.op('dve', lambda e: e.tensor_tensor(out=QD, in0=q4, in1=v4(er1), op=ALU.mult), reads=qres + [er1r], writes=[QDr])
                BPP.release((er1, er1r))
                yield
                gmu, gmur = yield from FPP.acquire()
                P.op('dve', lambda e: e.tensor_tensor(out=v4(gmu), in0=v4(bR2), in1=bc_mid(mUs, 4), op=ALU.add), reads=['CST'], writes=[bR2r, gmur])
                BANKS.release((bR2, bR2r))
                yield
                EU, EUr = yield from BPP.acquire()
                for hl in range(4):
                    P.op('act', lambda e, hl=hl: e.activation(out=EU[:, hl * 128:(hl + 1) * 128], in_=gmu[:, hl * 128:(hl + 1) * 128], func=AF.Exp, bias=sm["NCS"][:, h0 + hl:h0 + hl + 1]),
                         reads=[gmur, smr + 'NCS'], writes=[EUr])
                FPP.release((gmu, gmur))
                bK, bKr = yield from BANKS.acquire()
                for hl in range(4):
                    kh = QKV[:, 8 + h0 + hl, tk]
                    P.op('pe', lambda e, kh=kh, hl=hl: e.matmul(bK[:, hl * 128:(hl + 1) * 128], lhsT=kh, rhs=kh, start=True, stop=True),
                         reads=kres, writes=[bKr], signal=(hl == 3))
                bKQ, bKQr = yield from BANKS.acquire()
                for hl in range(4):
                    kh = QKV[:, 8 + h0 + hl, tk]
                    qh = QKV[:, h0 + hl, tk]
                    P.op('pe', lambda e, kh=kh, qh=qh, hl=hl: e.matmul(bKQ[:, hl * 128:(hl + 1) * 128], lhsT=kh, rhs=qh, start=True, stop=True),
                         reads=kres + qres, writes=[bKQr], signal=(hl == 3))
                yield
                P.op('dve', lambda e: e.scalar_tensor_tensor(out=f2(PTc), in0=bK, scalar=-1.0, in1=EL, op0=ALU.mult, op1=ALU.mult), reads=[ELr], writes=[bKr, PTcr])
                BPP.release((EL, ELr))
                yield
                P.op('dve', lambda e: e.scalar_tensor_tensor(out=f2(Pc), in0=bK, scalar=-1.0, in1=EU, op0=ALU.mult, op1=ALU.mult), reads=[EUr], writes=[bKr, Pcr])
                BPP.release((EU, EUr))
                BANKS.release((bK, bKr))
                yield
                P.op('dve', lambda e: e.tensor_tensor(out=f2(QKT), in0=bKQ, in1=EQ, op=ALU.mult), reads=[EQr], writes=[bKQr, QKTr])
                BPP.release((EQ, EQr))
                BANKS.release((bKQ, bKQr))
                P.op(mk_eng, lambda e: e.tensor_tensor(out=NOT, in0=PTc, in1=bc_mid(mOFF, 4), op=ALU.mult), reads=[PTcr, 'CST'], writes=[NOTr])
                P.op(mk_eng, lambda e: e.tensor_tensor(out=PTc, in0=PTc, in1=bc_mid(mBD, 4), op=ALU.mult), reads=[PTcr, 'CST'], writes=[PTcr])
                yield
                P.op(mk_eng, lambda e: e.tensor_tensor(out=Pc, in0=Pc, in1=bc_mid(mBD, 4), op=ALU.mult), reads=[Pcr, 'CST'], writes=[Pcr])
                P.op(mk_eng, lambda e: e.tensor_tensor(out=Tc, in0=Pc, in1=bc_mid(identB, 4), op=ALU.add), reads=[Pcr, 'identB'], writes=[Tcr])
                yield
                NLEV = 5
                for m in range(NLEV):
                    last = (m == NLEV - 1)
                    bPT, bPTr = yield from BANKS.acquire()
                    for hl in range(4):
                        P.op('pe', lambda e, hl=hl: e.matmul(bPT[:, hl * 128:(hl + 1) * 128], lhsT=Pc[:, hl, :], rhs=PTc[:, hl, :], start=True, stop=True),
                             reads=[PTcr, Pcr], writes=[bPTr], signal=(hl == 3))
                    if not last:
                        bP, bPr = yield from BANKS.acquire()
                        for hl in range(4):
                            P.op('pe', lambda e, hl=hl: e.matmul(bP[:, hl * 128:(hl + 1) * 128], lhsT=PTc[:, hl, :], rhs=Pc[:, hl, :], start=True, stop=True),
                                 reads=[PTcr, Pcr], writes=[bPr], signal=(hl == 3))
                    yield
                    P.op('act', lambda e: e.activation(out=f2(PTn), in_=bPT, func=AF.Copy), reads=[], writes=[bPTr, PTnr])
                    BANKS.release((bPT, bPTr))
                    if not last:
                        P.op('dve', lambda e: e.tensor_copy(out=f2(Pn), in_=bP), reads=[], writes=[bPr, Pnr])
                        BANKS.release((bP, bPr))
                    yield
                    bT, bTr = yield from BANKS.acquire()
                    for hl in range(4):
                        P.op('pe', lambda e, hl=hl: e.matmul(bT[:, hl * 128:(hl + 1) * 128], lhsT=identB, rhs=Tc[:, hl, :], start=True, stop=False),
                             reads=[Tcr, 'identB'], writes=[bTr], signal=False)
                        P.op('pe', lambda e, hl=hl: e.matmul(bT[:, hl * 128:(hl + 1) * 128], lhsT=PTn[:, hl, :], rhs=Tc[:, hl, :], start=False, stop=True),
                             reads=[Tcr, PTnr], writes=[bTr], signal=(hl == 3))
                    yield
                    P.op('dve' if (m & 1) else 'act',
                         (lambda e: e.tensor_copy(out=f2(Tn), in_=bT)) if (m & 1) else (lambda e: e.activation(out=f2(Tn), in_=bT, func=AF.Copy)),
                         reads=[], writes=[bTr, Tnr])
                    BANKS.release((bT, bTr))
                    Pc, Pcr, Pn, Pnr = Pn, Pnr, Pc, Pcr
                    PTc, PTcr, PTn, PTnr = PTn, PTnr, PTc, PTcr
                    Tc, Tcr, Tn, Tnr = Tn, Tnr, Tc, Tcr
                    yield
                bTT, bTTr = yield from BANKS.acquire()
                bTTb = bTT.bitcast(BF16)
                for hl in range(4):
                    P.op('pe', lambda e, hl=hl: e.transpose(out=bTTb[:, hl * 128:(hl + 1) * 128], in_=Tc[:, hl, :], identity=identB),
                         reads=[Tcr, 'identB'], writes=[bTTr], signal=(hl == 3))
                bA, bAr = yield from BANKS.acquire()
                for hl in range(4):
                    P.op('pe', lambda e, hl=hl: e.matmul(bA[:, hl * 128:(hl + 1) * 128], lhsT=NOT[:, hl, :], rhs=Tc[:, hl, :], start=True, stop=True),
                         reads=[NOTr, Tcr], writes=[bAr], signal=(hl == 3))
                yield
                DG, DGr = Pn, Pnr
                P.op('act', lambda e: e.activation(out=f2(DG), in_=bTTb[:, 0:512], func=AF.Copy), reads=[], writes=[bTTr, DGr])
                BANKS.release((bTT, bTTr))
                A1, A1r = PTn, PTnr
                P.op('act', lambda e: e.activation(out=f2(A1), in_=bA, func=AF.Copy), reads=[], writes=[bAr, A1r])
                BANKS.release((bA, bAr))
                yield
                bF, bFr = yield from BANKS.acquire()
                for hl in range(4):
                    P.op('pe', lambda e, hl=hl: e.matmul(bF[:, hl * 128:(hl + 1) * 128], lhsT=identB, rhs=Tc[:, hl, :], start=True, stop=False),
                         reads=[Tcr, 'identB'], writes=[bFr], signal=False)
                    P.op('pe', lambda e, hl=hl: e.matmul(bF[:, hl * 128:(hl + 1) * 128], lhsT=DG[:, hl, :], rhs=A1[:, hl, :], start=False, stop=True),
                         reads=[DGr, A1r], writes=[bFr], signal=(hl == 3))
                bTr_, bTr_r = yield from BANKS.acquire()
                bTb = bTr_.bitcast(BF16)
                for hl in range(4):
                    P.op('pe', lambda e, hl=hl: e.transpose(out=bTb[:, hl * 128:(hl + 1) * 128], in_=QKV[:, 8 + h0 + hl, tk], identity=identB),
                         reads=kres + ['identB'], writes=[bTr_r], signal=False)
                for hl in range(4):
                    P.op('pe', lambda e, hl=hl: e.transpose(out=bTb[:, 512 + hl * 128:512 + (hl + 1) * 128], in_=QKV[:, 16 + h0 + hl, tk], identity=identB),
                         reads=vres + ['identB'], writes=[bTr_r], signal=(hl == 3))
                yield
                P.op('act', lambda e: e.activation(out=f2(Tn), in_=bF, func=AF.Copy), reads=[], writes=[bFr, Tnr])
                BANKS.release((bF, bFr))
                Tc, Tcr, Tn, Tnr = Tn, Tnr, Tc, Tcr
                for it_ in ((Pc, Pcr), (PTc, PTcr), (Pn, Pnr), (PTn, PTnr), (NOT, NOTr), (Tn, Tnr)):
                    GP.release(it_)
                ktm = bTb[:, 0:512].rearrange("p (a b) -> p a b", a=4)
                vtm = bTb[:, 512:1024].rearrange("p (a b) -> p a b", a=4)
                (XK, XKr), (KDEC, KDECr), (BV, BVr), (WTN, WTNr), (VN, VNr) = yield from acquire_n(GP, 5)
                P.op('dve', lambda e: e.tensor_tensor(out=BV, in0=vtm, in1=bc_last(sm["BETA"][:, h0:h0 + 4], 128), op=ALU.mult), reads=[smr + 'BETA'], writes=[bTr_r, BVr])
                yield
                P.op('dve', lambda e: e.tensor_tensor(out=XK, in0=ktm, in1=bc_last(sm["BEG"][:, h0:h0 + 4], 128), op=ALU.mult), reads=[smr + 'BEG'], writes=[bTr_r, XKr])
                yield
                P.op('dve', lambda e: e.tensor_tensor(out=KDEC, in0=ktm, in1=bc_last(sm["EREM"][:, h0:h0 + 4], 128), op=ALU.mult), reads=[smr + 'EREM'], writes=[bTr_r, KDECr])
                BANKS.release((bTr_, bTr_r))
                bU, bUr = yield from BANKS.acquire()
                for hl in range(4):
                    P.op('pe', lambda e, hl=hl: e.matmul(bU[:, hl * 128:(hl + 1) * 128], lhsT=Tc[:, hl, :], rhs=BV[:, hl, :], start=True, stop=True),
                         reads=[Tcr, BVr], writes=[bUr], signal=(hl == 3))
                bW, bWr = yield from BANKS.acquire()
                for hl in range(4):
                    P.op('pe', lambda e, hl=hl: e.matmul(bW[:, hl * 128:(hl + 1) * 128], lhsT=XK[:, hl, :], rhs=Tc[:, hl, :], start=True, stop=True),
                         reads=[Tcr, XKr], writes=[bWr], signal=(hl == 3))
                yield
                UFf, UFr = yield from FPP.acquire()
                UF = v4(UFf)
                P.op('act', lambda e: e.activation(out=f2(UF), in_=bU, func=AF.Copy), reads=[], writes=[bUr, UFr])
                BANKS.release((bU, bUr))
                P.op('act', lambda e: e.activation(out=f2(WTN), in_=bW, func=AF.Copy, scale=-1.0), reads=[], writes=[bWr, WTNr])
                BANKS.release((bW, bWr))
                yield
                while i > 0 and not rec_done[hg][i - 1]:
                    yield
                sres = 'Sb%d' % hg
                bWS, bWSr = yield from BANKS.acquire()
                for hl in range(4):
                    P.op('pe', lambda e, hl=hl: e.matmul(bWS[:, hl * 128:(hl + 1) * 128], lhsT=WTN[:, hl, :], rhs=Sb[:, h0 + hl, :], start=True, stop=True),
                         reads=[WTNr, sres], writes=[bWSr], signal=(hl == 3))
                yield
                P.op('dve', lambda e: e.tensor_tensor(out=f2(VN), in0=bWS, in1=f2(UF), op=ALU.add), reads=[UFr], writes=[bWSr, VNr])
                BANKS.release((bWS, bWSr))
                yield
                bO, bOr = yield from BANKS.acquire()
                for hl in range(4):
                    P.op('pe', lambda e, hl=hl: e.matmul(bO[:, hl * 128:(hl + 1) * 128], lhsT=Sb[:, h0 + hl, :], rhs=QD[:, hl, :], start=True, stop=False),
                         reads=[sres, QDr], writes=[bOr], signal=False)
                    P.op('pe', lambda e, hl=hl: e.matmul(bO[:, hl * 128:(hl + 1) * 128], lhsT=VN[:, hl, :], rhs=QKT[:, hl, :], start=False, stop=True),
                         reads=[VNr, QKTr], writes=[bOr], signal=(hl == 3))
                bDS, bDSr = yield from BANKS.acquire()
                for hl in range(4):
                    P.op('pe', lambda e, hl=hl: e.matmul(bDS[:, hl * 128:(hl + 1) * 128], lhsT=KDEC[:, hl, :], rhs=VN[:, hl, :], start=True, stop=True),
                         reads=[KDECr, VNr], writes=[bDSr], signal=(hl == 3))
                S4 = S[:, h0:h0 + 4, :]
                srf = 'S%d' % hg
                yield
                for hl in range(4):
                    P.op('dve', lambda e, hl=hl: e.scalar_tensor_tensor(out=S[:, h0 + hl, :], in0=S[:, h0 + hl, :], scalar=sm["ETOT"][:, h0 + hl:h0 + hl + 1], in1=bDS[:, hl * 128:(hl + 1) * 128],
                                                                      op0=ALU.mult, op1=ALU.add),
                         reads=[srf, smr + 'ETOT'], writes=[bDSr, srf])
                BANKS.release((bDS, bDSr))
                sqo, sqor = yield from BPP.acquire()
                P.op('act', lambda e: e.activation(out=sqo, in_=bO, func=AF.Square), reads=[], writes=[bOr, sqor])
                yield
                P.op('act', lambda e: e.activation(out=Sb[:, h0:h0 + 4, :], in_=S4, func=AF.Copy), reads=[srf], writes=[sres])
                rec_done[hg][i] = True
                bN, bNr = yield from BANKS.acquire()
                P.op('pe', lambda e: e.matmul(bN, lhsT=onesB, rhs=sqo, start=True, stop=True), reads=[sqor, 'onesB'], writes=[bNr])
                BPP.release((sqo, sqor))
                yield
                rno, rnor = yield from FPP.acquire()
                P.op('act', lambda e: e.activation(out=rno, in_=bN, func=AF.Ln, scale=1.0 / 128, bias=EPS), reads=[], writes=[bNr, rnor])
                BANKS.release((bN, bNr))
                P.op('act', lambda e: e.activation(out=rno, in_=rno, func=AF.Exp, scale=-0.5), reads=[rnor], writes=[rnor])
                yield
                P.op('dve', lambda e: e.scalar_tensor_tensor(out=rno, in0=bO, scalar=par("gnw")[:, 0:1], in1=rno, op0=ALU.mult, op1=ALU.mult),
                     reads=[rnor, 'PAR'], writes=[bOr, rnor])
                BANKS.release((bO, bOr))
                yield
                P.op('dve', lambda e: e.tensor_tensor(out=MIX[:, h0:h0 + 4, tk], in0=v4(rno), in1=ZA[:, h0:h0 + 4, tk], op=ALU.mult),
                     reads=[rnor] + ['ZA%d' % (h0 + k) for k in range(4)], writes=['MIX%d' % (h0 + k) for k in range(4)])
                FPP.release((rno, rnor))
                FPP.release((UFf, UFr))
                for it_ in ((Tc, Tcr), (QD, QDr), (QKT, QKTr), (XK, XKr), (KDEC, KDECr), (BV, BVr), (WTN, WTNr), (VN, VNr)):
                    GP.release(it_)
                yield

            def ssd_chain(i):
                tk = slice(i * 128, (i + 1) * 128)
                sm = SMB[i % 2]
                smr = 'SMB%d' % (i % 2)
                bBC, bBCr = yield from BANKS.acquire()
                for gq in range(2):
                    P.op('pe', lambda e, gq=gq: e.matmul(bBC[:, gq * 128:(gq + 1) * 128], lhsT=XBC[:, 8 + gq, tk], rhs=XBC[:, 10 + gq, tk], start=True, stop=True),
                         reads=['XBC%d' % (8 + gq), 'XBC%d' % (10 + gq)], writes=[bBCr], signal=(gq == 1))
                bBT, bBTr = yield from BANKS.acquire()
                bBTb = bBT.bitcast(BF16)
                for gq in range(2):
                    P.op('pe', lambda e, gq=gq: e.transpose(out=bBTb[:, gq * 128:(gq + 1) * 128], in_=XBC[:, 8 + gq, tk], identity=identB),
                         reads=['XBC%d' % (8 + gq), 'identB'], writes=[bBTr], signal=(gq == 1))
                yield
                P.op('act', lambda e: e.activation(out=BCS, in_=bBC[:, 0:256], func=AF.Copy), reads=[], writes=[bBCr, 'BCS'])
                BANKS.release((bBC, bBCr))
                P.op('dve', lambda e: e.tensor_copy(out=BTM.rearrange("p a b -> p (a b)"), in_=bBTb[:, 0:256]), reads=[], writes=[bBTr, 'BTM'])
                BANKS.release((bBT, bBTr))
                yield
                for hq in range(4):
                    h0 = 4 * hq
                    gq = hq // 2
                    pi = hq % 2
                    acs4 = bc_last(sm["CSUM"][:, 8 + h0:8 + h0 + 4], 128)
                    bR3, bR3r = yield from BANKS.acquire()
                    for hl in range(4):
                        P.op('pe', lambda e, hl=hl: e.matmul(bR3[:, hl * 128:(hl + 1) * 128], lhsT=sm["CSUM"][:, 8 + h0 + hl:8 + h0 + hl + 1].to_broadcast([128, 128]), rhs=identF, start=True, stop=True),
                             reads=[smr + 'CSUM', 'CST'], writes=[bR3r], signal=(hl == 3))
                    bXT, bXTr = yield from BANKS.acquire()
                    bXTb = bXT.bitcast(BF16)
                    for k in range(2):
                        P.op('pe', lambda e, k=k: e.transpose(out=bXTb[:, k * 128:(k + 1) * 128], in_=XBC[:, 2 * hq + k, tk], identity=identB),
                             reads=['XBC%d' % (2 * hq + k), 'identB'], writes=[bXTr], signal=(k == 1))
                    yield
                    gms, gmsr = yield from FPP.acquire()
                    P.op('dve', lambda e: e.tensor_tensor(out=v4(gms), in0=v4(bR3), in1=bc_mid(mUi, 4), op=ALU.add), reads=['CST'], writes=[bR3r, gmsr])
                    yield
                    ES, ESr = yield from BPP.acquire()
                    for hl in range(4):
                        P.op('act', lambda e, hl=hl: e.activation(out=ES[:, hl * 128:(hl + 1) * 128], in_=gms[:, hl * 128:(hl + 1) * 128], func=AF.Exp, bias=sm["NCS"][:, 8 + h0 + hl:8 + h0 + hl + 1]),
                             reads=[gmsr, smr + 'NCS'], writes=[ESr])
                    FPP.release((gms, gmsr))
                    er3, er3r = yield from BPP.acquire()
                    P.op('act', lambda e: e.activation(out=er3, in_=bR3, func=AF.Exp), reads=[], writes=[bR3r, er3r])
                    BANKS.release((bR3, bR3r))
                    xtm = bXTb[:, 0:256].rearrange("p (a b) -> p a b", a=4)
                    XDT, XDTr = XDTb[pi], 'XDT%d' % pi
                    XDD, XDDr = XDDb[pi], 'XDD%d' % pi
                    P.op('dve', lambda e: e.tensor_tensor(out=XDT, in0=xtm, in1=bc_last(sm["SPL"][:, 8 + h0:8 + h0 + 4], 64), op=ALU.mult), reads=[smr + 'SPL'], writes=[bXTr, XDTr])
                    yield
                    P.op('dve', lambda e: e.tensor_tensor(out=XDD, in0=xtm, in1=bc_last(sm["DTE"][:, h0:h0 + 4], 64), op=ALU.mult), reads=[smr + 'DTE'], writes=[bXTr, XDDr])
                    BANKS.release((bXT, bXTr))
                    GT, GTr = GTb[pi], 'GT%d' % pi
                    P.op('dve', lambda e: e.tensor_tensor(out=GT, in0=v4(ES), in1=bc_mid(BCS[:, gq * 128:(gq + 1) * 128], 4), op=ALU.mult),
                         reads=[ESr, 'BCS'], writes=[GTr])
                    BPP.release((ES, ESr))
                    yield
                    CD, CDr = CDECb[pi], 'CDEC%d' % pi
                    P.op('dve', lambda e: e.tensor_tensor(out=CD, in0=v4(er3), in1=bc_mid(XBC[:, 10 + gq, tk], 4), op=ALU.mult),
                         reads=[er3r, 'XBC%d' % (10 + gq)], writes=[CDr])
                    BPP.release((er3, er3r))
                    yield
                    ssr = 'SSb%d' % hq
                    bY, bYr = yield from BANKS.acquire()
                    for hl in range(4):
                        pr = hl // 2
                        hf = hl % 2
                        o_ = bY[hf * 64:(hf + 1) * 64, pr * 128:(pr + 1) * 128]
                        P.op('pe', lambda e, hl=hl, o_=o_, hf=hf: e.matmul(o_, lhsT=XDT[:, hl, :], rhs=GT[:, hl, :], start=True, stop=False, tile_position=(0, 64 * hf)),
                             reads=[XDTr, GTr], writes=[bYr], signal=False)
                        P.op('pe', lambda e, hl=hl, o_=o_, hf=hf: e.matmul(o_, lhsT=SSb[:, h0 + hl, :], rhs=CD[:, hl, :], start=False, stop=True, tile_position=(0, 64 * hf)),
                             reads=[ssr, CDr], writes=[bYr], signal=(hl == 3))
                    bDSS, bDSSr = yield from BANKS.acquire()
                    P.op('pe', lambda e: e.matmul(bDSS[:, 0:256], lhsT=BTM[:, gq, :], rhs=XDD.rearrange("p a b -> p (a b)"), start=True, stop=True),
                         reads=['BTM', XDDr], writes=[bDSSr])
                    SS4 = SS[:, h0:h0 + 4, :]
                    ssf = 'SS%d' % hq
                    yield
                    for hl in range(4):
                        P.op('dve', lambda e, hl=hl: e.scalar_tensor_tensor(out=SS[:, h0 + hl, :], in0=SS[:, h0 + hl, :], scalar=sm["ETOT"][:, 8 + h0 + hl:8 + h0 + hl + 1], in1=bDSS[:, hl * 64:(hl + 1) * 64],
                                                                          op0=ALU.mult, op1=ALU.add),
                             reads=[ssf, smr + 'ETOT'], writes=[bDSSr, ssf])
                    BANKS.release((bDSS, bDSSr))
                    yield
                    P.op('act', lambda e: e.activation(out=SSb[:, h0:h0 + 4, :], in_=SS4, func=AF.Copy), reads=[ssf], writes=[ssr])
                    ys, ysr = yield from FPP.acquire()
                    for pr in range(2):
                        xt = 2 * hq + pr
                        P.op('dve', lambda e, pr=pr, xt=xt: e.scalar_tensor_tensor(out=ys[:, pr * 128:(pr + 1) * 128], in0=XBC[:, xt, tk], scalar=par("dexp")[:, xt:xt + 1],
                                                                               in1=bY[:, pr * 128:(pr + 1) * 128], op0=ALU.mult, op1=ALU.add),
                             reads=['XBC%d' % xt, 'PAR'], writes=[bYr, ysr])
                    BANKS.release((bY, bYr))
                    yield
                    P.op('dve', lambda e: e.tensor_tensor(out=YG[:, 2 * pi:2 * pi + 2, :], in0=ys[:, 0:256].rearrange("p (a b) -> p a b", a=2), in1=ZS[:, 2 * hq:2 * hq + 2, tk], op=ALU.mult),
                         reads=[ysr, 'ZS%d' % (2 * hq), 'ZS%d' % (2 * hq + 1)], writes=['YG%d' % pi])
                    FPP.release((ys, ysr))
                    yield
                    if pi == 1:
                        ygr = ['YG0', 'YG1']
                        sqy, sqyr = yield from BPP.acquire()
                        P.op('act', lambda e: e.activation(out=sqy, in_=YG.rearrange("p a b -> p (a b)"), func=AF.Square), reads=ygr, writes=[sqyr])
                        yield
                        bNS, bNSr = yield from BANKS.acquire()
                        for k in range(4):
                            P.op('pe', lambda e, k=k: e.matmul(bNS[:, 0:128], lhsT=onesB, rhs=sqy[:, k * 128:(k + 1) * 128], start=(k == 0), stop=(k == 3)),
                                 reads=[sqyr, 'onesB'], writes=[bNSr], signal=(k == 3))
                        BPP.release((sqy, sqyr))
                        yield
                        rns, rnsr = yield from FPP.acquire()
                        P.op('act', lambda e: e.activation(out=rns[:, 0:128], in_=bNS[:, 0:128], func=AF.Ln, scale=1.0 / 512, bias=EPS), reads=[], writes=[bNSr, rnsr])
                        BANKS.release((bNS, bNSr))
                        P.op('act', lambda e: e.activation(out=rns[:, 0:128], in_=rns[:, 0:128], func=AF.Exp, scale=-0.5), reads=[rnsr], writes=[rnsr])
                        yield
                        for k in range(4):
                            xt = 4 * gq + k
                            P.op('dve', lambda e, xt=xt, k=k: e.scalar_tensor_tensor(out=MIX[:, 8 + xt, tk], in0=YG[:, k, :], scalar=par("snw")[:, xt:xt + 1], in1=rns[:, 0:128],
                                                                                op0=ALU.mult, op1=ALU.mult),
                                 reads=ygr + [rnsr, 'PAR'], writes=['MIX%d' % (8 + xt)])
                        FPP.release((rns, rnsr))
                        yield

            run_chains([gates_all(), gdn_worker(), gdn_worker(), gdn_worker(), ssd_all()])
            dump("MIX", MIX, mixres)
            BANKS.flush()
            FPP.flush()
            BPP.flush()
            P.retire(regA_mix, regA_ffn)
            mores = ['MO%d' % j for j in range(8)]

            def proj_norm_residual(kind, npiece, nkc, src3, srcres, wname, sqbuf, sqname):
                bst, bstr = take_bank()
                for j in range(8):
                    bk, bkr = take_bank()
                    for q in range(npiece):
                        wv, wr = w_get(kind, j, q)
                        for k8 in range(min(8, nkc - 8 * q)):
                            kc = q * 8 + k8
                            P.op('pe', lambda e, kc=kc, k8=k8: e.matmul(bk[:, 0:TB], lhsT=wv[:, k8, :], rhs=src3[:, kc, :], start=(kc == 0), stop=(kc == nkc - 1)),
                                 reads=[wr] + srcres, writes=[bkr], signal=(kc == nkc - 1))
                    if j & 1:
                        P.op('act', lambda e: e.activation(out=MO[:, j, :], in_=bk[:, 0:TB], func=AF.Identity, scale=par(wname)[:, j:j + 1]),
                             reads=['PAR'], writes=[bkr, 'MO%d' % j])
                    else:
                        P.op('dve', lambda e: e.tensor_scalar_mul(out=MO[:, j, :], in0=bk[:, 0:TB], scalar1=par(wname)[:, j:j + 1]),
                             reads=['PAR'], writes=[bkr, 'MO%d' % j])
                    rms_sq_step(bst, bstr, j, bk[:, 0:TB], [], src_is_psum=bkr, sqbuf=sqbuf, sqname=sqname)
                    BANKS.release((bk, bkr))
                rms_finish(bst, bstr, RSTD, 'RSTD', D)

            proj_norm_residual('out', 2, 16, MIX, mixres, "w2", ACTB, 'ACTB')
            bst, bstr = take_bank()
            for kc in range(8):
                P.op('dve', lambda e, kc=kc: e.tensor_tensor(out=MO[:, kc, :], in0=MO[:, kc, :], in1=RSTD, op=ALU.mult),
                     reads=['MO%d' % kc, 'RSTD'], writes=['MO%d' % kc])
                P.op('pool', lambda e, kc=kc: e.tensor_tensor(out=X[:, kc, :], in0=X[:, kc, :], in1=MO[:, kc, :], op=ALU.add), reads=[xres[kc], 'MO%d' % kc], writes=[xres[kc]])
                rms_sq_step(bst, bstr, kc, X[:, kc, :], [xres[kc]])
            dump("X1", X, xres)
            rstd2, rstd2r = ftmp()
            rms_finish(bst, bstr, rstd2, rstd2r, D)
            for kc in range(8):
                P.op('dve', lambda e, kc=kc: e.scalar_tensor_tensor(out=H[:, kc, :], in0=X[:, kc, :], scalar=par("w3")[:, kc:kc + 1], in1=rstd2[:, 0:TB],
                                                                    op0=ALU.mult, op1=ALU.mult),
                     reads=[xres[kc], rstd2r, 'PAR'], writes=['H%d' % kc])
            BANKS.flush()
            FPP.flush()
            BPP.flush()
            active = []
            for jj in range(22):
                for half in range(2):
                    j = jj + 22 * half
                    wv, wr = w_get('up', j)
                    bk, bkr = take_bank(active)
                    for kc in range(8):
                        P.op('pe', lambda e, kc=kc: e.matmul(bk[:, 0:TB], lhsT=wv[:, kc, :], rhs=H[:, kc, :], start=(kc == 0), stop=(kc == 7)),
                             reads=[wr] + hres, writes=[bkr], signal=(kc == 7))
                    cwj = par("cwf")[:, 3 * j:3 * j + 3]
                    if half == 0:
                        fin = lambda acc, accr, jj=jj: P.op('act', lambda e: e.activation(out=ACTB[:, jj, :], in_=acc[:, 0:TB], func=AF.Silu), reads=[accr], writes=['ACTB%d' % jj])
                        npl = 0
                    else:
                        fin = lambda acc, accr, jj=jj: P.op('dve', lambda e: e.tensor_tensor(out=ACTB[:, jj, :], in0=ACTB[:, jj, :], in1=acc[:, 0:TB], op=ALU.mult),
                                                            reads=[accr, 'ACTB%d' % jj], writes=['ACTB%d' % jj])
                        npl = 0
                    active.append(conv_chain(bk, bkr, cwj, 3, par("cbf")[:, j:j + 1], TAILF[:, j, :], 'TF%d' % j, first_blk, TB, fin, npl, tiny))
                    step_active(active)
            run_chains(active)
            actres = ['ACTB%d' % j for j in range(22)]
            BANKS.flush()
            nb_ = b_ + 1
            ns_ = s_
            if nb_ == NBLK:
                nb_, ns_ = 0, s_ + 1
            if ns_ < NSEQ:
                load_x(ns_, nb_)
            proj_norm_residual('dn', 3, KC_DN, ACTB, actres, "w4", None, 'MIX')
            for kc in range(8):
                P.op('dve', lambda e, kc=kc: e.tensor_tensor(out=MO[:, kc, :], in0=MO[:, kc, :], in1=RSTD, op=ALU.mult),
                     reads=['MO%d' % kc, 'RSTD'], writes=['MO%d' % kc])
                P.op('pool', lambda e, kc=kc: e.tensor_tensor(out=MO[:, kc, :], in0=MO[:, kc, :], in1=X[:, kc, :], op=ALU.add), reads=[xres[kc], 'MO%d' % kc], writes=['MO%d' % kc])
            P.dma('sp', outT[s_, 0:4, :, t0b:t0b + TB].rearrange("k p t -> p k t"), MO[:, 0:4, :], reads=mores[0:4], semname='st_outa')
            P.dma('sp', outT[s_, 4:8, :, t0b:t0b + TB].rearrange("k p t -> p k t"), MO[:, 4:8, :], reads=mores[4:8], semname='st_outb')
    P.finish('sp')
    return nc, P, dbg_outs


def _tile_w(w, ncol_tiles):
    K, N = w.shape
    kc = K // 128
    t = w.reshape(kc, 128, ncol_tiles, 128).transpose(2, 1, 0, 3)
    return np.ascontiguousarray(t).reshape(ncol_tiles, 128, kc * 128)


def make_consts():
    i = np.arange(128)
    c = np.zeros((128, 8, 128), np.float32)
    c[:, 0, :] = np.eye(128)
    c[:, 1, :] = (i[:, None] <= i[None, :])
    c[:, 2, :] = 1.0
    c[:, 3, :] = np.where(i[None, :] < i[:, None], 0.0, -BIG)
    c[:, 4, :] = np.where(i[None, :] > i[:, None], 0.0, -BIG)
    c[:, 5, :] = np.where(i[None, :] >= i[:, None], 0.0, -BIG)
    c[:, 6, :] = ((i[:, None] // 64) == (i[None, :] // 64))
    c[:, 7, :] = ((i[:, None] >= 64) & (i[None, :] < 64))
    return c


def prep_shared(inp):
    g = lambda n: np.asarray(inp[n], dtype=np.float32)[0]
    w_in = g("w_in")
    offs = np.cumsum([0, 1024, 1024, 1024, 1024, 8, 8, 1024, 1024, 256, 256, 16])
    sl = lambda k: w_in[:, offs[k]:offs[k + 1]]
    main = np.concatenate([sl(0), sl(1), sl(2), sl(3), sl(6), sl(7), sl(8), sl(9)], axis=1)
    small = np.concatenate([sl(4), sl(5), sl(10)], axis=1)
    w_in_t = _tile_w(main, NJ_IN)
    w_small = np.ascontiguousarray(small.reshape(8, 128, 32).transpose(1, 0, 2)).reshape(128, 256)
    w_out_t = _tile_w(g("w_out"), 8)
    w_up_t = _tile_w(g("w_up"), NJ_UP)
    w_dn_t = _tile_w(g("w_down"), 8)
    par = np.zeros((128, NPAR), np.float32)

    def put(name, arr):
        a, b = PO[name]
        par[:, a:b] = arr.reshape(128, b - a)

    pp = lambda v: np.ascontiguousarray(v.reshape(-1, 128).T)
    put("w1", pp(g("pre_mix_norm")))
    put("w2", pp(g("post_mix_norm")))
    put("w3", pp(g("pre_ffn_norm")))
    put("w4", pp(g("post_ffn_norm")))
    put("gnw", g("gdn_norm_w").reshape(128, 1))
    put("snw", pp(g("ssd_norm_w")))
    put("dexp", pp(np.repeat(g("ssd_d"), 64)))
    put("cwg", np.ascontiguousarray(g("gdn_conv_w").reshape(4, 24, 128).transpose(2, 1, 0)))
    put("cws", np.ascontiguousarray(g("ssd_conv_w").reshape(4, 12, 128).transpose(2, 1, 0)))
    put("cbs", pp(g("ssd_conv_b")))
    put("cwf", np.ascontiguousarray(g("ffn_conv_w").reshape(3, 44, 128).transpose(2, 1, 0)))
    put("cbf", pp(g("ffn_conv_b")))
    put("alog", np.broadcast_to(np.concatenate([g("gdn_a_log"), g("ssd_a_log")])[None, :], (128, 24)))
    put("dtb", np.broadcast_to(np.concatenate([g("gdn_dt_bias"), g("ssd_dt_bias")])[None, :], (128, 24)))
    return {"params": par, "consts": make_consts(), "w_in_t": w_in_t, "w_small": w_small,
            "w_out_t": w_out_t, "w_up_t": w_up_t, "w_dn_t": w_dn_t}


def x_to_dev(xs):
    n, s, _ = xs.shape
    return np.ascontiguousarray(xs.transpose(0, 2, 1)).reshape(n, 8, 128, s)


def out_from_dev(o):
    n, _, _, s = o.shape
    return np.ascontiguousarray(o.reshape(n, 1024, s).transpose(0, 2, 1))


_CACHE = {}


def kernel(**inputs):
    x = np.asarray(inputs["x"], dtype=np.float32)
    B, SEQ, _ = x.shape
    nseq = B // NCORES
    key = (nseq, SEQ)
    if key not in _CACHE:
        _CACHE[key] = build(nseq, SEQ, 512)[0]
    nc = _CACHE[key]
    shared = prep_shared(inputs)
    in_maps = []
    for c in range(NCORES):
        m = dict(shared)
        m["xT"] = x_to_dev(x[c * nseq:(c + 1) * nseq])
        in_maps.append(m)
    res = run_bass_kernel_spmd(nc, in_maps, core_ids=list(range(NCORES)))
    outs = [out_from_dev(np.asarray(r["outT"])) for r in res.results]
    return np.concatenate(outs, axis=0).astype(np.float32)
```
